# Optimizing a Trainium2 kernel written in Bass

```python
import math
import jax
import jax.numpy as jnp
from jax import lax
import numpy as np

D_MODEL = 2048
BATCH = 16
SEQ = 256
DEPTH = 4
DEC_BATCH = 8
DEC_SEQ = 2048
PAST_LEN = 512

GRID_W = 64
ROPE_THETA = 10000.0
EPS = 1e-6
Q_BLOCK = 128
N_MOD = 9
D_FF = 2 * D_MODEL

MLA_HEADS = 8
MLA_Q_RANK = D_MODEL // 4
MLA_KV_RANK = D_MODEL // 8
MLA_NOPE = 128
MLA_ROPE = 64
MLA_V = 128

S5_CH = D_MODEL // 4
S5_GROUP = 16
S5_GROUPS = S5_CH // S5_GROUP
S5_STATE = 64

GQA_Q_HEADS = 4
GQA_KV_HEADS = 2
GQA_HEAD_DIM = 128

MLA_IN = MLA_Q_RANK + MLA_KV_RANK + MLA_ROPE
GQA_IN = (GQA_Q_HEADS + 2 * GQA_KV_HEADS) * GQA_HEAD_DIM
D_IN = MLA_IN + S5_CH + GQA_IN
D_MIX = MLA_HEADS * MLA_V + S5_CH + GQA_Q_HEADS * GQA_HEAD_DIM
SPLIT_POINTS = (MLA_Q_RANK, MLA_Q_RANK + MLA_KV_RANK, MLA_IN, MLA_IN + S5_CH,
                MLA_IN + S5_CH + GQA_Q_HEADS * GQA_HEAD_DIM,
                MLA_IN + S5_CH + (GQA_Q_HEADS + GQA_KV_HEADS) * GQA_HEAD_DIM)
MLA_SCALE = 1.0 / math.sqrt(MLA_NOPE + MLA_ROPE)
GQA_SCALE = 1.0 / math.sqrt(GQA_HEAD_DIM)

kernel_name = "hymba_mla_s5_gqa_prefix_dit_step"


def rms_norm(x, g):
    xf = x.astype(jnp.float32)
    y = xf * lax.rsqrt(jnp.mean(xf * xf, axis=-1, keepdims=True) + EPS)
    return (y * g.astype(jnp.float32)).astype(x.dtype)


def modulate(x, g, shift, scale):
    return rms_norm(x, g) * (1.0 + scale) + shift


def swiglu(h, w_gate, w_up, w_down):
    return (jax.nn.silu(h @ w_gate) * (h @ w_up)) @ w_down


def modulation(cvec, w_ada, b_ada):
    m = jax.nn.silu(cvec) @ w_ada + b_ada
    return jnp.split(m, N_MOD, axis=-1)


def grid_angles(seq_len, rot_dim):
    n_rows = seq_len // GRID_W
    row = jnp.broadcast_to(jnp.arange(n_rows, dtype=jnp.float32)[:, None], (n_rows, GRID_W)).reshape(seq_len)
    col = jnp.broadcast_to(jnp.arange(GRID_W, dtype=jnp.float32)[None, :], (n_rows, GRID_W)).reshape(seq_len)
    n_freq = rot_dim // 4
    inv = ROPE_THETA ** (-jnp.arange(n_freq, dtype=jnp.float32) / n_freq)
    ang = jnp.concatenate([row[:, None] * inv, col[:, None] * inv], axis=-1)
    return jnp.cos(ang), jnp.sin(ang)


def apply_rope(x, cos, sin):
    half = x.shape[-1] // 2
    x1, x2 = x[..., :half], x[..., half:]
    c = cos[None, :, None, :].astype(x.dtype)
    s = sin[None, :, None, :].astype(x.dtype)
    return jnp.concatenate([x1 * c - x2 * s, x2 * c + x1 * s], axis=-1)


def attention(q, k, v, scale):
    b, sq, hq, dk = q.shape
    hkv = k.shape[2]
    rep = hq // hkv
    nblk = sq // Q_BLOCK
    qb = q.reshape(b, nblk, Q_BLOCK, hkv, rep, dk).transpose(1, 0, 2, 3, 4, 5)

    def one_block(qi):
        s = jnp.einsum('bqgrd,bkgd->bgrqk', qi, k, preferred_element_type=jnp.float32) * scale
        p = jax.nn.softmax(s, axis=-1)
        return jnp.einsum('bgrqk,bkgd->bqgrd', p.astype(v.dtype), v)

    o = lax.map(one_block, qb)
    return o.transpose(1, 0, 2, 3, 4, 5).reshape(b, sq, hq, v.shape[-1])


def mla_queries(zq, q_norm, w_uq, cos, sin):
    b, s, _ = zq.shape
    q = (rms_norm(zq, q_norm) @ w_uq).reshape(b, s, MLA_HEADS, MLA_NOPE + MLA_ROPE)
    q_nope, q_rope = q[..., :MLA_NOPE], q[..., MLA_NOPE:]
    if cos is not None:
        q_rope = apply_rope(q_rope, cos, sin)
    return jnp.concatenate([q_nope, q_rope], axis=-1)


def mla_keys_values(ckv, k_rope, w_ukv, cos, sin):
    b, s, _ = ckv.shape
    kv = (ckv @ w_ukv).reshape(b, s, MLA_HEADS, MLA_NOPE + MLA_V)
    k_nope, v = kv[..., :MLA_NOPE], kv[..., MLA_NOPE:]
    kr = k_rope[:, :, None, :]
    if cos is not None:
        kr = apply_rope(kr, cos, sin)
    k = jnp.concatenate([k_nope, jnp.broadcast_to(kr, (b, s, MLA_HEADS, MLA_ROPE))], axis=-1)
    return k, v


def s5_discretize(lam_re, lam_im, log_dt, b_re, b_im):
    dt = jnp.exp(log_dt.astype(jnp.float32))[:, None]
    lr = lam_re.astype(jnp.float32)
    li = lam_im.astype(jnp.float32)
    mag = jnp.exp(lr * dt)
    ab_re = mag * jnp.cos(li * dt)
    ab_im = mag * jnp.sin(li * dt)
    den = lr * lr + li * li
    nr = ab_re - 1.0
    f_re = (nr * lr + ab_im * li) / den
    f_im = (ab_im * lr - nr * li) / den
    br = b_re.astype(jnp.float32)
    bi = b_im.astype(jnp.float32)
    bb_re = f_re[..., None] * br - f_im[..., None] * bi
    bb_im = f_re[..., None] * bi + f_im[..., None] * br
    return ab_re, ab_im, bb_re, bb_im


def s5_combine(e1, e2):
    a1r, a1i, b1r, b1i = e1
    a2r, a2i, b2r, b2i = e2
    return (a1r * a2r - a1i * a2i, a1r * a2i + a1i * a2r,
            a2r * b1r - a2i * b1i + b2r, a2r * b1i + a2i * b1r + b2i)


def s5_scan(uf, ab_re, ab_im, bb_re, bb_im, h0, reverse):
    bu_re = jnp.einsum('gpc,bsgc->bsgp', bb_re, uf)
    bu_im = jnp.einsum('gpc,bsgc->bsgp', bb_im, uf)
    if h0 is not None:
        idx = -1 if reverse else 0
        bu_re = bu_re.at[:, idx].add(ab_re * h0[..., 0] - ab_im * h0[..., 1])
        bu_im = bu_im.at[:, idx].add(ab_re * h0[..., 1] + ab_im * h0[..., 0])
    a_re = jnp.broadcast_to(ab_re, bu_re.shape)
    a_im = jnp.broadcast_to(ab_im, bu_re.shape)
    _, _, hr, hi = lax.associative_scan(s5_combine, (a_re, a_im, bu_re, bu_im), axis=1, reverse=reverse)
    return hr, hi


def s5_mixer(u, lam_re, lam_im, log_dt, b_re, b_im, c_re, c_im, d_skip, w_glu, b_glu, h0, return_state):
    bsz, s, _ = u.shape
    uf = u.astype(jnp.float32).reshape(bsz, s, S5_GROUPS, S5_GROUP)
    y = d_skip.astype(jnp.float32).reshape(S5_GROUPS, S5_GROUP) * uf
    finals = []
    for dirn, reverse in ((0, False), (1, True)):
        ab_re, ab_im, bb_re, bb_im = s5_discretize(lam_re[dirn], lam_im[dirn], log_dt[dirn], b_re[dirn], b_im[dirn])
        init = None if h0 is None else h0[:, dirn].astype(jnp.float32)
        hr, hi = s5_scan(uf, ab_re, ab_im, bb_re, bb_im, init, reverse)
        y = y + jnp.einsum('gcp,bsgp->bsgc', c_re[dirn].astype(jnp.float32), hr) \
              - jnp.einsum('gcp,bsgp->bsgc', c_im[dirn].astype(jnp.float32), hi)
        if return_state:
            last = 0 if reverse else s - 1
            finals.append(jnp.stack([hr[:, last], hi[:, last]], axis=-1))
    y = y.reshape(bsz, s, S5_CH).astype(u.dtype)
    out = jax.nn.gelu(y) * jax.nn.sigmoid(y @ w_glu + b_glu)
    if return_state:
        return out, jnp.stack(finals, axis=1).astype(u.dtype)
    return out


def s5_params(lp):
    return (lp['s5_lambda_re'], lp['s5_lambda_im'], lp['s5_log_dt'], lp['s5_b_re'], lp['s5_b_im'],
            lp['s5_c_re'], lp['s5_c_im'], lp['s5_d'], lp['s5_w_glu'], lp['s5_b_glu'])


def mixer_context(h, lp):
    b, s, _ = h.shape
    zq, zkv, zkr, u, gq, gk, gv = jnp.split(h @ lp['w_in'], SPLIT_POINTS, axis=-1)
    ckv = rms_norm(zkv, lp['mla_kv_norm'])
    q = mla_queries(zq, lp['mla_q_norm'], lp['mla_w_uq'], None, None)
    k, v = mla_keys_values(ckv, zkr, lp['mla_w_ukv'], None, None)
    o_mla = attention(q, k, v, MLA_SCALE).reshape(b, s, MLA_HEADS * MLA_V)
    y_s5, s5_state = s5_mixer(u, *s5_params(lp), None, True)
    qg = rms_norm(gq.reshape(b, s, GQA_Q_HEADS, GQA_HEAD_DIM), lp['gqa_q_norm'])
    kg = rms_norm(gk.reshape(b, s, GQA_KV_HEADS, GQA_HEAD_DIM), lp['gqa_k_norm'])
    vg = gv.reshape(b, s, GQA_KV_HEADS, GQA_HEAD_DIM)
    o_gqa = attention(qg, kg, vg, GQA_SCALE).reshape(b, s, GQA_Q_HEADS * GQA_HEAD_DIM)
    out = jnp.concatenate([o_mla, y_s5, o_gqa], axis=-1) @ lp['w_out']
    return out, (ckv, zkr, kg, vg, s5_state)


def mixer_latent(h, ctx_ckv, ctx_krope, ctx_k, ctx_v, ctx_s5, lp):
    b, s, _ = h.shape
    cos_m, sin_m = grid_angles(s, MLA_ROPE)
    cos_g, sin_g = grid_angles(s, GQA_HEAD_DIM)
    zq, zkv, zkr, u, gq, gk, gv = jnp.split(h @ lp['w_in'], SPLIT_POINTS, axis=-1)
    q = mla_queries(zq, lp['mla_q_norm'], lp['mla_w_uq'], cos_m, sin_m)
    k_lat, v_lat = mla_keys_values(rms_norm(zkv, lp['mla_kv_norm']), zkr, lp['mla_w_ukv'], cos_m, sin_m)
    k_ctx, v_ctx = mla_keys_values(ctx_ckv, ctx_krope, lp['mla_w_ukv'], None, None)
    o_mla = attention(q, jnp.concatenate([k_ctx, k_lat], axis=1), jnp.concatenate([v_ctx, v_lat], axis=1),
                      MLA_SCALE).reshape(b, s, MLA_HEADS * MLA_V)
    y_s5 = s5_mixer(u, *s5_params(lp), ctx_s5, False)
    qg = apply_rope(rms_norm(gq.reshape(b, s, GQA_Q_HEADS, GQA_HEAD_DIM), lp['gqa_q_norm']), cos_g, sin_g)
    kg = apply_rope(rms_norm(gk.reshape(b, s, GQA_KV_HEADS, GQA_HEAD_DIM), lp['gqa_k_norm']), cos_g, sin_g)
    vg = gv.reshape(b, s, GQA_KV_HEADS, GQA_HEAD_DIM)
    o_gqa = attention(qg, jnp.concatenate([ctx_k, kg], axis=1), jnp.concatenate([ctx_v, vg], axis=1),
                      GQA_SCALE).reshape(b, s, GQA_Q_HEADS * GQA_HEAD_DIM)
    return jnp.concatenate([o_mla, y_s5, o_gqa], axis=-1) @ lp['w_out']


def trunk_layer(x, mods, lp, mix_fn):
    sh1, sc1, g1, sh2, sc2, g2, sh3, sc3, g3 = mods
    x = x + 0.5 * g1 * swiglu(modulate(x, lp['norm_ffn1'], sh1, sc1),
                              lp['ffn1_w_gate'], lp['ffn1_w_up'], lp['ffn1_w_down'])
    mixed, extra = mix_fn(modulate(x, lp['norm_mix'], sh2, sc2))
    x = x + g2 * mixed
    x = x + 0.5 * g3 * swiglu(modulate(x, lp['norm_ffn2'], sh3, sc3),
                              lp['ffn2_w_gate'], lp['ffn2_w_up'], lp['ffn2_w_down'])
    return x, extra


def setup_inputs(seed: int = 0) -> dict:
    key = jax.random.key(seed)
    ks = iter(jax.random.split(key, 64))

    def nrm(shape, std):
        return std * jax.random.normal(next(ks), shape, jnp.float32)

    def gain(shape):
        return 1.0 + nrm(shape, 0.05)

    L, D = DEPTH, D_MODEL
    G, P = S5_GROUPS, S5_STATE
    inp = {}
    inp['x_prompt'] = nrm((BATCH, SEQ, D), 1.0)
    inp['x_sample'] = nrm((DEC_BATCH, DEC_SEQ, D), 1.0)
    inp['cache_mla_ckv'] = nrm((DEC_BATCH, L, PAST_LEN, MLA_KV_RANK), 1.0)
    inp['cache_mla_krope'] = nrm((DEC_BATCH, L, PAST_LEN, MLA_ROPE), 1.0)
    inp['cache_gqa_k'] = nrm((DEC_BATCH, L, PAST_LEN, GQA_KV_HEADS, GQA_HEAD_DIM), 1.0)
    inp['cache_gqa_v'] = nrm((DEC_BATCH, L, PAST_LEN, GQA_KV_HEADS, GQA_HEAD_DIM), 1.0)
    inp['state_s5'] = nrm((DEC_BATCH, L, 2, G, P, 2), 0.3)
    inp['c'] = nrm((DEC_BATCH, D), 1.0)
    inp['c_ctx'] = nrm((D,), 1.0)
    inp['w_ada'] = nrm((L, D, N_MOD * D), 0.5 * D ** -0.5)
    inp['b_ada'] = nrm((L, N_MOD * D), 0.02)
    inp['norm_ffn1'] = gain((L, D))
    inp['ffn1_w_gate'] = nrm((L, D, D_FF), D ** -0.5)
    inp['ffn1_w_up'] = nrm((L, D, D_FF), D ** -0.5)
    inp['ffn1_w_down'] = nrm((L, D_FF, D), D_FF ** -0.5)
    inp['norm_mix'] = gain((L, D))
    inp['w_in'] = nrm((L, D, D_IN), D ** -0.5)
    inp['mla_q_norm'] = gain((L, MLA_Q_RANK))
    inp['mla_w_uq'] = nrm((L, MLA_Q_RANK, MLA_HEADS * (MLA_NOPE + MLA_ROPE)), MLA_Q_RANK ** -0.5)
    inp['mla_kv_norm'] = gain((L, MLA_KV_RANK))
    inp['mla_w_ukv'] = nrm((L, MLA_KV_RANK, MLA_HEADS * (MLA_NOPE + MLA_V)), MLA_KV_RANK ** -0.5)
    inp['s5_lambda_re'] = -0.5 + nrm((L, 2, G, P), 0.01)
    inp['s5_lambda_im'] = math.pi * jnp.arange(P, dtype=jnp.float32) + nrm((L, 2, G, P), 0.01)
    inp['s5_log_dt'] = jax.random.uniform(next(ks), (L, 2, G), jnp.float32, math.log(1e-3), math.log(1e-1))
    inp['s5_b_re'] = nrm((L, 2, G, P, S5_GROUP), (2 * S5_GROUP) ** -0.5)
    inp['s5_b_im'] = nrm((L, 2, G, P, S5_GROUP), (2 * S5_GROUP) ** -0.5)
    inp['s5_c_re'] = nrm((L, 2, G, S5_GROUP, P), (2 * P) ** -0.5)
    inp['s5_c_im'] = nrm((L, 2, G, S5_GROUP, P), (2 * P) ** -0.5)
    inp['s5_d'] = nrm((L, S5_CH), 0.5)
    inp['s5_w_glu'] = nrm((L, S5_CH, S5_CH), S5_CH ** -0.5)
    inp['s5_b_glu'] = nrm((L, S5_CH), 0.02)
    inp['gqa_q_norm'] = gain((L, GQA_HEAD_DIM))
    inp['gqa_k_norm'] = gain((L, GQA_HEAD_DIM))
    inp['w_out'] = nrm((L, D_MIX, D), D_MIX ** -0.5)
    inp['norm_ffn2'] = gain((L, D))
    inp['ffn2_w_gate'] = nrm((L, D, D_FF), D ** -0.5)
    inp['ffn2_w_up'] = nrm((L, D, D_FF), D ** -0.5)
    inp['ffn2_w_down'] = nrm((L, D_FF, D), D_FF ** -0.5)
    inp['norm_final'] = gain((D,))
    return inp


def reference(x_prompt, x_sample, cache_mla_ckv, cache_mla_krope, cache_gqa_k, cache_gqa_v, state_s5, c,
              c_ctx, w_ada, b_ada, norm_ffn1, ffn1_w_gate, ffn1_w_up, ffn1_w_down, norm_mix, w_in,
              mla_q_norm, mla_w_uq, mla_kv_norm, mla_w_ukv, s5_lambda_re, s5_lambda_im, s5_log_dt,
              s5_b_re, s5_b_im, s5_c_re, s5_c_im, s5_d, s5_w_glu, s5_b_glu, gqa_q_norm, gqa_k_norm,
              w_out, norm_ffn2, ffn2_w_gate, ffn2_w_up, ffn2_w_down, norm_final):
    xp = x_prompt
    xs = x_sample
    st_ckv, st_krope, st_k, st_v, st_s5 = [], [], [], [], []
    for l in range(DEPTH):
        lp = {
            'norm_ffn1': norm_ffn1[l], 'ffn1_w_gate': ffn1_w_gate[l], 'ffn1_w_up': ffn1_w_up[l],
            'ffn1_w_down': ffn1_w_down[l], 'norm_mix': norm_mix[l], 'w_in': w_in[l],
            'mla_q_norm': mla_q_norm[l], 'mla_w_uq': mla_w_uq[l], 'mla_kv_norm': mla_kv_norm[l],
            'mla_w_ukv': mla_w_ukv[l], 's5_lambda_re': s5_lambda_re[l], 's5_lambda_im': s5_lambda_im[l],
            's5_log_dt': s5_log_dt[l], 's5_b_re': s5_b_re[l], 's5_b_im': s5_b_im[l], 's5_c_re': s5_c_re[l],
            's5_c_im': s5_c_im[l], 's5_d': s5_d[l], 's5_w_glu': s5_w_glu[l], 's5_b_glu': s5_b_glu[l],
            'gqa_q_norm': gqa_q_norm[l], 'gqa_k_norm': gqa_k_norm[l], 'w_out': w_out[l],
            'norm_ffn2': norm_ffn2[l], 'ffn2_w_gate': ffn2_w_gate[l], 'ffn2_w_up': ffn2_w_up[l],
            'ffn2_w_down': ffn2_w_down[l],
        }
        mods_ctx = modulation(c_ctx, w_ada[l], b_ada[l])
        xp, (ckv, krope, kg, vg, s5s) = trunk_layer(xp, mods_ctx, lp, lambda hm: mixer_context(hm, lp))
        st_ckv.append(ckv)
        st_krope.append(krope)
        st_k.append(kg)
        st_v.append(vg)
        st_s5.append(s5s)
        mods_lat = [m[:, None, :] for m in modulation(c, w_ada[l], b_ada[l])]
        xs, _ = trunk_layer(
            xs, mods_lat, lp,
            lambda hm: (mixer_latent(hm, cache_mla_ckv[:, l], cache_mla_krope[:, l], cache_gqa_k[:, l],
                                     cache_gqa_v[:, l], state_s5[:, l], lp), None))
    y_prompt = rms_norm(xp, norm_final)
    y_sample = rms_norm(xs, norm_final)
    new_mla_ckv = jnp.stack(st_ckv, axis=1)
    new_mla_krope = jnp.stack(st_krope, axis=1)
    new_gqa_k = jnp.stack(st_k, axis=1)
    new_gqa_v = jnp.stack(st_v, axis=1)
    new_state_s5 = jnp.stack(st_s5, axis=1)
    return (y_prompt, y_sample, new_mla_ckv, new_mla_krope, new_gqa_k, new_gqa_v, new_state_s5)
```

```python
import math
import contextlib
import numpy as np
import concourse.bass as bass
import concourse.mybir as mybir
from concourse.bass_utils import run_bass_kernel_spmd

F32 = mybir.dt.float32
BF16 = mybir.dt.bfloat16
ALU = mybir.AluOpType
AF = mybir.ActivationFunctionType

ENGS = ("pe", "act", "dve", "pool", "sp")
SEM_ROT = 16000


class Tok:
    __slots__ = ("w", "r", "rd")

    def __init__(self):
        self.w = None
        self.r = {}
        self.rd = []


def toks(n):
    return [Tok() for _ in range(n)]


class DSem:
    __slots__ = ("idx", "count", "waited_max", "acked")

    def __init__(self, idx):
        self.idx = idx
        self.count = 0
        self.waited_max = 0
        self.acked = {}


class _Rec:
    def __init__(self):
        self.call = None

    def __getattr__(self, name):
        def f(*a, **k):
            self.call = (name, a, k)
            return None
        return f


def _freeze(fn):
    if fn is None:
        return None
    r = _Rec()
    fn(r)
    name, a, k = r.call
    import sys
    try:
        line = sys._getframe(3).f_lineno
    except ValueError:
        line = -1

    def g(e):
        return getattr(e, name)(*a, **k)
    g.line = line
    return g


class Prog:
    def __init__(self, nc, selfsync=True):
        self.nc = nc
        self.ops = {e: [] for e in ENGS}
        self.selfsync = selfsync
        self.dsems = []

    def new_dsem(self):
        d = DSem(len(self.dsems))
        self.dsems.append(d)
        return d

    def _collect(self, eng, reads, writes):
        deps = set()
        for t in reads:
            if t.w is not None:
                deps.add(t.w)
        for t in writes:
            if t.w is not None:
                deps.add(t.w)
            for e, s in t.r.items():
                if e != eng or (self.selfsync and eng not in ("pe", "sp")):
                    deps.add(("e", e, s))
            for d in t.rd:
                deps.add(d)
        waits = []
        for d in deps:
            if d[0] == "e":
                if d[1] == eng and (eng in ("pe", "sp") or not self.selfsync):
                    continue
                self.ops[d[1]][d[2]][2] = True
                waits.append(d)
            else:
                ds = self.dsems[d[1]]
                v = max(d[2], ds.count)
                ds.waited_max = max(ds.waited_max, v)
                waits.append(("d", d[1], v))
        return waits

    def op(self, eng, fn, reads=(), writes=()):
        waits = self._collect(eng, reads, writes)
        seq = len(self.ops[eng])
        fn = _freeze(fn)
        self.ops[eng].append([fn, waits, False, None])
        ev = ("e", eng, seq)
        for t in reads:
            if t.r.get(eng, -1) < seq:
                t.r[eng] = seq
        for t in writes:
            t.w = ev
            t.r = {}
            t.rd = []
        return ev

    def dma(self, q, fn, dsem, reads=(), writes=()):
        waits = self._collect(q, reads, writes)
        if dsem.waited_max > dsem.acked.get(q, 0):
            waits.append(("d", dsem.idx, dsem.waited_max))
            dsem.acked[q] = dsem.waited_max
        fn = _freeze(fn)
        self.ops[q].append([fn, waits, False, dsem.idx])
        dsem.count += 16
        ev = ("d", dsem.idx, dsem.count)
        for t in reads:
            t.rd.append(ev)
        for t in writes:
            t.w = ev
            t.r = {}
            t.rd = []
        return ev

    def barrier(self):
        last = {}
        for e in ENGS:
            for i in range(len(self.ops[e]) - 1, -1, -1):
                if self.ops[e][i][0] is not None and self.ops[e][i][3] is None:
                    last[e] = i
                    break
        for e in ENGS:
            waits = []
            for f, s in last.items():
                if f != e:
                    self.ops[f][s][2] = True
                    waits.append(("e", f, s))
            for d in self.dsems:
                if d.count:
                    waits.append(("d", d.idx, d.count))
                    d.waited_max = max(d.waited_max, d.count)
            self.ops[e].append([None, waits, False, None])

    def emit(self):
        nc = self.nc
        with contextlib.ExitStack() as st:
            nsig = {e: sum(1 for o in self.ops[e] if o[2]) for e in ENGS}
            esems = {}
            for e in ENGS:
                n = nsig[e] // SEM_ROT + 1
                esems[e] = [st.enter_context(nc.semaphore("s_%s%d" % (e, i))) for i in range(n)]
            dsem_h = [st.enter_context(nc.semaphore("d%d" % i)) for i in range(len(self.dsems))]
            signum = {}
            for e in ENGS:
                c = 0
                arr = []
                for o in self.ops[e]:
                    if o[2]:
                        c += 1
                    arr.append(c)
                signum[e] = arr
            block = st.enter_context(nc.Block())

            def run(ename, eng):
                waited = {}
                sc = 0
                for (fn, waits, signal, dsi) in self.ops[ename]:
                    for w in waits:
                        if w[0] == "e":
                            n = signum[w[1]][w[2]]
                            k = (n - 1) // SEM_ROT
                            v = n - k * SEM_ROT
                            key = ("e", w[1], k)
                            sem = esems[w[1]][k]
                        else:
                            key = ("d", w[1])
                            v = w[2]
                            sem = dsem_h[w[1]]
                        if waited.get(key, 0) >= v:
                            continue
                        waited[key] = v
                        eng.wait_ge(sem, v)
                    if fn is None:
                        continue
                    try:
                        ins = fn(eng)
                    except BaseException as ex:
                        print("EMIT FAIL on", ename, "line", getattr(fn, "line", None), repr(ex)[:300])
                        raise
                    if dsi is not None:
                        ins.then_inc(dsem_h[dsi], 16)
                    elif signal:
                        sc += 1
                        ins.then_inc(esems[ename][(sc - 1) // SEM_ROT], 1)

            @block.tensor
            def _(eng):
                run("pe", eng)

            @block.scalar
            def _(eng):
                run("act", eng)

            @block.vector
            def _(eng):
                run("dve", eng)

            @block.gpsimd
            def _(eng):
                run("pool", eng)

            @block.sync
            def _(eng):
                run("sp", eng)


D = 2048
DFF = 4096
T = 512
NCH = 16
EPS = 1e-6
PAST = 512
SEQ = 256
NCTX = 512
D_IN = 2368
C_ZQ, C_ZKV, C_ZKR, C_U, C_GQ, C_GK, C_GV = 0, 512, 768, 832, 1344, 1856, 2112
MLA_SCALE = 1.0 / math.sqrt(192.0)
GQA_SCALE = 1.0 / math.sqrt(128.0)
SLOT = 4096
NSLOT = 4

WEIGHT_SPECS = [
    ("w_ada", (D, 9 * D)), ("b_ada", (9 * D,)), ("norm_ffn1", (D,)),
    ("ffn1_w_gate", (D, DFF)), ("ffn1_w_up", (D, DFF)), ("ffn1_w_down", (DFF, D)),
    ("norm_mix", (D,)), ("w_in", (D, D_IN)), ("mla_q_norm", (512,)), ("mla_w_uq", (512, 1536)),
    ("mla_kv_norm", (256,)), ("mla_w_ukv", (256, 2048)),
    ("s5_lambda_re", (2, 32, 64)), ("s5_lambda_im", (2, 32, 64)), ("s5_log_dt", (2, 32)),
    ("s5_b_re", (2, 32, 64, 16)), ("s5_b_im", (2, 32, 64, 16)),
    ("s5_c_re", (2, 32, 16, 64)), ("s5_c_im", (2, 32, 16, 64)),
    ("s5_d", (512,)), ("s5_w_glu", (512, 512)), ("s5_b_glu", (512,)),
    ("gqa_q_norm", (128,)), ("gqa_k_norm", (128,)), ("w_out", (D, D)), ("norm_ffn2", (D,)),
    ("ffn2_w_gate", (D, DFF)), ("ffn2_w_up", (D, DFF)), ("ffn2_w_down", (DFF, D)),
]


def build(L=4, NLAT=2048, debug=None, STOP=0):
    nc = bass.Bass("TRN2", target_bir_lowering=False)
    P = Prog(nc)
    NTL = NLAT // T
    NT = NTL + 1
    NTOK = NLAT + NCTX
    NKL = PAST + NLAT
    KTOT = NKL + NCTX
    SPAD = 1024 if NLAT > 1024 else NLAT // 2
    NLEV = int(math.log2(NLAT))

    def din(name, shape):
        return nc.dram_tensor(name, list(shape), F32, kind="ExternalInput").ap()

    def dout(name, shape):
        return nc.dram_tensor(name, list(shape), F32, kind="ExternalOutput").ap()

    x_lat = din("x_lat", (NLAT, D))
    x_ctx = din("x_ctx", (NCTX, D))
    cvec = din("cvec", (2, D))
    c_ckv = din("c_ckv", (L, PAST, 256))
    c_kr = din("c_kr", (L, PAST, 64))
    c_k = din("c_k", (L, PAST, 256))
    c_v = din("c_v", (L, PAST, 256))
    st_s5 = din("st_s5", (L, 2, 2048, 2))
    W = {}
    for name, shp in WEIGHT_SPECS:
        W[name] = din(name, (L,) + shp)
    norm_final = din("norm_final", (D,))
    ident_d = din("ident", (128, 128))
    perm128_d = din("perm128", (128, 128))
    perm64_d = din("perm64", (64, 64))
    ropeg_d = din("ropeg", (2, 128, NLAT))
    ropem_d = din("ropem", (2, 64, NLAT))

    y_lat = dout("y_lat", (NLAT, D))
    y_ctx = dout("y_ctx", (NCTX, D))
    o_ckv = dout("o_ckv", (2, L, SEQ, 256))
    o_kr = dout("o_kr", (2, L, SEQ, 64))
    o_k = dout("o_k", (2, L, SEQ, 256))
    o_v = dout("o_v", (2, L, SEQ, 256))
    o_s5 = dout("o_s5", (2, L, 2, 2048, 2))
    xs_d = nc.dram_tensor("xs_scratch", [NT, 128, NCH, T], F32).ap()
    dbg = {}
    if debug:
        for nm, shp in debug.items():
            dbg[nm] = dout("dbg_" + nm, shp)

    SB_BASE = 16512
    SB_END = 229376
    cur = [SB_BASE]

    def sb(name, shape, dt, at=None):
        nb = int(np.prod(shape[1:])) * (4 if dt == F32 else 2)
        nb = (nb + 63) // 64 * 64
        if at is None:
            off = cur[0]
            cur[0] += nb
            assert cur[0] <= SB_END, ("SBUF overflow", name, cur[0])
        else:
            off = at
        return nc.alloc_sbuf_tensor_at(name, list(shape), dt, offset=off)

    ident = sb("ident", [128, 128], F32)
    identb = sb("identb", [128, 128], BF16)
    perm128 = sb("perm128", [128, 128], F32)
    perm64 = sb("perm64", [64, 64], F32)
    ones = sb("ones", [128, 128], BF16)
    perm128b = sb("perm128b", [128, 128], BF16)
    perm64b = sb("perm64b", [64, 64], BF16)
    epsc = sb("epsc", [128, 8], F32)
    cT = sb("cT", [128, NCH, 2], F32)
    csT = sb("csT", [128, NCH, 2], BF16)
    badaT = sb("badaT", [128, L, 144], F32)
    gnorm = sb("gnorm", [128, L, 3, NCH], F32)
    gfin = sb("gfin", [128, NCH], F32)
    qnorm = sb("qnorm", [128, L, 4], F32)
    kvnorm = sb("kvnorm", [128, L, 2], F32)
    gqn = sb("gqn", [128, L, 2], F32)
    s5d = sb("s5d", [128, L, 4], F32)
    bglu = sb("bglu", [128, L, 4], F32)
    mods = sb("mods", [128, 9, NCH, 2], F32)
    gmod = sb("gmod", [128, 3, NCH, 2], F32)
    geff = sb("geff", [128, 3, NCH, 2], F32)
    ROPE0 = cur[0]
    rope_g = sb("rope_g", [128, 2, T], F32)
    rope_m = sb("rope_m", [64, 2, T], F32)
    ckvT = sb("ckvT", [128, 2, KTOT], BF16)
    krT = sb("krT", [64, KTOT], BF16)
    gkT = sb("gkT", [128, 2, KTOT], BF16)
    gvS = sb("gvS", [128, KTOT // 128, 256], BF16)
    uT = sb("uT", [128, 4, NTOK], BF16)
    ring = [sb("ring%d" % i, [128, SLOT], BF16) for i in range(NSLOT)]
    rstd = sb("rstd", [128, T], F32)
    tmpA = [sb("tmpA%d" % i, [128, T], F32) for i in range(3)]
    sqb = [sb("sqb%d" % i, [128, T], BF16) for i in range(2)]
    pTb = [sb("pTb%d" % i, [128, T], BF16) for i in range(4)]
    U0 = cur[0]
    xT = sb("xT", [128, NCH, T], F32)
    hT = sb("hT", [128, NCH, T], BF16)
    A0 = cur[0]
    aT = sb("aT", [128, 32, T], BF16)
    UEND = cur[0]
    H0_ = U0 + NCH * T * 4
    catT = sb("catT", [128, NCH, T], BF16, at=A0)
    zqn = sb("zqn", [128, 4, T], BF16, at=A0 + 16384)
    gqT = sb("gqT", [128, 4, T], BF16, at=A0 + 20480)
    knope = sb("knope", [128, NKL], BF16, at=H0_)
    vh = sb("vh", [128, NKL // 128, 128], BF16, at=H0_ + NKL * 2)
    qn_h = sb("qn_h", [128, T], BF16, at=H0_ + NKL * 4)
    qr_h = sb("qr_h", [64, T], BF16, at=H0_ + NKL * 4 + 1024)
    assert NKL * 4 + 2048 <= NCH * T * 2
    stage = sb("stage", [128, D], F32, at=A0)
    SBUFW = max(SPAD + NLAT, 1024)
    hbuf = [sb("hbuf%d" % i, [128, SBUFW], F32, at=U0 + i * SBUFW * 4) for i in range(4)]
    o = U0 + 4 * SBUFW * 4
    bbT = sb("bbT", [128, 2, 16, 2, 128], BF16, at=o); o += 2 * 16 * 2 * 128 * 2
    ccT = sb("ccT", [128, 2, 16, 2, 128], BF16, at=o); o += 2 * 16 * 2 * 128 * 2
    assert o <= UEND, ("s5 carve overflow", o - UEND)
    negpi = sb("negpi", [128, 8], F32)
    halfpi = sb("halfpi", [128, 8], F32)
    S5ENG = "dve"
    o = ROPE0
    s5p = sb("s5p", [128, 2, 16, 8], F32, at=o); o += 1024
    s5pow = sb("s5pow", [128, 2, 16, 12, 3], F32, at=o); o += 4608
    s5h0 = sb("s5h0", [128, 2, 16, 2], F32, at=o); o += 256
    s5ah0 = sb("s5ah0", [128, 2, 16, 2], F32, at=o); o += 256
    s5fin = sb("s5fin", [128, 2, 2, 16, 2], F32, at=o); o += 512
    s5f = sb("s5f", [128, 2, 16, 3], F32, at=o); o += 384
    assert o <= ROPE0 + 8192
    s5frow = None
    print("SBUF used", cur[0] - SB_BASE, "of", SB_END - SB_BASE)

    psb = [nc.alloc_psum_tensor("psb%d" % i, [128, T], F32) for i in range(8)]
    pst = toks(8)

    t_const = Tok()
    t_par = Tok()
    t_mods = Tok()
    t_x = toks(NCH)
    t_h = toks(NCH)
    t_a = toks(32)
    t_ring = toks(NSLOT)
    s_ring = [P.new_dsem() for _ in range(NSLOT)]
    t_xs = toks(NT)
    s_xld = P.new_dsem()
    s_xst = P.new_dsem()
    s_misc = P.new_dsem()
    s_out = P.new_dsem()
    t_out = Tok()
    t_rstd = Tok()
    t_tmpA = toks(3)
    t_sqb = toks(2)
    t_pT = toks(4)
    t_rope = Tok()
    s_rope = P.new_dsem()
    t_kv = Tok()
    t_u = toks(4 * NT)
    t_stage = Tok()
    s_stage = P.new_dsem()
    cnt = {"ring": 0, "tmp": 0, "sq": 0, "pT": 0}

    def act(fn, r, w):
        return P.op("act", fn, r, w)

    def dve(fn, r, w):
        return P.op("dve", fn, r, w)

    def pe(fn, r, w):
        return P.op("pe", fn, r, w)

    def wload(src_ap, nk, ncols, ndma=None):
        s = cnt["ring"] % NSLOT
        cnt["ring"] += 1
        dst = ring[s][:, 0:nk * ncols].rearrange("p (k c) -> p k c", c=ncols)
        dd = dst if ndma is None else dst[:, :, 0:ndma]
        P.dma("pool", lambda e: e.dma_start(out=dd, in_=src_ap), s_ring[s], writes=[t_ring[s]])
        return dst, t_ring[s]

    def wview(wap, k0, nk, c0, ncols):
        return wap.rearrange("(k p) n -> p k n", p=128)[:, k0:k0 + nk, c0:c0 + ncols]

    def next_tmp():
        i = cnt["tmp"] % 3
        cnt["tmp"] += 1
        return tmpA[i], t_tmpA[i]

    def next_sq():
        i = cnt["sq"] % 2
        cnt["sq"] += 1
        return sqb[i], t_sqb[i]

    def next_pT():
        i = cnt["pT"] % 4
        cnt["pT"] += 1
        return pTb[i], t_pT[i]

    ncd = nc.allow_non_contiguous_dma(reason="small strided parameter loads")
    ncd.__enter__()

    def sdma(out, in_, tok):
        P.dma("sp", lambda e: e.dma_start(out=out, in_=in_), s_misc, writes=[tok])

    sdma(ident[:], ident_d, t_const)
    sdma(perm128[:], perm128_d, t_const)
    sdma(perm64[:], perm64_d, t_const)
    for r in range(2):
        sdma(cT[:, :, r], cvec[r].rearrange("(k p) -> p k", p=128), t_par)
    for l in range(L):
        sdma(badaT[:, l, :], W["b_ada"][l].rearrange("(c p) -> p c", p=128), t_par)
        for j, nm in enumerate(("norm_ffn1", "norm_mix", "norm_ffn2")):
            sdma(gnorm[:, l, j, :], W[nm][l].rearrange("(k p) -> p k", p=128), t_par)
        sdma(qnorm[:, l, :], W["mla_q_norm"][l].rearrange("(k p) -> p k", p=128), t_par)
        sdma(kvnorm[:, l, :], W["mla_kv_norm"][l].rearrange("(k p) -> p k", p=128), t_par)
        sdma(gqn[:, l, 0:1], W["gqa_q_norm"][l].rearrange("(p o) -> p o", o=1), t_par)
        sdma(gqn[:, l, 1:2], W["gqa_k_norm"][l].rearrange("(p o) -> p o", o=1), t_par)
        sdma(s5d[:, l, :], W["s5_d"][l].rearrange("(k p) -> p k", p=128), t_par)
        sdma(bglu[:, l, :], W["s5_b_glu"][l].rearrange("(k p) -> p k", p=128), t_par)
    sdma(gfin[:], norm_final.rearrange("(k p) -> p k", p=128), t_par)
    P.op("pool", lambda e: e.memset(ones[:], 1.0), [], [t_const])
    for i_ in range(NSLOT):
        P.op("pool", lambda e, i_=i_: e.memset(ring[i_][:], 0.0), [], [t_ring[i_]])
    P.op("pool", lambda e: e.memset(epsc[:], EPS), [], [t_const])
    P.op("pool", lambda e: e.memset(negpi[:], -math.pi), [], [t_const])
    P.op("pool", lambda e: e.memset(halfpi[:], 0.5 * math.pi), [], [t_const])
    act(lambda e: e.activation(out=identb[:], in_=ident[:], func=AF.Copy), [t_const], [t_const])
    act(lambda e: e.activation(out=perm128b[:], in_=perm128[:], func=AF.Copy), [t_const], [t_const])
    act(lambda e: e.activation(out=perm64b[:], in_=perm64[:], func=AF.Copy), [t_const], [t_const])
    tm, tt = next_tmp()
    act(lambda e: e.activation(out=tm[:, 0:32], in_=cT[:].rearrange("p k r -> p (k r)"), func=AF.Sigmoid), [t_par], [tt])
    dve(lambda e: e.tensor_tensor(out=csT[:].rearrange("p k r -> p (k r)"), in0=tm[:, 0:32],
                                  in1=cT[:].rearrange("p k r -> p (k r)"), op=ALU.mult), [tt, t_par], [t_par])

    def load_x_tile(ti, first):
        if not first:
            P.dma("sp", lambda e: e.dma_start(out=xT[:], in_=xs_d[ti]), s_xld, reads=[t_xs[ti]], writes=t_x)
            return
        src = x_lat[ti * T:(ti + 1) * T, :] if ti < NTL else x_ctx
        for tc4 in range(4):
            P.dma("sp", lambda e, tc4=tc4: e.dma_start(out=stage[:], in_=src[tc4 * 128:(tc4 + 1) * 128, :]),
                  s_stage, writes=[t_stage] + t_a)
            for k0 in range(0, NCH, 4):
                b = (k0 // 4) % 2
                for j in range(4):
                    pe(lambda e, b=b, j=j, k=k0 + j: e.transpose(
                        out=psb[b][:, j * 128:(j + 1) * 128], in_=stage[:, k * 128:(k + 1) * 128], identity=ident[:]),
                       [t_stage, t_const], [pst[b]])
                dve(lambda e, b=b, k0=k0, tc4=tc4: e.tensor_copy(
                    out=xT[:, k0:k0 + 4, tc4 * 128:(tc4 + 1) * 128],
                    in_=psb[b][:].rearrange("p (j t) -> p j t", t=128)), [pst[b]], t_x[k0:k0 + 4])

    def store_x_tile(ti):
        P.dma("sp", lambda e: e.dma_start(out=xs_d[ti], in_=xT[:]), s_xst, reads=t_x, writes=[t_xs[ti]])

    def sumsq_rstd(chunks, nfeat, bank, n=T):
        nchk = len(chunks)
        for i, (ap, tk, pn) in enumerate(chunks):
            sq, tq = next_sq()
            act(lambda e, sq=sq, ap=ap, pn=pn: e.activation(out=sq[0:pn, 0:n], in_=ap, func=AF.Square), [tk], [tq])
            pe(lambda e, sq=sq, pn=pn, i=i: e.matmul(psb[bank][:, 0:n], lhsT=ones[0:pn, :], rhs=sq[0:pn, 0:n],
                                                     start=(i == 0), stop=(i == nchk - 1)), [tq, t_const], [pst[bank]])
        act(lambda e: e.activation(out=rstd[:, 0:n], in_=psb[bank][:, 0:n], func=AF.Sqrt, scale=1.0 / nfeat,
                                   bias=epsc[:, 0:1]), [pst[bank], t_const], [t_rstd])
        dve(lambda e: e.reciprocal(out=rstd[:, 0:n], in_=rstd[:, 0:n]), [t_rstd], [t_rstd])

    def norm_mod(j, s):
        sumsq_rstd([(xT[:, k, :], t_x[k], 128) for k in range(NCH)], D, 7)
        for k in range(NCH):
            tm, tt = next_tmp()
            dve(lambda e, tm=tm, k=k: e.tensor_tensor(out=tm[:], in0=xT[:, k, :], in1=rstd[:], op=ALU.mult),
                [t_x[k], t_rstd], [tt])
            act(lambda e, tm=tm, k=k: e.activation(out=hT[:, k, :], in_=tm[:], func=AF.Identity,
                                                   scale=gmod[:, j, k, s:s + 1], bias=mods[:, 3 * j, k, s:s + 1]),
                [tt, t_mods], [t_h[k]])

    def compute_mods(l):
        bank = 6
        for sl in range(72):
            wv, wt = wload(wview(W["w_ada"][l], 0, NCH, sl * 256, 256), NCH, 256)
            for c in range(2):
                ch = 2 * sl + c
                for k in range(NCH):
                    pe(lambda e, wv=wv, c=c, k=k, ch=ch: e.matmul(
                        psb[bank][:, 2 * ch:2 * ch + 2], lhsT=wv[:, k, c * 128:(c + 1) * 128], rhs=csT[:, k, :],
                        start=(k == 0), stop=(k == NCH - 1)), [wt, t_par], [pst[bank]])
        dve(lambda e: e.tensor_tensor(out=mods[:].rearrange("p v k s -> p (v k) s"),
                                      in0=psb[bank][:, 0:288].rearrange("p (c s) -> p c s", s=2),
                                      in1=badaT[:, l, :].unsqueeze(2).to_broadcast([128, 144, 2]), op=ALU.add),
            [pst[bank], t_par], [t_mods])
        for j in range(3):
            dve(lambda e, j=j: e.scalar_tensor_tensor(
                out=gmod[:, j, :, :], in0=mods[:, 3 * j + 1, :, :], scalar=1.0,
                in1=gnorm[:, l, j, :].unsqueeze(2).to_broadcast([128, NCH, 2]), op0=ALU.add, op1=ALU.mult),
                [t_mods, t_par], [t_mods])
            dve(lambda e, j=j: e.tensor_scalar(out=geff[:, j, :, :], in0=mods[:, 3 * j + 2, :, :],
                                               scalar1=(1.0 if j == 1 else 0.5), scalar2=None, op0=ALU.mult),
                [t_mods], [t_mods])

    def ffn(l, which, s):
        j = 0 if which == 1 else 2
        wg, wu, wd = (W["ffn%d_w_gate" % which][l], W["ffn%d_w_up" % which][l], W["ffn%d_w_down" % which][l])
        for sl in range(16):
            gv_, gt = wload(wview(wg, 0, NCH, sl * 256, 256), NCH, 256)
            uv_, ut = wload(wview(wu, 0, NCH, sl * 256, 256), NCH, 256)
            for c in range(2):
                f = 2 * sl + c
                bg, bu = f % 2, 2 + f % 2
                for k in range(NCH):
                    pe(lambda e, gv_=gv_, c=c, k=k, bg=bg: e.matmul(
                        psb[bg][:], lhsT=gv_[:, k, c * 128:(c + 1) * 128], rhs=hT[:, k, :],
                        start=(k == 0), stop=(k == NCH - 1)), [gt, t_h[k]], [pst[bg]])
                for k in range(NCH):
                    pe(lambda e, uv_=uv_, c=c, k=k, bu=bu: e.matmul(
                        psb[bu][:], lhsT=uv_[:, k, c * 128:(c + 1) * 128], rhs=hT[:, k, :],
                        start=(k == 0), stop=(k == NCH - 1)), [ut, t_h[k]], [pst[bu]])
                tm, tt = next_tmp()
                act(lambda e, tm=tm, bg=bg: e.activation(out=tm[:], in_=psb[bg][:], func=AF.Silu), [pst[bg]], [tt])
                dve(lambda e, tm=tm, bu=bu, f=f: e.tensor_tensor(out=aT[:, f, :], in0=tm[:], in1=psb[bu][:], op=ALU.mult),
                    [tt, pst[bu]], [t_a[f]])
        for sl in range(8):
            b0 = 4 + 2 * (sl % 2)
            for part in range(2):
                dv_, dt_ = wload(wview(wd, part * 16, 16, sl * 256, 256), 16, 256)
                for c in range(2):
                    for k in range(16):
                        kk = part * 16 + k
                        pe(lambda e, dv_=dv_, c=c, k=k, kk=kk, b=b0 + c, part=part: e.matmul(
                            psb[b][:], lhsT=dv_[:, k, c * 128:(c + 1) * 128], rhs=aT[:, kk, :],
                            start=(kk == 0), stop=(kk == 31)), [dt_, t_a[kk]], [pst[b0 + c]])
            for c in range(2):
                d_ = 2 * sl + c
                dve(lambda e, d_=d_, b=b0 + c: e.scalar_tensor_tensor(
                    out=xT[:, d_, :], in0=psb[b][:], scalar=geff[:, j, d_, s:s + 1], in1=xT[:, d_, :],
                    op0=ALU.mult, op1=ALU.add), [pst[b0 + c], t_mods, t_x[d_]], [t_x[d_]])

    def final_out(ti):
        sumsq_rstd([(xT[:, k, :], t_x[k], 128) for k in range(NCH)], D, 7)
        for k in range(NCH):
            dve(lambda e, k=k: e.scalar_tensor_tensor(out=xT[:, k, :], in0=xT[:, k, :], scalar=gfin[:, k:k + 1],
                                                      in1=rstd[:], op0=ALU.mult, op1=ALU.mult),
                [t_x[k], t_rstd, t_par], [t_x[k]])
        dst = y_lat[ti * T:(ti + 1) * T, :] if ti < NTL else y_ctx
        for tc4 in range(4):
            for k0 in range(0, NCH, 4):
                b = (k0 // 4) % 2
                for jj in range(4):
                    pe(lambda e, b=b, jj=jj, k=k0 + jj, tc4=tc4: e.transpose(
                        out=psb[b][:, jj * 128:(jj + 1) * 128], in_=xT[:, k, tc4 * 128:(tc4 + 1) * 128], identity=ident[:]),
                       [t_x[k0 + jj], t_const], [pst[b]])
                dve(lambda e, b=b, k0=k0: e.tensor_copy(out=stage[:, k0 * 128:(k0 + 4) * 128], in_=psb[b][:]),
                    [pst[b]], [t_stage])
            P.dma("sp", lambda e, tc4=tc4: e.dma_start(out=dst[tc4 * 128:(tc4 + 1) * 128, :], in_=stage[:]),
                  s_out, reads=[t_stage] + t_a, writes=[t_out])

    H0 = U0 + NCH * T * 4
    mt = [sb("mt%d" % i, [128, T], F32, at=A0 + 24576 + i * 2048) for i in range(4)]
    t_mt = toks(4)
    iost = sb("iost", [128, 4, 832], F32, at=A0)
    t_io = Tok()
    s_io = P.new_dsem()
    s_oo = P.new_dsem()
    t_oo = Tok()
    t_cat = toks(NCH)
    t_kn = Tok()
    t_vh = Tok()
    t_zqn = Tok()
    t_gq = toks(4)
    t_qh = Tok()
    t_hb = toks(4)
    t_s5w = Tok()
    t_s5p = Tok()
    s_s5 = P.new_dsem()
    t_s5st = Tok()
    S5ST = [sb("s5st%d" % i, [128, 16, 128], F32, at=U0 + i * 8192) for i in range(2)]
    t_st = toks(2)
    bankrot = [0]

    def nbank(lst):
        b = lst[bankrot[0] % len(lst)]
        bankrot[0] += 1
        return b

    def copy_alt(i, out, in_, r, w):
        if i % 2 == 0:
            act(lambda e: e.activation(out=out, in_=in_, func=AF.Copy), r, w)
        else:
            dve(lambda e: e.tensor_copy(out=out, in_=in_), r, w)

    def linear(wap, nk, rhs_fn, rhs_toks, chunks, evac, banks, n=T):
        i = 0
        while i < len(chunks):
            c_start = chunks[i][0]
            j = i
            while j < len(chunks) and chunks[j][0] >= c_start and (chunks[j][0] + max(chunks[j][1], 128) - c_start) * nk <= SLOT:
                j += 1
            ncols = max(c[0] + max(c[1], 128) for c in chunks[i:j]) - c_start
            wv, wt = wload(wview(wap, 0, nk, c_start, ncols), nk, ncols)
            for idx in range(i, j):
                c0, m = chunks[idx]
                b = nbank(banks)
                for k in range(nk):
                    pe(lambda e, wv=wv, k=k, b=b, o=c0 - c_start: e.matmul(
                        psb[b][:, 0:n], lhsT=wv[:, k, o:o + 128], rhs=rhs_fn(k), start=(k == 0), stop=(k == nk - 1)),
                       [wt, rhs_toks[k]], [pst[b]])
                evac(idx, b, m)
            i = j

    def rope(xap, xtok, pn, perm, tab, out_ap, out_toks):
        b = nbank([4, 5])
        permb = perm128b if pn == 128 else perm64b
        xb, txb = next_sq()
        act(lambda e: e.activation(out=xb[0:pn, :], in_=xap, func=AF.Copy), [xtok], [txb])
        pe(lambda e: e.matmul(psb[b][0:pn, :], lhsT=permb[0:pn, 0:pn], rhs=xb[0:pn, :], start=True, stop=True),
           [txb, t_const], [pst[b]])
        tm, tt = next_tmp()
        dve(lambda e: e.tensor_tensor(out=tm[0:pn, :], in0=xap, in1=tab[0:pn, 0, :], op=ALU.mult), [xtok, t_rope], [tt])
        dve(lambda e: e.tensor_tensor(out=xap, in0=psb[b][0:pn, :], in1=tab[0:pn, 1, :], op=ALU.mult),
            [pst[b], t_rope], [xtok])
        dve(lambda e: e.tensor_tensor(out=out_ap, in0=tm[0:pn, :], in1=xap, op=ALU.add), [tt, xtok], out_toks)

    def load_rope(ti):
        P.dma("sp", lambda e: e.dma_start(out=rope_g[:], in_=ropeg_d[:, :, ti * T:(ti + 1) * T].rearrange("a p t -> p a t")),
              s_rope, writes=[t_rope])
        P.dma("sp", lambda e: e.dma_start(out=rope_m[:], in_=ropem_d[:, :, ti * T:(ti + 1) * T].rearrange("a p t -> p a t")),
              s_rope, writes=[t_rope])

    def load_cache(l):
        for src, c0, w_ in ((c_ckv, 0, 256), (c_kr, 256, 64), (c_k, 320, 256), (c_v, 576, 256)):
            P.dma("sp", lambda e, src=src, c0=c0, w_=w_: e.dma_start(
                out=iost[:, :, c0:c0 + w_], in_=src[l].rearrange("(tc p) f -> p tc f", p=128)), s_io, writes=[t_io])
        n_ = 0
        for (c0, dst) in ((0, ckvT), (320, gkT)):
            for c in range(2):
                b = nbank([0, 1, 2, 3])
                for tc in range(4):
                    pe(lambda e, b=b, tc=tc, o=c0 + c * 128: e.transpose(
                        out=psb[b][:, tc * 128:(tc + 1) * 128], in_=iost[:, tc, o:o + 128], identity=ident[:]),
                       [t_io, t_const], [pst[b]])
                copy_alt(n_, dst[:, c, 0:PAST], psb[b][:], [pst[b]], [t_kv])
                n_ += 1
        b = nbank([0, 1, 2, 3])
        for tc in range(4):
            pe(lambda e, b=b, tc=tc: e.transpose(out=psb[b][:, tc * 128:(tc + 1) * 128], in_=iost[:, tc, 256:384],
                                                 identity=ident[:]), [t_io, t_const], [pst[b]])
        copy_alt(0, krT[0:64, 0:PAST], psb[b][0:64, :], [pst[b]], [t_kv])
        copy_alt(1, gvS[:, 0:4, :], iost[:, :, 576:832], [t_io], [t_kv])

    def mix_proj(l, ti):
        lat = ti < NTL
        s = 0 if lat else 1
        kcol = PAST + ti * T if lat else NKL
        tcol = ti * T if lat else NLAT
        load_x_tile(ti, False)
        norm_mod(1, s)
        if lat:
            load_rope(ti)
        chunks = [(C_ZKV, 128), (C_ZKV + 128, 128), (C_ZKR, 64), (C_GK, 128), (C_GK + 128, 128)] + \
                 [(C_U + 128 * i, 128) for i in range(4)]
        mtmap = {0: 0, 1: 1, 2: 2, 3: 3, 4: 0}

        def out_T(src_ap_fn, src_tok, pn, col0, width_chunks):
            for tc in range(4):
                b = nbank([0, 1, 2, 3])
                for c in range(width_chunks):
                    pe(lambda e, b=b, c=c, tc=tc: e.transpose(
                        out=psb[b][:, c * 128:(c + 1) * 128], in_=src_ap_fn(c)[:, tc * 128:(tc + 1) * 128],
                        identity=ident[:]), [src_tok(c), t_const], [pst[b]])
                if pn == 128:
                    w_ = width_chunks * 128
                    dve(lambda e, b=b, tc=tc, w_=w_: e.tensor_copy(out=iost[:, tc, col0:col0 + w_], in_=psb[b][:, 0:w_]),
                        [pst[b]], [t_io])
                else:
                    dve(lambda e, b=b, tc=tc: e.tensor_copy(out=iost[:, tc, col0:col0 + pn], in_=psb[b][:, 0:pn]),
                        [pst[b]], [t_io])

        def finish_gk(c, mi):
            sumsq_rstd([(mt[mi][:], t_mt[mi], 128)], 128, 7)
            dve(lambda e: e.scalar_tensor_tensor(out=mt[mi][:], in0=mt[mi][:], scalar=gqn[:, l, 1:2], in1=rstd[:],
                                                 op0=ALU.mult, op1=ALU.mult), [t_mt[mi], t_rstd, t_par], [t_mt[mi]])
            if lat:
                rope(mt[mi][:], t_mt[mi], 128, perm128, rope_g, gkT[:, c, kcol:kcol + T], [t_kv])
            else:
                act(lambda e: e.activation(out=gkT[:, c, kcol:kcol + T], in_=mt[mi][:], func=AF.Copy), [t_mt[mi]], [t_kv])
                out_T(lambda cc_: mt[mi], lambda cc_: t_mt[mi], 128, 320 + c * 128, 1)

        def evac(idx, b, m):
            if idx in mtmap:
                mi = mtmap[idx]
                act(lambda e: e.activation(out=mt[mi][0:m, :], in_=psb[b][0:m, :], func=AF.Copy), [pst[b]], [t_mt[mi]])
                if idx == 1:
                    sumsq_rstd([(mt[0][:], t_mt[0], 128), (mt[1][:], t_mt[1], 128)], 256, 7)
                    for c in range(2):
                        dve(lambda e, c=c: e.scalar_tensor_tensor(
                            out=mt[c][:], in0=mt[c][:], scalar=kvnorm[:, l, c:c + 1], in1=rstd[:],
                            op0=ALU.mult, op1=ALU.mult), [t_mt[c], t_rstd, t_par], [t_mt[c]])
                        act(lambda e, c=c: e.activation(out=ckvT[:, c, kcol:kcol + T], in_=mt[c][:], func=AF.Copy),
                            [t_mt[c]], [t_kv])
                    if not lat:
                        out_T(lambda c: mt[c], lambda c: t_mt[c], 128, 0, 2)
                elif idx == 2:
                    if lat:
                        rope(mt[2][0:64, :], t_mt[2], 64, perm64, rope_m, krT[0:64, kcol:kcol + T], [t_kv])
                    else:
                        act(lambda e: e.activation(out=krT[0:64, kcol:kcol + T], in_=mt[2][0:64, :], func=AF.Copy),
                            [t_mt[2]], [t_kv])
                        out_T(lambda c: mt[2], lambda c: t_mt[2], 64, 256, 1)
                elif idx == 3:
                    finish_gk(0, 3)
                elif idx == 4:
                    finish_gk(1, 0)
            else:
                cc_ = idx - 5
                copy_alt(idx, uT[:, cc_, tcol:tcol + T], psb[b][:], [pst[b]], [t_u[cc_ * NT + ti]])

        if STOP == 21:
            return
        linear(W["w_in"][l], NCH, lambda k: hT[:, k, :], t_h, chunks if STOP != 22 else chunks[5:], evac, [0, 1, 2, 3])
        if STOP in (22, 23):
            return
        wv, wt = wload(wview(W["w_in"][l], 0, NCH, C_GV, 256), NCH, 256)
        for tc in range(4):
            b = nbank([0, 1, 2, 3])
            for k in range(NCH):
                pe(lambda e, b=b, k=k, tc=tc: e.matmul(psb[b][:, 0:256], lhsT=hT[:, k, tc * 128:(tc + 1) * 128],
                                                       rhs=wv[:, k, :], start=(k == 0), stop=(k == NCH - 1)),
                   [wt, t_h[k]], [pst[b]])
            if lat:
                act(lambda e, b=b, tc=tc: e.activation(out=gvS[:, kcol // 128 + tc, :], in_=psb[b][:, 0:256], func=AF.Copy),
                    [pst[b]], [t_kv])
            else:
                dve(lambda e, b=b, tc=tc: e.tensor_copy(out=iost[:, tc, 576:832], in_=psb[b][:, 0:256]), [pst[b]], [t_io])
                act(lambda e, tc=tc: e.activation(out=gvS[:, kcol // 128 + tc, :], in_=iost[:, tc, 576:832], func=AF.Copy),
                    [t_io], [t_kv])
        if not lat and STOP != 24:
            for (dst, c0, w_) in ((o_ckv, 0, 256), (o_kr, 256, 64), (o_k, 320, 256), (o_v, 576, 256)):
                for sq_ in range(2):
                    P.dma("sp", lambda e, dst=dst, c0=c0, w_=w_, sq_=sq_: e.dma_start(
                        out=dst[sq_, l].rearrange("(tc p) f -> p tc f", p=128), in_=iost[:, 2 * sq_:2 * sq_ + 2, c0:c0 + w_]),
                        s_oo, reads=[t_io], writes=[t_oo])

    def s5_prep(l):
        sp_ = s5p
        for d_ in range(2):
            P.dma("sp", lambda e, d_=d_: e.dma_start(out=sp_[:, d_, :, 0], in_=W["s5_lambda_re"][l, d_].rearrange(
                "g p -> (g p)").rearrange("(s q) -> q s", q=128)), s_s5, writes=[t_s5p])
            P.dma("sp", lambda e, d_=d_: e.dma_start(out=sp_[:, d_, :, 1], in_=W["s5_lambda_im"][l, d_].rearrange(
                "g p -> (g p)").rearrange("(s q) -> q s", q=128)), s_s5, writes=[t_s5p])
            for j in range(2):
                P.dma("sp", lambda e, d_=d_, j=j: e.dma_start(
                    out=sp_[64 * j:64 * j + 64, d_, :, 2],
                    in_=W["s5_log_dt"][l, d_].rearrange("(s j) -> j s", j=2)[j:j + 1, :].to_broadcast([64, 16])),
                    s_s5, writes=[t_s5p])
            P.dma("sp", lambda e, d_=d_: e.dma_start(out=s5h0[:, d_, :, :], in_=st_s5[l, d_].rearrange(
                "(s q) r -> q s r", q=128)), s_s5, writes=[t_s5p])
        A = lambda i: sp_[:, :, :, i]
        PI = math.pi
        r, w = [t_s5p, t_const], [t_s5p]
        act(lambda e: e.activation(out=A(3), in_=A(2), func=AF.Exp), r, w)
        dve(lambda e: e.tensor_tensor(out=A(4), in0=A(0), in1=A(3), op=ALU.mult), r, w)
        act(lambda e: e.activation(out=A(4), in_=A(4), func=AF.Exp, scale=1.0 / 32), r, w)
        dve(lambda e: e.tensor_tensor(out=A(7), in0=A(1), in1=A(3), op=ALU.mult), r, w)
        act(lambda e: e.activation(out=A(6), in_=A(7), func=AF.Sin, scale=1.0 / 32), r, w)
        act(lambda e: e.activation(out=A(5), in_=A(7), func=AF.Sin, scale=1.0 / 32, bias=halfpi[:, 0:1]), r, w)
        dve(lambda e: e.tensor_tensor(out=A(5), in0=A(5), in1=A(4), op=ALU.mult), r, w)
        dve(lambda e: e.tensor_tensor(out=A(6), in0=A(6), in1=A(4), op=ALU.mult), r, w)
        for _ in range(5):
            dve(lambda e: e.tensor_tensor(out=A(4), in0=A(5), in1=A(6), op=ALU.mult), r, w)
            dve(lambda e: e.tensor_tensor(out=A(5), in0=A(5), in1=A(5), op=ALU.mult), r, w)
            dve(lambda e: e.tensor_tensor(out=A(6), in0=A(6), in1=A(6), op=ALU.mult), r, w)
            dve(lambda e: e.tensor_tensor(out=A(5), in0=A(5), in1=A(6), op=ALU.subtract), r, w)
            dve(lambda e: e.tensor_tensor(out=A(6), in0=A(4), in1=A(4), op=ALU.add), r, w)
        F_ = lambda i: s5f[:, :, :, i]
        dve(lambda e: e.tensor_tensor(out=A(3), in0=A(0), in1=A(0), op=ALU.mult), r, w)
        dve(lambda e: e.tensor_tensor(out=A(4), in0=A(1), in1=A(1), op=ALU.mult), r, w)
        dve(lambda e: e.tensor_tensor(out=A(3), in0=A(3), in1=A(4), op=ALU.add), r, w)
        dve(lambda e: e.reciprocal(out=A(3), in_=A(3)), r, w)
        dve(lambda e: e.tensor_scalar(out=A(7), in0=A(5), scalar1=-1.0, scalar2=None, op0=ALU.add), r, w)
        dve(lambda e: e.tensor_tensor(out=A(4), in0=A(7), in1=A(0), op=ALU.mult), r, w)
        dve(lambda e: e.tensor_tensor(out=F_(0), in0=A(6), in1=A(1), op=ALU.mult), r, w)
        dve(lambda e: e.tensor_tensor(out=F_(0), in0=F_(0), in1=A(4), op=ALU.add), r, w)
        dve(lambda e: e.tensor_tensor(out=F_(0), in0=F_(0), in1=A(3), op=ALU.mult), r, w)
        dve(lambda e: e.tensor_tensor(out=A(4), in0=A(6), in1=A(0), op=ALU.mult), r, w)
        dve(lambda e: e.tensor_tensor(out=F_(1), in0=A(7), in1=A(1), op=ALU.mult), r, w)
        dve(lambda e: e.tensor_tensor(out=F_(1), in0=A(4), in1=F_(1), op=ALU.subtract), r, w)
        dve(lambda e: e.tensor_tensor(out=F_(1), in0=F_(1), in1=A(3), op=ALU.mult), r, w)
        dve(lambda e: e.tensor_scalar(out=F_(2), in0=F_(1), scalar1=-1.0, scalar2=None, op0=ALU.mult), r, w)
        PW = lambda k, i: s5pow[:, :, :, k, i]
        dve(lambda e: e.tensor_copy(out=PW(0, 0), in_=A(5)), r, w)
        dve(lambda e: e.tensor_copy(out=PW(0, 1), in_=A(6)), r, w)
        for k in range(NLEV):
            dve(lambda e, k=k: e.tensor_scalar(out=PW(k, 2), in0=PW(k, 1), scalar1=-1.0, scalar2=None, op0=ALU.mult), r, w)
            if k + 1 < NLEV:
                dve(lambda e, k=k: e.tensor_tensor(out=A(3), in0=PW(k, 0), in1=PW(k, 0), op=ALU.mult), r, w)
                dve(lambda e, k=k: e.tensor_tensor(out=A(4), in0=PW(k, 1), in1=PW(k, 1), op=ALU.mult), r, w)
                dve(lambda e, k=k: e.tensor_tensor(out=PW(k + 1, 0), in0=A(3), in1=A(4), op=ALU.subtract), r, w)
                dve(lambda e, k=k: e.tensor_tensor(out=A(3), in0=PW(k, 0), in1=PW(k, 1), op=ALU.mult), r, w)
                dve(lambda e, k=k: e.tensor_scalar(out=PW(k + 1, 1), in0=A(3), scalar1=2.0, scalar2=None, op0=ALU.mult), r, w)
        H = lambda i: s5h0[:, :, :, i]
        AH = lambda i: s5ah0[:, :, :, i]
        dve(lambda e: e.tensor_tensor(out=A(3), in0=A(5), in1=H(0), op=ALU.mult), r, w)
        dve(lambda e: e.tensor_tensor(out=A(4), in0=A(6), in1=H(1), op=ALU.mult), r, w)
        dve(lambda e: e.tensor_tensor(out=AH(0), in0=A(3), in1=A(4), op=ALU.subtract), r, w)
        dve(lambda e: e.tensor_tensor(out=A(3), in0=A(5), in1=H(1), op=ALU.mult), r, w)
        dve(lambda e: e.tensor_tensor(out=A(4), in0=A(6), in1=H(0), op=ALU.mult), r, w)
        dve(lambda e: e.tensor_tensor(out=AH(1), in0=A(3), in1=A(4), op=ALU.add), r, w)
        rnd = 0
        for d_ in range(2):
            for kind in range(4):
                st_, tst = S5ST[rnd % 2], t_st[rnd % 2]
                rnd += 1
                P.op("pool", lambda e, st_=st_: e.memset(st_[:], 0.0), [], [tst])
                for m in range(4):
                    for gl in range(2):
                        if kind < 2:
                            src = W["s5_b_re" if kind == 0 else "s5_b_im"][l, d_]
                            sv = src.rearrange("(i r) p c -> r p i c", r=8)[2 * m + gl]
                            dstv = st_[64 * gl:64 * gl + 64, :, 32 * m + 16 * gl:32 * m + 16 * gl + 16].rearrange(
                                "q (i r) c -> q r i c", r=4)[:, m]
                        else:
                            src = W["s5_c_re" if kind == 2 else "s5_c_im"][l, d_]
                            sv = src.rearrange("(i r) c p -> r c i p", r=8)[2 * m + gl]
                            dstv = st_[32 * m + 16 * gl:32 * m + 16 * gl + 16, :, 64 * gl:64 * gl + 64].rearrange(
                                "q (i r) c -> q r i c", r=4)[:, m]
                        P.dma("sp", lambda e, dstv=dstv, sv=sv: e.dma_start(out=dstv, in_=sv), s_s5, writes=[tst])
                dstT = bbT if kind < 2 else ccT
                ri = kind % 2
                for s0 in range(0, 16, 4):
                    b = nbank([0, 1, 2, 3])
                    for j in range(4):
                        pe(lambda e, b=b, j=j, sc=s0 + j, st_=st_: e.transpose(
                            out=psb[b][:, j * 128:(j + 1) * 128], in_=st_[:, sc, :], identity=ident[:]),
                           [tst, t_const], [pst[b]])
                    sgn = -1.0 if kind == 3 else 1.0
                    act(lambda e, b=b, s0=s0, d_=d_, ri=ri, dstT=dstT, sgn=sgn: e.activation(
                        out=dstT[:, d_, s0:s0 + 4, ri, :], in_=psb[b][:].rearrange("p (j c) -> p j c", c=128),
                        func=AF.Copy, scale=sgn), [pst[b]], [t_s5w])

    def s5_scan(l):
        W_ = SBUFW
        hb = hbuf
        hbb = [sb("hbb%d" % i, [128, 2 * W_], BF16, at=U0 + i * W_ * 4) for i in range(4)]
        for cc_ in range(4):
            for d_ in range(2):
                fwd = d_ == 0
                for scl in range(4):
                    sc = 4 * cc_ + scl
                    fr, fi, nfi = (s5f[:, d_, sc, i:i + 1] for i in range(3))
                    for phase in range(2):
                        if phase == 0:
                            off = SPAD if fwd else 0
                            pads = (0, SPAD) if fwd else (NLAT, NLAT + SPAD)
                            nlev = NLEV
                            tiles = list(range(NTL))

                            def dview(buf, sh):
                                return buf[:, off + sh:off + sh + NLAT]
                        else:
                            pads = (0, 1024)
                            nlev = 8
                            tiles = [NTL]

                            def dview(buf, sh):
                                return buf[:, 128 + sh:128 + sh + 768].rearrange("p (s c) -> p s c", c=384)[:, :, 0:256]
                        for i in range(4):
                            P.op("pool", lambda e, i=i, pads=pads: e.memset(hb[i][:, pads[0]:pads[1]], 0.0), [], [t_hb[i]])
                        for ti in tiles:
                            tcol = ti * T if ti < NTL else NLAT
                            pe(lambda e, tcol=tcol: e.matmul(psb[5][:], lhsT=bbT[:, d_, sc, 0, :], rhs=uT[:, cc_, tcol:tcol + T],
                                                             start=True, stop=True), [t_s5w, t_u[cc_ * NT + ti]], [pst[5]])
                            pe(lambda e, tcol=tcol: e.matmul(psb[6][:], lhsT=bbT[:, d_, sc, 1, :], rhs=uT[:, cc_, tcol:tcol + T],
                                                             start=True, stop=True), [t_s5w, t_u[cc_ * NT + ti]], [pst[6]])
                            if phase == 0:
                                o_r = hb[0][:, off + ti * T:off + ti * T + T]
                                o_i = hb[1][:, off + ti * T:off + ti * T + T]
                                p_r, p_i = psb[5][:], psb[6][:]
                            else:
                                o_r, o_i = dview(hb[0], 0), dview(hb[1], 0)
                                p_r = psb[5][:].rearrange("p (s c) -> p s c", c=256)
                                p_i = psb[6][:].rearrange("p (s c) -> p s c", c=256)
                            dve(lambda e, o_r=o_r, p_r=p_r: e.tensor_scalar(out=o_r, in0=p_r, scalar1=fr, scalar2=None, op0=ALU.mult),
                                [pst[5], t_s5p], [t_hb[0]])
                            dve(lambda e, o_r=o_r, p_i=p_i: e.scalar_tensor_tensor(out=o_r, in0=p_i, scalar=nfi, in1=o_r,
                                                                                   op0=ALU.mult, op1=ALU.add),
                                [pst[6], t_s5p], [t_hb[0]])
                            dve(lambda e, o_i=o_i, p_i=p_i: e.tensor_scalar(out=o_i, in0=p_i, scalar1=fr, scalar2=None, op0=ALU.mult),
                                [pst[6], t_s5p], [t_hb[1]])
                            dve(lambda e, o_i=o_i, p_r=p_r: e.scalar_tensor_tensor(out=o_i, in0=p_r, scalar=fi, in1=o_i,
                                                                                   op0=ALU.mult, op1=ALU.add),
                                [pst[5], t_s5p], [t_hb[1]])
                        if phase == 0:
                            col = off if fwd else off + NLAT - 1
                            for i in range(2):
                                dve(lambda e, i=i, col=col: e.tensor_tensor(out=hb[i][:, col:col + 1], in0=hb[i][:, col:col + 1],
                                                                            in1=s5ah0[:, d_, sc, i:i + 1], op=ALU.add),
                                    [t_s5p], [t_hb[i]])
                        src, dst = (0, 1), (2, 3)
                        for k in range(nlev):
                            sh = -(1 << k) if fwd else (1 << k)
                            pr, pi_, npi = (s5pow[:, d_, sc, k, i:i + 1] for i in range(3))
                            sr, si, dr, di = src[0], src[1], dst[0], dst[1]
                            dve(lambda e, sr=sr, dr=dr, sh=sh, pr=pr: e.scalar_tensor_tensor(
                                out=dview(hb[dr], 0), in0=dview(hb[sr], sh), scalar=pr, in1=dview(hb[sr], 0),
                                op0=ALU.mult, op1=ALU.add), [t_hb[sr], t_s5p], [t_hb[dr]])
                            dve(lambda e, si=si, dr=dr, sh=sh, npi=npi: e.scalar_tensor_tensor(
                                out=dview(hb[dr], 0), in0=dview(hb[si], sh), scalar=npi, in1=dview(hb[dr], 0),
                                op0=ALU.mult, op1=ALU.add), [t_hb[si], t_s5p], [t_hb[dr]])
                            P.op(S5ENG, lambda e, si=si, di=di, sh=sh, pr=pr: e.scalar_tensor_tensor(
                                out=dview(hb[di], 0), in0=dview(hb[si], sh), scalar=pr, in1=dview(hb[si], 0),
                                op0=ALU.mult, op1=ALU.add), [t_hb[si], t_s5p], [t_hb[di]])
                            P.op(S5ENG, lambda e, sr=sr, di=di, sh=sh, pi_=pi_: e.scalar_tensor_tensor(
                                out=dview(hb[di], 0), in0=dview(hb[sr], sh), scalar=pi_, in1=dview(hb[di], 0),
                                op0=ALU.mult, op1=ALU.add), [t_hb[sr], t_s5p], [t_hb[di]])
                            src, dst = dst, src
                        fr_, fi_ = src
                        ob = dst[0]
                        if phase == 1:
                            col = 255 if fwd else 0
                            for i, fb in enumerate((fr_, fi_)):
                                dve(lambda e, i=i, fb=fb, col=col: e.tensor_copy(
                                    out=s5fin[:, :, d_, sc, i:i + 1], in_=dview(hb[fb], 0)[:, :, col:col + 1]),
                                    [t_hb[fb]], [t_s5st])
                        nn = NLAT if phase == 0 else T
                        if phase == 0:
                            cr, ci = hbb[dst[0]][:, 0:NLAT], hbb[dst[1]][:, 0:NLAT]
                            vr, vi = cr, ci
                        else:
                            cr, ci = hbb[dst[0]][:, 0:T], hbb[dst[1]][:, 0:T]
                            vr = cr.rearrange("p (s c) -> p s c", c=256)
                            vi = ci.rearrange("p (s c) -> p s c", c=256)
                        act(lambda e, vr=vr, fr_=fr_: e.activation(out=vr, in_=dview(hb[fr_], 0), func=AF.Copy),
                            [t_hb[fr_]], [t_hb[dst[0]]])
                        act(lambda e, vi=vi, fi_=fi_: e.activation(out=vi, in_=dview(hb[fi_], 0), func=AF.Copy),
                            [t_hb[fi_]], [t_hb[dst[1]]])
                        first = (d_ == 0 and scl == 0)
                        last = (d_ == 1 and scl == 3)
                        for n_, ti in enumerate(tiles):
                            yb = ti if ti < NTL else 4
                            c0 = n_ * T
                            pe(lambda e, yb=yb, c0=c0, cr=cr, first=first: e.matmul(
                                psb[yb][:], lhsT=ccT[:, d_, sc, 0, :], rhs=cr[:, c0:c0 + T], start=first, stop=False),
                               [t_s5w, t_hb[dst[0]]], [pst[yb]])
                            pe(lambda e, yb=yb, c0=c0, ci=ci, last=last: e.matmul(
                                psb[yb][:], lhsT=ccT[:, d_, sc, 1, :], rhs=ci[:, c0:c0 + T], start=False, stop=last),
                               [t_s5w, t_hb[dst[1]]], [pst[yb]])
            for ti in range(NT):
                yb = ti if ti < NTL else 4
                tcol = ti * T if ti < NTL else NLAT
                dve(lambda e, yb=yb, tcol=tcol: e.scalar_tensor_tensor(
                    out=uT[:, cc_, tcol:tcol + T], in0=uT[:, cc_, tcol:tcol + T], scalar=s5d[:, l, cc_:cc_ + 1],
                    in1=psb[yb][:], op0=ALU.mult, op1=ALU.add), [pst[yb], t_par, t_u[cc_ * NT + ti]], [t_u[cc_ * NT + ti]])
        for sq_ in range(2):
            P.dma("sp", lambda e, sq_=sq_: e.dma_start(out=o_s5[sq_, l].rearrange("d (s q) r -> q d s r", q=128),
                                                       in_=s5fin[:, sq_, :, :, :]), s_oo, reads=[t_s5st], writes=[t_oo])

    def s5_glu(l):
        wv, wt = wload(wview(W["s5_w_glu"][l], 0, 4, 0, 512), 4, 512)
        for ti in range(NT):
            tcol = ti * T if ti < NTL else NLAT
            ut = [t_u[c * NT + ti] for c in range(4)]
            for co in range(4):
                for k in range(4):
                    pe(lambda e, co=co, k=k: e.matmul(psb[co][:], lhsT=wv[:, k, co * 128:(co + 1) * 128],
                                                      rhs=uT[:, k, tcol:tcol + T], start=(k == 0), stop=(k == 3)),
                       [wt, ut[k]], [pst[co]])
            for co in range(4):
                y = uT[:, co, tcol:tcol + T]
                t1, tt1 = next_tmp()
                dve(lambda e, y=y, t1=t1: e.tensor_tensor(out=t1[:], in0=y, in1=y, op=ALU.mult), [ut[co]], [tt1])
                dve(lambda e, t1=t1: e.tensor_scalar(out=t1[:], in0=t1[:], scalar1=0.044715, scalar2=1.0,
                                                     op0=ALU.mult, op1=ALU.add), [tt1], [tt1])
                dve(lambda e, y=y, t1=t1: e.tensor_tensor(out=t1[:], in0=t1[:], in1=y, op=ALU.mult), [tt1, ut[co]], [tt1])
                act(lambda e, t1=t1: e.activation(out=t1[:], in_=t1[:], func=AF.Sigmoid, scale=1.5957691216057308),
                    [tt1], [tt1])
                t2, tt2 = next_tmp()
                act(lambda e, t2=t2, co=co: e.activation(out=t2[:], in_=psb[co][:], func=AF.Sigmoid,
                                                         bias=bglu[:, l, co:co + 1]), [pst[co], t_par], [tt2])
                dve(lambda e, t1=t1, t2=t2: e.tensor_tensor(out=t1[:], in0=t1[:], in1=t2[:], op=ALU.mult), [tt1, tt2], [tt1])
                dve(lambda e, y=y, t1=t1: e.tensor_tensor(out=y, in0=y, in1=t1[:], op=ALU.mult), [tt1, ut[co]], [ut[co]])

    def attend(qparts, qtoks, kfn, vfn, kcs, scale, out_c, qc0, n, pair):
        bo, bd = (3, 4) if pair == 0 else (5, 6)
        nk = len(kcs)

        def qk(i):
            bs = i % 3
            kaps = kfn(kcs[i])
            for pi_, (qap, kap) in enumerate(zip(qparts, kaps)):
                pe(lambda e, bs=bs, qap=qap, kap=kap, pi_=pi_: e.matmul(
                    psb[bs][:, 0:n], lhsT=kap, rhs=qap, start=(pi_ == 0), stop=(pi_ == len(qparts) - 1)),
                   [t_kv, t_kn] + qtoks, [pst[bs]])
        qk(0)
        for i in range(nk):
            bs = i % 3
            pT_, tp = next_pT()
            act(lambda e, bs=bs, pT_=pT_: e.activation(out=pT_[:, 0:n], in_=psb[bs][:, 0:n], func=AF.Exp, scale=scale),
                [pst[bs]], [tp])
            if i + 1 < nk:
                qk(i + 1)
            pe(lambda e, pT_=pT_, i=i, vap=vfn(kcs[i]): e.matmul(psb[bo][:, 0:n], lhsT=vap, rhs=pT_[:, 0:n],
                                                                 start=(i == 0), stop=(i == nk - 1)),
               [tp, t_kv, t_vh], [pst[bo]])
            pe(lambda e, pT_=pT_, i=i: e.matmul(psb[bd][:, 0:n], lhsT=ones[:], rhs=pT_[:, 0:n],
                                                start=(i == 0), stop=(i == nk - 1)), [tp, t_const], [pst[bd]])
        tm, tt = next_tmp()
        dve(lambda e, tm=tm: e.reciprocal(out=tm[:, 0:n], in_=psb[bd][:, 0:n]), [pst[bd]], [tt])
        dve(lambda e, tm=tm: e.tensor_tensor(out=catT[:, out_c, qc0:qc0 + n], in0=psb[bo][:, 0:n], in1=tm[:, 0:n],
                                             op=ALU.mult), [pst[bo], tt], [t_cat[out_c]])

    def mix_attn(l, ti):
        lat = ti < NTL
        s = 0 if lat else 1
        tcol = ti * T if lat else NLAT
        load_x_tile(ti, False)
        norm_mod(1, s)
        if lat:
            load_rope(ti)
            kbase, nkeys = 0, NKL
        else:
            kbase, nkeys = NKL, NCTX
        def ev_zq(idx, b, m):
            act(lambda e: e.activation(out=mt[idx][:], in_=psb[b][:], func=AF.Copy), [pst[b]], [t_mt[idx]])
        linear(W["w_in"][l], NCH, lambda k: hT[:, k, :], t_h, [(C_ZQ + 128 * i, 128) for i in range(4)], ev_zq, [0, 1, 2])
        sumsq_rstd([(mt[i][:], t_mt[i], 128) for i in range(4)], 512, 7)
        for c in range(4):
            dve(lambda e, c=c: e.scalar_tensor_tensor(out=zqn[:, c, :], in0=mt[c][:], scalar=qnorm[:, l, c:c + 1],
                                                      in1=rstd[:], op0=ALU.mult, op1=ALU.mult),
                [t_mt[c], t_rstd, t_par], [t_zqn])
        def ev_gq(idx, b, m):
            act(lambda e: e.activation(out=mt[idx][:], in_=psb[b][:], func=AF.Copy), [pst[b]], [t_mt[idx]])
            sumsq_rstd([(mt[idx][:], t_mt[idx], 128)], 128, 7)
            dve(lambda e: e.scalar_tensor_tensor(out=mt[idx][:], in0=mt[idx][:], scalar=gqn[:, l, 0:1], in1=rstd[:],
                                                 op0=ALU.mult, op1=ALU.mult), [t_mt[idx], t_rstd, t_par], [t_mt[idx]])
            if lat:
                rope(mt[idx][:], t_mt[idx], 128, perm128, rope_g, gqT[:, idx, :], [t_gq[idx]])
            else:
                act(lambda e: e.activation(out=gqT[:, idx, :], in_=mt[idx][:], func=AF.Copy), [t_mt[idx]], [t_gq[idx]])
        linear(W["w_in"][l], NCH, lambda k: hT[:, k, :], t_h, [(C_GQ + 128 * i, 128) for i in range(4)], ev_gq, [0, 1, 2])
        for h in range(8):
            wq, wqt = wload(wview(W["mla_w_uq"][l], 0, 4, h * 192, 192), 4, 256, 192)
            b = nbank([0, 1, 2])
            for k in range(4):
                pe(lambda e, b=b, k=k: e.matmul(psb[b][:], lhsT=wq[:, k, 0:128], rhs=zqn[:, k, :], start=(k == 0), stop=(k == 3)),
                   [wqt, t_zqn], [pst[b]])
            act(lambda e, b=b: e.activation(out=qn_h[:], in_=psb[b][:], func=AF.Copy), [pst[b]], [t_qh])
            b = nbank([0, 1, 2])
            for k in range(4):
                pe(lambda e, b=b, k=k: e.matmul(psb[b][:, :], lhsT=wq[:, k, 128:256], rhs=zqn[:, k, :], start=(k == 0), stop=(k == 3)),
                   [wqt, t_zqn], [pst[b]])
            if lat:
                act(lambda e, b=b: e.activation(out=mt[0][0:64, :], in_=psb[b][0:64, :], func=AF.Copy), [pst[b]], [t_mt[0]])
                rope(mt[0][0:64, :], t_mt[0], 64, perm64, rope_m, qr_h[:], [t_qh])
            else:
                act(lambda e, b=b: e.activation(out=qr_h[:], in_=psb[b][0:64, :], func=AF.Copy), [pst[b]], [t_qh])
            wk, wkt = wload(wview(W["mla_w_ukv"][l], 0, 2, h * 256, 256), 2, 256)
            for c0 in range(0, nkeys, T):
                b = nbank([0, 1, 2])
                for k in range(2):
                    pe(lambda e, b=b, k=k, c0=c0: e.matmul(psb[b][:], lhsT=wk[:, k, 0:128], rhs=ckvT[:, k, kbase + c0:kbase + c0 + T],
                                                           start=(k == 0), stop=(k == 1)), [wkt, t_kv], [pst[b]])
                copy_alt(c0 // T, knope[:, c0:c0 + T], psb[b][:], [pst[b]], [t_kn])
            for kc0 in range(0, nkeys // 128, 4):
                b = nbank([0, 1, 2])
                for j in range(4):
                    for k in range(2):
                        kc = kbase // 128 + kc0 + j
                        pe(lambda e, b=b, j=j, k=k, kc=kc: e.matmul(
                            psb[b][:, j * 128:(j + 1) * 128], lhsT=ckvT[:, k, kc * 128:(kc + 1) * 128], rhs=wk[:, k, 128:256],
                            start=(k == 0), stop=(k == 1)), [wkt, t_kv], [pst[b]])
                copy_alt(kc0 // 4 + 1, vh[:, kc0:kc0 + 4, :], psb[b][:].rearrange("p (j c) -> p j c", c=128), [pst[b]], [t_vh])
            if lat:
                attend([qn_h[:], qr_h[:]], [t_qh],
                       lambda kc: [knope[:, kc * 128:(kc + 1) * 128], krT[0:64, kc * 128:(kc + 1) * 128]],
                       lambda kc: vh[:, kc, :], list(range(NKL // 128)), MLA_SCALE, h, 0, T, h % 2)
            else:
                for sq_ in range(2):
                    attend([qn_h[:, sq_ * 256:(sq_ + 1) * 256], qr_h[:, sq_ * 256:(sq_ + 1) * 256]], [t_qh],
                           lambda kc: [knope[:, kc * 128:(kc + 1) * 128], krT[0:64, NKL + kc * 128:NKL + (kc + 1) * 128]],
                           lambda kc: vh[:, kc, :], [2 * sq_, 2 * sq_ + 1], MLA_SCALE, h, sq_ * 256, 256, (2 * h + sq_) % 2)
        for g in range(4):
            kvh = g // 2
            if lat:
                attend([gqT[:, g, :]], [t_gq[g]],
                       lambda kc: [gkT[:, kvh, kc * 128:(kc + 1) * 128]],
                       lambda kc: gvS[:, kc, kvh * 128:(kvh + 1) * 128], list(range(NKL // 128)), GQA_SCALE, 12 + g, 0, T, g % 2)
            else:
                for sq_ in range(2):
                    attend([gqT[:, g, sq_ * 256:(sq_ + 1) * 256]], [t_gq[g]],
                           lambda kc: [gkT[:, kvh, kc * 128:(kc + 1) * 128]],
                           lambda kc: gvS[:, kc, kvh * 128:(kvh + 1) * 128],
                           [NKL // 128 + 2 * sq_, NKL // 128 + 2 * sq_ + 1], GQA_SCALE, 12 + g, sq_ * 256, 256, (2 * g + sq_) % 2)
        def rhs_fn(k):
            if 8 <= k < 12:
                return uT[:, k - 8, tcol:tcol + T]
            return catT[:, k, :]
        rt = [t_u[(k - 8) * NT + ti] if 8 <= k < 12 else t_cat[k] for k in range(NCH)]

        def ev_out(idx, b, m):
            dve(lambda e: e.scalar_tensor_tensor(out=xT[:, idx, :], in0=psb[b][:], scalar=geff[:, 1, idx, s:s + 1],
                                                 in1=xT[:, idx, :], op0=ALU.mult, op1=ALU.add),
                [pst[b], t_mods, t_x[idx]], [t_x[idx]])
        linear(W["w_out"][l], NCH, rhs_fn, rt, [(128 * i, 128) for i in range(NCH)], ev_out, [0, 1, 2, 7])
        store_x_tile(ti)
        P.barrier()

    def mixer(l):
        if STOP == -2:
            return
        P.barrier()
        if STOP == -1:
            return
        load_cache(l)
        if STOP == 1:
            return
        for ti in range(NT):
            mix_proj(l, ti)
        P.barrier()
        if STOP in (2, 21, 22, 23, 24):
            return
        s5_prep(l)
        P.barrier()
        if STOP == 3:
            return
        s5_scan(l)
        if STOP == 4:
            P.barrier()
            return
        s5_glu(l)
        P.barrier()
        if STOP == 5:
            return
        for ti in range(NT):
            mix_attn(l, ti)

    for l in range(L):
        compute_mods(l)
        for ti in range(NT):
            s = 0 if ti < NTL else 1
            load_x_tile(ti, first=(l == 0))
            norm_mod(0, s)
            ffn(l, 1, s)
            store_x_tile(ti)
        mixer(l)
        for ti in range(NT):
            s = 0 if ti < NTL else 1
            load_x_tile(ti, first=False)
            norm_mod(2, s)
            ffn(l, 2, s)
            if l == L - 1:
                final_out(ti)
            else:
                store_x_tile(ti)
    P.barrier()
    P.emit()
    ncd.__exit__(None, None, None)
    return nc


def _rope_tables(nlat, rot_dim):
    n_rows = nlat // 64
    row = np.repeat(np.arange(n_rows, dtype=np.float32), 64)
    col = np.tile(np.arange(64, dtype=np.float32), n_rows)
    n_freq = rot_dim // 4
    inv = (np.float32(10000.0) ** (-np.arange(n_freq, dtype=np.float32) / np.float32(n_freq))).astype(np.float32)
    ang = np.concatenate([row[:, None] * inv, col[:, None] * inv], axis=-1).astype(np.float32)
    c = np.cos(ang).astype(np.float32).T
    s = np.sin(ang).astype(np.float32).T
    return np.ascontiguousarray(np.stack([np.concatenate([c, c], 0), np.concatenate([s, s], 0)], 0))


def _perm(n):
    h = n // 2
    p = np.zeros((n, n), np.float32)
    for d in range(h):
        p[d + h, d] = -1.0
        p[d, d + h] = 1.0
    return p


def make_in_maps(inp, ncores, L, NLAT):
    consts = {
        "ident": np.eye(128, dtype=np.float32),
        "perm128": _perm(128),
        "perm64": _perm(64),
        "ropeg": _rope_tables(NLAT, 128),
        "ropem": _rope_tables(NLAT, 64),
    }
    shared = {}
    for name, shp in WEIGHT_SPECS:
        shared[name] = np.ascontiguousarray(np.asarray(inp[name], dtype=np.float32).reshape((L,) + shp))
    shared["norm_final"] = np.ascontiguousarray(np.asarray(inp["norm_final"], dtype=np.float32))
    shared.update(consts)
    maps = []
    xp = np.asarray(inp["x_prompt"])
    for i in range(ncores):
        m = dict(shared)
        m["x_lat"] = np.ascontiguousarray(np.asarray(inp["x_sample"])[i])
        m["x_ctx"] = np.ascontiguousarray(xp[2 * i:2 * i + 2].reshape(NCTX, D))
        m["cvec"] = np.ascontiguousarray(np.stack([np.asarray(inp["c"])[i], np.asarray(inp["c_ctx"])], 0))
        m["c_ckv"] = np.ascontiguousarray(np.asarray(inp["cache_mla_ckv"])[i])
        m["c_kr"] = np.ascontiguousarray(np.asarray(inp["cache_mla_krope"])[i])
        m["c_k"] = np.ascontiguousarray(np.asarray(inp["cache_gqa_k"])[i].reshape(L, PAST, 256))
        m["c_v"] = np.ascontiguousarray(np.asarray(inp["cache_gqa_v"])[i].reshape(L, PAST, 256))
        m["st_s5"] = np.ascontiguousarray(np.asarray(inp["state_s5"])[i].reshape(L, 2, 2048, 2))
        maps.append(m)
    return maps


def gather_outputs(results, ncores, L, NLAT):
    y_prompt = np.concatenate([r["y_ctx"].reshape(2, SEQ, D) for r in results], 0)
    y_sample = np.stack([r["y_lat"] for r in results], 0)
    ckv = np.concatenate([r["o_ckv"] for r in results], 0)
    kr = np.concatenate([r["o_kr"] for r in results], 0)
    k = np.concatenate([r["o_k"].reshape(2, L, SEQ, 2, 128) for r in results], 0)
    v = np.concatenate([r["o_v"].reshape(2, L, SEQ, 2, 128) for r in results], 0)
    s5 = np.concatenate([r["o_s5"].reshape(2, L, 2, 32, 64, 2) for r in results], 0)
    return tuple(np.ascontiguousarray(a.astype(np.float32)) for a in (y_prompt, y_sample, ckv, kr, k, v, s5))


_NC_CACHE = {}


def kernel(**inputs):
    L, NLAT, ncores = 4, 2048, 8
    key = (L, NLAT)
    if key not in _NC_CACHE:
        _NC_CACHE[key] = build(L, NLAT)
    nc = _NC_CACHE[key]
    maps = make_in_maps(inputs, ncores, L, NLAT)
    res = run_bass_kernel_spmd(nc, maps, core_ids=list(range(ncores)))
    return gather_outputs(res.results, ncores, L, NLAT)
```

```python
import math
import contextlib
import numpy as np
import concourse.bass as bass
import concourse.mybir as mybir
from concourse.bass_utils import run_bass_kernel_spmd

F32 = mybir.dt.float32
BF16 = mybir.dt.bfloat16
ALU = mybir.AluOpType
AF = mybir.ActivationFunctionType

ENGS = ("pe", "act", "dve", "pool", "sp")
SEM_ROT = 16000


class Tok:
    __slots__ = ("w", "r", "rd")

    def __init__(self):
        self.w = None
        self.r = {}
        self.rd = []


def toks(n):
    return [Tok() for _ in range(n)]


class DSem:
    __slots__ = ("idx", "count", "waited_max", "acked")

    def __init__(self, idx):
        self.idx = idx
        self.count = 0
        self.waited_max = 0
        self.acked = {}


class _Rec:
    def __init__(self):
        self.call = None

    def __getattr__(self, name):
        def f(*a, **k):
            self.call = (name, a, k)
            return None
        return f


def _freeze(fn):
    if fn is None:
        return None
    r = _Rec()
    fn(r)
    name, a, k = r.call
    import sys
    try:
        line = sys._getframe(3).f_lineno
    except ValueError:
        line = -1

    def g(e):
        return getattr(e, name)(*a, **k)
    g.line = line
    return g


class Prog:
    def __init__(self, nc, selfsync=True):
        self.nc = nc
        self.ops = {e: [] for e in ENGS}
        self.selfsync = selfsync
        self.dsems = []

    def new_dsem(self):
        d = DSem(len(self.dsems))
        self.dsems.append(d)
        return d

    def _collect(self, eng, reads, writes, auto_self=True, extra=()):
        deps = set(extra)
        for t in reads:
            if t.w is not None:
                deps.add(t.w)
        for t in writes:
            if t.w is not None:
                deps.add(t.w)
            for e, s in t.r.items():
                if e != eng or (self.selfsync and auto_self and eng not in ("pe", "sp")):
                    deps.add(("e", e, s))
            for d in t.rd:
                deps.add(d)
        waits = []
        for d in deps:
            if d[0] == "e":
                if d[1] == eng and d not in extra and (eng in ("pe", "sp") or not self.selfsync or not auto_self):
                    continue
                self.ops[d[1]][d[2]][2] = True
                waits.append(d)
            else:
                ds = self.dsems[d[1]]
                v = max(d[2], ds.count)
                ds.waited_max = max(ds.waited_max, v)
                waits.append(("d", d[1], v))
        return waits

    def op(self, eng, fn, reads=(), writes=(), auto_self=True, extra=()):
        waits = self._collect(eng, reads, writes, auto_self, tuple(e for e in extra if e is not None))
        seq = len(self.ops[eng])
        fn = _freeze(fn)
        self.ops[eng].append([fn, waits, False, None])
        ev = ("e", eng, seq)
        for t in reads:
            if t.r.get(eng, -1) < seq:
                t.r[eng] = seq
        for t in writes:
            t.w = ev
            t.r = {}
            t.rd = []
        return ev

    def dma(self, q, fn, dsem, reads=(), writes=()):
        waits = self._collect(q, reads, writes)
        if dsem.waited_max > dsem.acked.get(q, 0):
            waits.append(("d", dsem.idx, dsem.waited_max))
            dsem.acked[q] = dsem.waited_max
        fn = _freeze(fn)
        self.ops[q].append([fn, waits, False, dsem.idx])
        dsem.count += 16
        ev = ("d", dsem.idx, dsem.count)
        for t in reads:
            t.rd.append(ev)
        for t in writes:
            t.w = ev
            t.r = {}
            t.rd = []
        return ev

    def barrier(self):
        last = {}
        for e in ENGS:
            for i in range(len(self.ops[e]) - 1, -1, -1):
                if self.ops[e][i][0] is not None and self.ops[e][i][3] is None:
                    last[e] = i
                    break
        for e in ENGS:
            waits = []
            for f, s in last.items():
                if f != e:
                    self.ops[f][s][2] = True
                    waits.append(("e", f, s))
            for d in self.dsems:
                if d.count:
                    waits.append(("d", d.idx, d.count))
                    d.waited_max = max(d.waited_max, d.count)
            self.ops[e].append([None, waits, False, None])

    def emit(self):
        nc = self.nc
        with contextlib.ExitStack() as st:
            nsig = {e: sum(1 for o in self.ops[e] if o[2]) for e in ENGS}
            esems = {}
            for e in ENGS:
                n = nsig[e] // SEM_ROT + 1
                esems[e] = [st.enter_context(nc.semaphore("s_%s%d" % (e, i))) for i in range(n)]
            dsem_h = [st.enter_context(nc.semaphore("d%d" % i)) for i in range(len(self.dsems))]
            signum = {}
            for e in ENGS:
                c = 0
                arr = []
                for o in self.ops[e]:
                    if o[2]:
                        c += 1
                    arr.append(c)
                signum[e] = arr
            block = st.enter_context(nc.Block())

            def run(ename, eng):
                waited = {}
                sc = 0
                for (fn, waits, signal, dsi) in self.ops[ename]:
                    for w in waits:
                        if w[0] == "e":
                            n = signum[w[1]][w[2]]
                            k = (n - 1) // SEM_ROT
                            v = n - k * SEM_ROT
                            key = ("e", w[1], k)
                            sem = esems[w[1]][k]
                        else:
                            key = ("d", w[1])
                            v = w[2]
                            sem = dsem_h[w[1]]
                        if waited.get(key, 0) >= v:
                            continue
                        waited[key] = v
                        eng.wait_ge(sem, v)
                    if fn is None:
                        continue
                    try:
                        ins = fn(eng)
                    except BaseException as ex:
                        print("EMIT FAIL on", ename, "line", getattr(fn, "line", None), repr(ex)[:300])
                        raise
                    if dsi is not None:
                        ins.then_inc(dsem_h[dsi], 16)
                    elif signal:
                        sc += 1
                        ins.then_inc(esems[ename][(sc - 1) // SEM_ROT], 1)

            @block.tensor
            def _(eng):
                run("pe", eng)

            @block.scalar
            def _(eng):
                run("act", eng)

            @block.vector
            def _(eng):
                run("dve", eng)

            @block.gpsimd
            def _(eng):
                run("pool", eng)

            @block.sync
            def _(eng):
                run("sp", eng)


D = 2048
DFF = 4096
T = 512
NCH = 16
EPS = 1e-6
PAST = 512
SEQ = 256
NCTX = 512
D_IN = 2368
C_ZQ, C_ZKV, C_ZKR, C_U, C_GQ, C_GK, C_GV = 0, 512, 768, 832, 1344, 1856, 2112
MLA_SCALE = 1.0 / math.sqrt(192.0)
GQA_SCALE = 1.0 / math.sqrt(128.0)
SLOT = 4096
NSLOT = 4

WEIGHT_SPECS = [
    ("w_ada", (D, 9 * D)), ("b_ada", (9 * D,)), ("norm_ffn1", (D,)),
    ("ffn1_w_gate", (D, DFF)), ("ffn1_w_up", (D, DFF)), ("ffn1_w_down", (DFF, D)),
    ("norm_mix", (D,)), ("w_in", (D, D_IN)), ("mla_q_norm", (512,)), ("mla_w_uq", (512, 1536)),
    ("mla_kv_norm", (256,)), ("mla_w_ukv", (256, 2048)),
    ("s5_lambda_re", (2, 32, 64)), ("s5_lambda_im", (2, 32, 64)), ("s5_log_dt", (2, 32)),
    ("s5_b_re", (2, 32, 64, 16)), ("s5_b_im", (2, 32, 64, 16)),
    ("s5_c_re", (2, 32, 16, 64)), ("s5_c_im", (2, 32, 16, 64)),
    ("s5_d", (512,)), ("s5_w_glu", (512, 512)), ("s5_b_glu", (512,)),
    ("gqa_q_norm", (128,)), ("gqa_k_norm", (128,)), ("w_out", (D, D)), ("norm_ffn2", (D,)),
    ("ffn2_w_gate", (D, DFF)), ("ffn2_w_up", (D, DFF)), ("ffn2_w_down", (DFF, D)),
]


def build(L=4, NLAT=2048, debug=None, STOP=0):
    nc = bass.Bass("TRN2", target_bir_lowering=False)
    P = Prog(nc)
    NTL = NLAT // T
    NT = NTL + 1
    NTOK = NLAT + NCTX
    NKL = PAST + NLAT
    KTOT = NKL + NCTX
    SPAD = 1024 if NLAT > 1024 else NLAT // 2
    NLEV = int(math.log2(NLAT))

    def din(name, shape):
        return nc.dram_tensor(name, list(shape), F32, kind="ExternalInput").ap()

    def dout(name, shape):
        return nc.dram_tensor(name, list(shape), F32, kind="ExternalOutput").ap()

    x_lat = din("x_lat", (NLAT, D))
    x_ctx = din("x_ctx", (NCTX, D))
    cvec = din("cvec", (2, D))
    c_ckv = din("c_ckv", (L, PAST, 256))
    c_kr = din("c_kr", (L, PAST, 64))
    c_k = din("c_k", (L, PAST, 256))
    c_v = din("c_v", (L, PAST, 256))
    st_s5 = din("st_s5", (L, 2, 2048, 2))
    W = {}
    for name, shp in WEIGHT_SPECS:
        W[name] = din(name, (L,) + shp)
    norm_final = din("norm_final", (D,))
    ident_d = din("ident", (128, 128))
    perm128_d = din("perm128", (128, 128))
    perm64_d = din("perm64", (64, 64))
    ropeg_d = din("ropeg", (2, 128, NLAT))
    ropem_d = din("ropem", (2, 64, NLAT))

    y_lat = dout("y_lat", (NLAT, D))
    y_ctx = dout("y_ctx", (NCTX, D))
    o_ckv = dout("o_ckv", (2, L, SEQ, 256))
    o_kr = dout("o_kr", (2, L, SEQ, 64))
    o_k = dout("o_k", (2, L, SEQ, 256))
    o_v = dout("o_v", (2, L, SEQ, 256))
    o_s5 = dout("o_s5", (2, L, 2, 2048, 2))
    xs_d = nc.dram_tensor("xs_scratch", [NT, 128, NCH, T], F32).ap()
    dbg = {}
    if debug:
        for nm, shp in debug.items():
            dbg[nm] = dout("dbg_" + nm, shp)

    SB_BASE = 16512
    SB_END = 229376
    cur = [SB_BASE]

    def sb(name, shape, dt, at=None):
        nb = int(np.prod(shape[1:])) * (4 if dt == F32 else 2)
        nb = (nb + 63) // 64 * 64
        if at is None:
            off = cur[0]
            cur[0] += nb
            assert cur[0] <= SB_END, ("SBUF overflow", name, cur[0])
        else:
            off = at
        return nc.alloc_sbuf_tensor_at(name, list(shape), dt, offset=off)

    ident = sb("ident", [128, 128], F32)
    identb = sb("identb", [128, 128], BF16)
    perm128 = sb("perm128", [128, 128], F32)
    perm64 = sb("perm64", [64, 64], F32)
    ones = sb("ones", [128, 128], BF16)
    perm128b = sb("perm128b", [128, 128], BF16)
    perm64b = sb("perm64b", [64, 64], BF16)
    epsc = sb("epsc", [128, 8], F32)
    cT = sb("cT", [128, NCH, 2], F32)
    csT = sb("csT", [128, NCH, 2], BF16)
    badaT = sb("badaT", [128, L, 144], F32)
    gnorm = sb("gnorm", [128, L, 3, NCH], F32)
    gfin = sb("gfin", [128, NCH], F32)
    qnorm = sb("qnorm", [128, L, 4], F32)
    kvnorm = sb("kvnorm", [128, L, 2], F32)
    gqn = sb("gqn", [128, L, 2], F32)
    s5d = sb("s5d", [128, L, 4], F32)
    bglu = sb("bglu", [128, L, 4], F32)
    mods = sb("mods", [128, 9, NCH, 2], F32)
    gmod = sb("gmod", [128, 3, NCH, 2], F32)
    geff = sb("geff", [128, 3, NCH, 2], F32)
    ROPE0 = cur[0]
    rope_g = sb("rope_g", [128, 2, T], F32)
    rope_m = sb("rope_m", [64, 2, T], F32)
    ckvT = sb("ckvT", [128, 2, KTOT], BF16)
    krT = sb("krT", [64, KTOT], BF16)
    gkT = sb("gkT", [128, 2, KTOT], BF16)
    gvS = sb("gvS", [128, KTOT // 128, 256], BF16)
    uT = sb("uT", [128, 4, NTOK], BF16)
    ring = [sb("ring%d" % i, [128, SLOT], BF16) for i in range(NSLOT)]
    rstd = sb("rstd", [128, T], F32)
    tmpA = [sb("tmpA%d" % i, [128, T], F32) for i in range(3)]
    sqb = [sb("sqb%d" % i, [128, T], BF16) for i in range(2)]
    pTb = [sb("pTb%d" % i, [128, T], BF16) for i in range(4)]
    U0 = cur[0]
    xT = sb("xT", [128, NCH, T], F32)
    hT = sb("hT", [128, NCH, T], BF16)
    A0 = cur[0]
    aT = sb("aT", [128, 32, T], BF16)
    UEND = cur[0]
    H0_ = U0 + NCH * T * 4
    catT = sb("catT", [128, NCH, T], BF16, at=A0)
    zqn = sb("zqn", [128, 4, T], BF16, at=A0 + 16384)
    gqT = sb("gqT", [128, 4, T], BF16, at=A0 + 20480)
    knope = sb("knope", [128, NKL], BF16, at=H0_)
    vh = sb("vh", [128, NKL // 128, 128], BF16, at=H0_ + NKL * 2)
    qn_h = sb("qn_h", [128, T], BF16, at=H0_ + NKL * 4)
    qr_h = sb("qr_h", [64, T], BF16, at=H0_ + NKL * 4 + 1024)
    assert NKL * 4 + 2048 <= NCH * T * 2
    stage = sb("stage", [128, D], F32, at=A0)
    SBUFW = max(SPAD + NLAT, 1024)
    hbuf = [sb("hbuf%d" % i, [128, SBUFW], F32, at=U0 + i * SBUFW * 4) for i in range(4)]
    o = U0 + 4 * SBUFW * 4
    bbT = sb("bbT", [128, 2, 16, 2, 128], BF16, at=o); o += 2 * 16 * 2 * 128 * 2
    ccT = sb("ccT", [128, 2, 16, 2, 128], BF16, at=o); o += 2 * 16 * 2 * 128 * 2
    assert o <= UEND, ("s5 carve overflow", o - UEND)
    negpi = sb("negpi", [128, 8], F32)
    halfpi = sb("halfpi", [128, 8], F32)
    import os as _os
    S5ENG = _os.environ.get("S5ENG", "dve")
    o = ROPE0
    s5p = sb("s5p", [128, 2, 16, 8], F32, at=o); o += 1024
    s5pow = sb("s5pow", [128, 2, 16, 12, 3], F32, at=o); o += 4608
    s5h0 = sb("s5h0", [128, 2, 16, 2], F32, at=o); o += 256
    s5ah0 = sb("s5ah0", [128, 2, 16, 2], F32, at=o); o += 256
    s5fin = sb("s5fin", [128, 2, 2, 16, 2], F32, at=o); o += 512
    s5f = sb("s5f", [128, 2, 16, 3], F32, at=o); o += 384
    assert o <= ROPE0 + 8192
    s5frow = None
    print("SBUF used", cur[0] - SB_BASE, "of", SB_END - SB_BASE)

    psb = [nc.alloc_psum_tensor("psb%d" % i, [128, T], F32) for i in range(8)]
    pst = toks(8)

    t_const = Tok()
    t_par = Tok()
    t_mods = Tok()
    t_x = toks(NCH)
    t_h = toks(NCH)
    t_a = toks(32)
    t_ring = toks(NSLOT)
    s_ring = [P.new_dsem() for _ in range(NSLOT)]
    t_xs = toks(NT)
    s_xld = P.new_dsem()
    s_xst = P.new_dsem()
    s_misc = P.new_dsem()
    s_out = P.new_dsem()
    t_out = Tok()
    t_rstd = Tok()
    t_tmpA = toks(3)
    t_sqb = toks(2)
    t_pT = toks(4)
    t_rope = Tok()
    s_rope = P.new_dsem()
    t_kv = Tok()
    t_u = toks(4 * NT)
    t_stage = Tok()
    s_stage = P.new_dsem()
    cnt = {"ring": 0, "tmp": 0, "sq": 0, "pT": 0}

    def act(fn, r, w):
        return P.op("act", fn, r, w)

    def dve(fn, r, w):
        return P.op("dve", fn, r, w)

    def pe(fn, r, w):
        return P.op("pe", fn, r, w)

    def wload(src_ap, nk, ncols, ndma=None):
        s = cnt["ring"] % NSLOT
        cnt["ring"] += 1
        dst = ring[s][:, 0:nk * ncols].rearrange("p (k c) -> p k c", c=ncols)
        dd = dst if ndma is None else dst[:, :, 0:ndma]
        P.dma("pool", lambda e: e.dma_start(out=dd, in_=src_ap), s_ring[s], writes=[t_ring[s]])
        return dst, t_ring[s]

    def wview(wap, k0, nk, c0, ncols):
        return wap.rearrange("(k p) n -> p k n", p=128)[:, k0:k0 + nk, c0:c0 + ncols]

    def next_tmp():
        i = cnt["tmp"] % 3
        cnt["tmp"] += 1
        return tmpA[i], t_tmpA[i]

    def next_sq():
        i = cnt["sq"] % 2
        cnt["sq"] += 1
        return sqb[i], t_sqb[i]

    def next_pT():
        i = cnt["pT"] % 4
        cnt["pT"] += 1
        return pTb[i], t_pT[i]

    ncd = nc.allow_non_contiguous_dma(reason="small strided parameter loads")
    ncd.__enter__()

    def sdma(out, in_, tok):
        P.dma("sp", lambda e: e.dma_start(out=out, in_=in_), s_misc, writes=[tok])

    sdma(ident[:], ident_d, t_const)
    sdma(perm128[:], perm128_d, t_const)
    sdma(perm64[:], perm64_d, t_const)
    for r in range(2):
        sdma(cT[:, :, r], cvec[r].rearrange("(k p) -> p k", p=128), t_par)
    for l in range(L):
        sdma(badaT[:, l, :], W["b_ada"][l].rearrange("(c p) -> p c", p=128), t_par)
        for j, nm in enumerate(("norm_ffn1", "norm_mix", "norm_ffn2")):
            sdma(gnorm[:, l, j, :], W[nm][l].rearrange("(k p) -> p k", p=128), t_par)
        sdma(qnorm[:, l, :], W["mla_q_norm"][l].rearrange("(k p) -> p k", p=128), t_par)
        sdma(kvnorm[:, l, :], W["mla_kv_norm"][l].rearrange("(k p) -> p k", p=128), t_par)
        sdma(gqn[:, l, 0:1], W["gqa_q_norm"][l].rearrange("(p o) -> p o", o=1), t_par)
        sdma(gqn[:, l, 1:2], W["gqa_k_norm"][l].rearrange("(p o) -> p o", o=1), t_par)
        sdma(s5d[:, l, :], W["s5_d"][l].rearrange("(k p) -> p k", p=128), t_par)
        sdma(bglu[:, l, :], W["s5_b_glu"][l].rearrange("(k p) -> p k", p=128), t_par)
    sdma(gfin[:], norm_final.rearrange("(k p) -> p k", p=128), t_par)
    P.op("pool", lambda e: e.memset(ones[:], 1.0), [], [t_const])
    for i_ in range(NSLOT):
        P.op("pool", lambda e, i_=i_: e.memset(ring[i_][:], 0.0), [], [t_ring[i_]])
    P.op("pool", lambda e: e.memset(epsc[:], EPS), [], [t_const])
    P.op("pool", lambda e: e.memset(negpi[:], -math.pi), [], [t_const])
    P.op("pool", lambda e: e.memset(halfpi[:], 0.5 * math.pi), [], [t_const])
    act(lambda e: e.activation(out=identb[:], in_=ident[:], func=AF.Copy), [t_const], [t_const])
    act(lambda e: e.activation(out=perm128b[:], in_=perm128[:], func=AF.Copy), [t_const], [t_const])
    act(lambda e: e.activation(out=perm64b[:], in_=perm64[:], func=AF.Copy), [t_const], [t_const])
    tm, tt = next_tmp()
    act(lambda e: e.activation(out=tm[:, 0:32], in_=cT[:].rearrange("p k r -> p (k r)"), func=AF.Sigmoid), [t_par], [tt])
    dve(lambda e: e.tensor_tensor(out=csT[:].rearrange("p k r -> p (k r)"), in0=tm[:, 0:32],
                                  in1=cT[:].rearrange("p k r -> p (k r)"), op=ALU.mult), [tt, t_par], [t_par])

    def load_x_tile(ti, first):
        if not first:
            P.dma("sp", lambda e: e.dma_start(out=xT[:], in_=xs_d[ti]), s_xld, reads=[t_xs[ti]], writes=t_x)
            return
        src = x_lat[ti * T:(ti + 1) * T, :] if ti < NTL else x_ctx
        for tc4 in range(4):
            P.dma("sp", lambda e, tc4=tc4: e.dma_start(out=stage[:], in_=src[tc4 * 128:(tc4 + 1) * 128, :]),
                  s_stage, writes=[t_stage] + t_a)
            for k0 in range(0, NCH, 4):
                b = (k0 // 4) % 2
                for j in range(4):
                    pe(lambda e, b=b, j=j, k=k0 + j: e.transpose(
                        out=psb[b][:, j * 128:(j + 1) * 128], in_=stage[:, k * 128:(k + 1) * 128], identity=ident[:]),
                       [t_stage, t_const], [pst[b]])
                dve(lambda e, b=b, k0=k0, tc4=tc4: e.tensor_copy(
                    out=xT[:, k0:k0 + 4, tc4 * 128:(tc4 + 1) * 128],
                    in_=psb[b][:].rearrange("p (j t) -> p j t", t=128)), [pst[b]], t_x[k0:k0 + 4])

    def store_x_tile(ti):
        P.dma("sp", lambda e: e.dma_start(out=xs_d[ti], in_=xT[:]), s_xst, reads=t_x, writes=[t_xs[ti]])

    def sumsq_rstd(chunks, nfeat, bank, n=T):
        nchk = len(chunks)
        for i, (ap, tk, pn) in enumerate(chunks):
            sq, tq = next_sq()
            act(lambda e, sq=sq, ap=ap, pn=pn: e.activation(out=sq[0:pn, 0:n], in_=ap, func=AF.Square), [tk], [tq])
            pe(lambda e, sq=sq, pn=pn, i=i: e.matmul(psb[bank][:, 0:n], lhsT=ones[0:pn, :], rhs=sq[0:pn, 0:n],
                                                     start=(i == 0), stop=(i == nchk - 1)), [tq, t_const], [pst[bank]])
        act(lambda e: e.activation(out=rstd[:, 0:n], in_=psb[bank][:, 0:n], func=AF.Sqrt, scale=1.0 / nfeat,
                                   bias=epsc[:, 0:1]), [pst[bank], t_const], [t_rstd])
        dve(lambda e: e.reciprocal(out=rstd[:, 0:n], in_=rstd[:, 0:n]), [t_rstd], [t_rstd])

    def norm_mod(j, s):
        sumsq_rstd([(xT[:, k, :], t_x[k], 128) for k in range(NCH)], D, 7)
        for k in range(NCH):
            tm, tt = next_tmp()
            dve(lambda e, tm=tm, k=k: e.tensor_tensor(out=tm[:], in0=xT[:, k, :], in1=rstd[:], op=ALU.mult),
                [t_x[k], t_rstd], [tt])
            act(lambda e, tm=tm, k=k: e.activation(out=hT[:, k, :], in_=tm[:], func=AF.Identity,
                                                   scale=gmod[:, j, k, s:s + 1], bias=mods[:, 3 * j, k, s:s + 1]),
                [tt, t_mods], [t_h[k]])

    def compute_mods(l):
        bank = 6
        for sl in range(72):
            wv, wt = wload(wview(W["w_ada"][l], 0, NCH, sl * 256, 256), NCH, 256)
            for c in range(2):
                ch = 2 * sl + c
                for k in range(NCH):
                    pe(lambda e, wv=wv, c=c, k=k, ch=ch: e.matmul(
                        psb[bank][:, 2 * ch:2 * ch + 2], lhsT=wv[:, k, c * 128:(c + 1) * 128], rhs=csT[:, k, :],
                        start=(k == 0), stop=(k == NCH - 1)), [wt, t_par], [pst[bank]])
        dve(lambda e: e.tensor_tensor(out=mods[:].rearrange("p v k s -> p (v k) s"),
                                      in0=psb[bank][:, 0:288].rearrange("p (c s) -> p c s", s=2),
                                      in1=badaT[:, l, :].unsqueeze(2).to_broadcast([128, 144, 2]), op=ALU.add),
            [pst[bank], t_par], [t_mods])
        for j in range(3):
            dve(lambda e, j=j: e.scalar_tensor_tensor(
                out=gmod[:, j, :, :], in0=mods[:, 3 * j + 1, :, :], scalar=1.0,
                in1=gnorm[:, l, j, :].unsqueeze(2).to_broadcast([128, NCH, 2]), op0=ALU.add, op1=ALU.mult),
                [t_mods, t_par], [t_mods])
            dve(lambda e, j=j: e.tensor_scalar(out=geff[:, j, :, :], in0=mods[:, 3 * j + 2, :, :],
                                               scalar1=(1.0 if j == 1 else 0.5), scalar2=None, op0=ALU.mult),
                [t_mods], [t_mods])

    def ffn(l, which, s):
        j = 0 if which == 1 else 2
        wg, wu, wd = (W["ffn%d_w_gate" % which][l], W["ffn%d_w_up" % which][l], W["ffn%d_w_down" % which][l])
        for sl in range(16):
            gv_, gt = wload(wview(wg, 0, NCH, sl * 256, 256), NCH, 256)
            uv_, ut = wload(wview(wu, 0, NCH, sl * 256, 256), NCH, 256)
            for c in range(2):
                f = 2 * sl + c
                bg, bu = f % 2, 2 + f % 2
                for k in range(NCH):
                    pe(lambda e, gv_=gv_, c=c, k=k, bg=bg: e.matmul(
                        psb[bg][:], lhsT=gv_[:, k, c * 128:(c + 1) * 128], rhs=hT[:, k, :],
                        start=(k == 0), stop=(k == NCH - 1)), [gt, t_h[k]], [pst[bg]])
                for k in range(NCH):
                    pe(lambda e, uv_=uv_, c=c, k=k, bu=bu: e.matmul(
                        psb[bu][:], lhsT=uv_[:, k, c * 128:(c + 1) * 128], rhs=hT[:, k, :],
                        start=(k == 0), stop=(k == NCH - 1)), [ut, t_h[k]], [pst[bu]])
                tm, tt = next_tmp()
                act(lambda e, tm=tm, bg=bg: e.activation(out=tm[:], in_=psb[bg][:], func=AF.Silu), [pst[bg]], [tt])
                dve(lambda e, tm=tm, bu=bu, f=f: e.tensor_tensor(out=aT[:, f, :], in0=tm[:], in1=psb[bu][:], op=ALU.mult),
                    [tt, pst[bu]], [t_a[f]])
        for sl in range(8):
            b0 = 4 + 2 * (sl % 2)
            for part in range(2):
                dv_, dt_ = wload(wview(wd, part * 16, 16, sl * 256, 256), 16, 256)
                for c in range(2):
                    for k in range(16):
                        kk = part * 16 + k
                        pe(lambda e, dv_=dv_, c=c, k=k, kk=kk, b=b0 + c, part=part: e.matmul(
                            psb[b][:], lhsT=dv_[:, k, c * 128:(c + 1) * 128], rhs=aT[:, kk, :],
                            start=(kk == 0), stop=(kk == 31)), [dt_, t_a[kk]], [pst[b0 + c]])
            for c in range(2):
                d_ = 2 * sl + c
                dve(lambda e, d_=d_, b=b0 + c: e.scalar_tensor_tensor(
                    out=xT[:, d_, :], in0=psb[b][:], scalar=geff[:, j, d_, s:s + 1], in1=xT[:, d_, :],
                    op0=ALU.mult, op1=ALU.add), [pst[b0 + c], t_mods, t_x[d_]], [t_x[d_]])

    def final_out(ti):
        sumsq_rstd([(xT[:, k, :], t_x[k], 128) for k in range(NCH)], D, 7)
        for k in range(NCH):
            dve(lambda e, k=k: e.scalar_tensor_tensor(out=xT[:, k, :], in0=xT[:, k, :], scalar=gfin[:, k:k + 1],
                                                      in1=rstd[:], op0=ALU.mult, op1=ALU.mult),
                [t_x[k], t_rstd, t_par], [t_x[k]])
        dst = y_lat[ti * T:(ti + 1) * T, :] if ti < NTL else y_ctx
        for tc4 in range(4):
            for k0 in range(0, NCH, 4):
                b = (k0 // 4) % 2
                for jj in range(4):
                    pe(lambda e, b=b, jj=jj, k=k0 + jj, tc4=tc4: e.transpose(
                        out=psb[b][:, jj * 128:(jj + 1) * 128], in_=xT[:, k, tc4 * 128:(tc4 + 1) * 128], identity=ident[:]),
                       [t_x[k0 + jj], t_const], [pst[b]])
                dve(lambda e, b=b, k0=k0: e.tensor_copy(out=stage[:, k0 * 128:(k0 + 4) * 128], in_=psb[b][:]),
                    [pst[b]], [t_stage])
            P.dma("sp", lambda e, tc4=tc4: e.dma_start(out=dst[tc4 * 128:(tc4 + 1) * 128, :], in_=stage[:]),
                  s_out, reads=[t_stage] + t_a, writes=[t_out])

    H0 = U0 + NCH * T * 4
    mt = [sb("mt%d" % i, [128, T], F32, at=A0 + 24576 + i * 2048) for i in range(4)]
    t_mt = toks(4)
    iost = sb("iost", [128, 4, 832], F32, at=A0)
    t_io = Tok()
    s_io = P.new_dsem()
    s_oo = P.new_dsem()
    t_oo = Tok()
    t_cat = toks(NCH)
    t_kn = Tok()
    t_vh = Tok()
    t_zqn = Tok()
    t_gq = toks(4)
    t_qh = Tok()
    t_hb = toks(4)
    t_s5w = Tok()
    t_s5p = Tok()
    s_s5 = P.new_dsem()
    t_s5st = Tok()
    S5ST = [sb("s5st%d" % i, [128, 16, 128], F32, at=U0 + i * 8192) for i in range(2)]
    t_st = toks(2)
    bankrot = [0]

    def nbank(lst):
        b = lst[bankrot[0] % len(lst)]
        bankrot[0] += 1
        return b

    def copy_alt(i, out, in_, r, w):
        if i % 2 == 0:
            act(lambda e: e.activation(out=out, in_=in_, func=AF.Copy), r, w)
        else:
            dve(lambda e: e.tensor_copy(out=out, in_=in_), r, w)

    def linear(wap, nk, rhs_fn, rhs_toks, chunks, evac, banks, n=T):
        i = 0
        while i < len(chunks):
            c_start = chunks[i][0]
            j = i
            while j < len(chunks) and chunks[j][0] >= c_start and (chunks[j][0] + max(chunks[j][1], 128) - c_start) * nk <= SLOT:
                j += 1
            ncols = max(c[0] + max(c[1], 128) for c in chunks[i:j]) - c_start
            wv, wt = wload(wview(wap, 0, nk, c_start, ncols), nk, ncols)
            for idx in range(i, j):
                c0, m = chunks[idx]
                b = nbank(banks)
                for k in range(nk):
                    pe(lambda e, wv=wv, k=k, b=b, o=c0 - c_start: e.matmul(
                        psb[b][:, 0:n], lhsT=wv[:, k, o:o + 128], rhs=rhs_fn(k), start=(k == 0), stop=(k == nk - 1)),
                       [wt, rhs_toks[k]], [pst[b]])
                evac(idx, b, m)
            i = j

    def rope(xap, xtok, pn, perm, tab, out_ap, out_toks):
        b = nbank([4, 5])
        permb = perm128b if pn == 128 else perm64b
        xb, txb = next_sq()
        act(lambda e: e.activation(out=xb[0:pn, :], in_=xap, func=AF.Copy), [xtok], [txb])
        pe(lambda e: e.matmul(psb[b][0:pn, :], lhsT=permb[0:pn, 0:pn], rhs=xb[0:pn, :], start=True, stop=True),
           [txb, t_const], [pst[b]])
        tm, tt = next_tmp()
        dve(lambda e: e.tensor_tensor(out=tm[0:pn, :], in0=xap, in1=tab[0:pn, 0, :], op=ALU.mult), [xtok, t_rope], [tt])
        dve(lambda e: e.tensor_tensor(out=xap, in0=psb[b][0:pn, :], in1=tab[0:pn, 1, :], op=ALU.mult),
            [pst[b], t_rope], [xtok])
        dve(lambda e: e.tensor_tensor(out=out_ap, in0=tm[0:pn, :], in1=xap, op=ALU.add), [tt, xtok], out_toks)

    def load_rope(ti):
        P.dma("sp", lambda e: e.dma_start(out=rope_g[:], in_=ropeg_d[:, :, ti * T:(ti + 1) * T].rearrange("a p t -> p a t")),
              s_rope, writes=[t_rope])
        P.dma("sp", lambda e: e.dma_start(out=rope_m[:], in_=ropem_d[:, :, ti * T:(ti + 1) * T].rearrange("a p t -> p a t")),
              s_rope, writes=[t_rope])

    def load_cache(l):
        for src, c0, w_ in ((c_ckv, 0, 256), (c_kr, 256, 64), (c_k, 320, 256), (c_v, 576, 256)):
            P.dma("sp", lambda e, src=src, c0=c0, w_=w_: e.dma_start(
                out=iost[:, :, c0:c0 + w_], in_=src[l].rearrange("(tc p) f -> p tc f", p=128)), s_io, writes=[t_io])
        n_ = 0
        for (c0, dst) in ((0, ckvT), (320, gkT)):
            for c in range(2):
                b = nbank([0, 1, 2, 3])
                for tc in range(4):
                    pe(lambda e, b=b, tc=tc, o=c0 + c * 128: e.transpose(
                        out=psb[b][:, tc * 128:(tc + 1) * 128], in_=iost[:, tc, o:o + 128], identity=ident[:]),
                       [t_io, t_const], [pst[b]])
                copy_alt(n_, dst[:, c, 0:PAST], psb[b][:], [pst[b]], [t_kv])
                n_ += 1
        b = nbank([0, 1, 2, 3])
        for tc in range(4):
            pe(lambda e, b=b, tc=tc: e.transpose(out=psb[b][:, tc * 128:(tc + 1) * 128], in_=iost[:, tc, 256:384],
                                                 identity=ident[:]), [t_io, t_const], [pst[b]])
        copy_alt(0, krT[0:64, 0:PAST], psb[b][0:64, :], [pst[b]], [t_kv])
        copy_alt(1, gvS[:, 0:4, :], iost[:, :, 576:832], [t_io], [t_kv])

    def mix_proj(l, ti):
        lat = ti < NTL
        s = 0 if lat else 1
        kcol = PAST + ti * T if lat else NKL
        tcol = ti * T if lat else NLAT
        load_x_tile(ti, False)
        norm_mod(1, s)
        if lat:
            load_rope(ti)
        chunks = [(C_ZKV, 128), (C_ZKV + 128, 128), (C_ZKR, 64), (C_GK, 128), (C_GK + 128, 128)] + \
                 [(C_U + 128 * i, 128) for i in range(4)]
        mtmap = {0: 0, 1: 1, 2: 2, 3: 3, 4: 0}

        def out_T(src_ap_fn, src_tok, pn, col0, width_chunks):
            for tc in range(4):
                b = nbank([0, 1, 2, 3])
                for c in range(width_chunks):
                    pe(lambda e, b=b, c=c, tc=tc: e.transpose(
                        out=psb[b][:, c * 128:(c + 1) * 128], in_=src_ap_fn(c)[:, tc * 128:(tc + 1) * 128],
                        identity=ident[:]), [src_tok(c), t_const], [pst[b]])
                if pn == 128:
                    w_ = width_chunks * 128
                    dve(lambda e, b=b, tc=tc, w_=w_: e.tensor_copy(out=iost[:, tc, col0:col0 + w_], in_=psb[b][:, 0:w_]),
                        [pst[b]], [t_io])
                else:
                    dve(lambda e, b=b, tc=tc: e.tensor_copy(out=iost[:, tc, col0:col0 + pn], in_=psb[b][:, 0:pn]),
                        [pst[b]], [t_io])

        def finish_gk(c, mi):
            sumsq_rstd([(mt[mi][:], t_mt[mi], 128)], 128, 7)
            dve(lambda e: e.scalar_tensor_tensor(out=mt[mi][:], in0=mt[mi][:], scalar=gqn[:, l, 1:2], in1=rstd[:],
                                                 op0=ALU.mult, op1=ALU.mult), [t_mt[mi], t_rstd, t_par], [t_mt[mi]])
            if lat:
                rope(mt[mi][:], t_mt[mi], 128, perm128, rope_g, gkT[:, c, kcol:kcol + T], [t_kv])
            else:
                act(lambda e: e.activation(out=gkT[:, c, kcol:kcol + T], in_=mt[mi][:], func=AF.Copy), [t_mt[mi]], [t_kv])
                out_T(lambda cc_: mt[mi], lambda cc_: t_mt[mi], 128, 320 + c * 128, 1)

        def evac(idx, b, m):
            if idx in mtmap:
                mi = mtmap[idx]
                act(lambda e: e.activation(out=mt[mi][0:m, :], in_=psb[b][0:m, :], func=AF.Copy), [pst[b]], [t_mt[mi]])
                if idx == 1:
                    sumsq_rstd([(mt[0][:], t_mt[0], 128), (mt[1][:], t_mt[1], 128)], 256, 7)
                    for c in range(2):
                        dve(lambda e, c=c: e.scalar_tensor_tensor(
                            out=mt[c][:], in0=mt[c][:], scalar=kvnorm[:, l, c:c + 1], in1=rstd[:],
                            op0=ALU.mult, op1=ALU.mult), [t_mt[c], t_rstd, t_par], [t_mt[c]])
                        act(lambda e, c=c: e.activation(out=ckvT[:, c, kcol:kcol + T], in_=mt[c][:], func=AF.Copy),
                            [t_mt[c]], [t_kv])
                    if not lat:
                        out_T(lambda c: mt[c], lambda c: t_mt[c], 128, 0, 2)
                elif idx == 2:
                    if lat:
                        rope(mt[2][0:64, :], t_mt[2], 64, perm64, rope_m, krT[0:64, kcol:kcol + T], [t_kv])
                    else:
                        act(lambda e: e.activation(out=krT[0:64, kcol:kcol + T], in_=mt[2][0:64, :], func=AF.Copy),
                            [t_mt[2]], [t_kv])
                        out_T(lambda c: mt[2], lambda c: t_mt[2], 64, 256, 1)
                elif idx == 3:
                    finish_gk(0, 3)
                elif idx == 4:
                    finish_gk(1, 0)
            else:
                cc_ = idx - 5
                copy_alt(idx, uT[:, cc_, tcol:tcol + T], psb[b][:], [pst[b]], [t_u[cc_ * NT + ti]])

        if STOP == 21:
            return
        linear(W["w_in"][l], NCH, lambda k: hT[:, k, :], t_h, chunks if STOP != 22 else chunks[5:], evac, [0, 1, 2, 3])
        if STOP in (22, 23):
            return
        wv, wt = wload(wview(W["w_in"][l], 0, NCH, C_GV, 256), NCH, 256)
        for tc in range(4):
            b = nbank([0, 1, 2, 3])
            for k in range(NCH):
                pe(lambda e, b=b, k=k, tc=tc: e.matmul(psb[b][:, 0:256], lhsT=hT[:, k, tc * 128:(tc + 1) * 128],
                                                       rhs=wv[:, k, :], start=(k == 0), stop=(k == NCH - 1)),
                   [wt, t_h[k]], [pst[b]])
            if lat:
                act(lambda e, b=b, tc=tc: e.activation(out=gvS[:, kcol // 128 + tc, :], in_=psb[b][:, 0:256], func=AF.Copy),
                    [pst[b]], [t_kv])
            else:
                dve(lambda e, b=b, tc=tc: e.tensor_copy(out=iost[:, tc, 576:832], in_=psb[b][:, 0:256]), [pst[b]], [t_io])
                act(lambda e, tc=tc: e.activation(out=gvS[:, kcol // 128 + tc, :], in_=iost[:, tc, 576:832], func=AF.Copy),
                    [t_io], [t_kv])
        if not lat and STOP != 24:
            for (dst, c0, w_) in ((o_ckv, 0, 256), (o_kr, 256, 64), (o_k, 320, 256), (o_v, 576, 256)):
                for sq_ in range(2):
                    P.dma("sp", lambda e, dst=dst, c0=c0, w_=w_, sq_=sq_: e.dma_start(
                        out=dst[sq_, l].rearrange("(tc p) f -> p tc f", p=128), in_=iost[:, 2 * sq_:2 * sq_ + 2, c0:c0 + w_]),
                        s_oo, reads=[t_io], writes=[t_oo])

    def s5_prep(l):
        sp_ = s5p
        for d_ in range(2):
            P.dma("sp", lambda e, d_=d_: e.dma_start(out=sp_[:, d_, :, 0], in_=W["s5_lambda_re"][l, d_].rearrange(
                "g p -> (g p)").rearrange("(s q) -> q s", q=128)), s_s5, writes=[t_s5p])
            P.dma("sp", lambda e, d_=d_: e.dma_start(out=sp_[:, d_, :, 1], in_=W["s5_lambda_im"][l, d_].rearrange(
                "g p -> (g p)").rearrange("(s q) -> q s", q=128)), s_s5, writes=[t_s5p])
            for j in range(2):
                P.dma("sp", lambda e, d_=d_, j=j: e.dma_start(
                    out=sp_[64 * j:64 * j + 64, d_, :, 2],
                    in_=W["s5_log_dt"][l, d_].rearrange("(s j) -> j s", j=2)[j:j + 1, :].to_broadcast([64, 16])),
                    s_s5, writes=[t_s5p])
            P.dma("sp", lambda e, d_=d_: e.dma_start(out=s5h0[:, d_, :, :], in_=st_s5[l, d_].rearrange(
                "(s q) r -> q s r", q=128)), s_s5, writes=[t_s5p])
        A = lambda i: sp_[:, :, :, i]
        PI = math.pi
        r, w = [t_s5p, t_const], [t_s5p]
        act(lambda e: e.activation(out=A(3), in_=A(2), func=AF.Exp), r, w)
        dve(lambda e: e.tensor_tensor(out=A(4), in0=A(0), in1=A(3), op=ALU.mult), r, w)
        act(lambda e: e.activation(out=A(4), in_=A(4), func=AF.Exp, scale=1.0 / 32), r, w)
        dve(lambda e: e.tensor_tensor(out=A(7), in0=A(1), in1=A(3), op=ALU.mult), r, w)
        act(lambda e: e.activation(out=A(6), in_=A(7), func=AF.Sin, scale=1.0 / 32), r, w)
        act(lambda e: e.activation(out=A(5), in_=A(7), func=AF.Sin, scale=1.0 / 32, bias=halfpi[:, 0:1]), r, w)
        dve(lambda e: e.tensor_tensor(out=A(5), in0=A(5), in1=A(4), op=ALU.mult), r, w)
        dve(lambda e: e.tensor_tensor(out=A(6), in0=A(6), in1=A(4), op=ALU.mult), r, w)
        for _ in range(5):
            dve(lambda e: e.tensor_tensor(out=A(4), in0=A(5), in1=A(6), op=ALU.mult), r, w)
            dve(lambda e: e.tensor_tensor(out=A(5), in0=A(5), in1=A(5), op=ALU.mult), r, w)
            dve(lambda e: e.tensor_tensor(out=A(6), in0=A(6), in1=A(6), op=ALU.mult), r, w)
            dve(lambda e: e.tensor_tensor(out=A(5), in0=A(5), in1=A(6), op=ALU.subtract), r, w)
            dve(lambda e: e.tensor_tensor(out=A(6), in0=A(4), in1=A(4), op=ALU.add), r, w)
        F_ = lambda i: s5f[:, :, :, i]
        dve(lambda e: e.tensor_tensor(out=A(3), in0=A(0), in1=A(0), op=ALU.mult), r, w)
        dve(lambda e: e.tensor_tensor(out=A(4), in0=A(1), in1=A(1), op=ALU.mult), r, w)
        dve(lambda e: e.tensor_tensor(out=A(3), in0=A(3), in1=A(4), op=ALU.add), r, w)
        dve(lambda e: e.reciprocal(out=A(3), in_=A(3)), r, w)
        dve(lambda e: e.tensor_scalar(out=A(7), in0=A(5), scalar1=-1.0, scalar2=None, op0=ALU.add), r, w)
        dve(lambda e: e.tensor_tensor(out=A(4), in0=A(7), in1=A(0), op=ALU.mult), r, w)
        dve(lambda e: e.tensor_tensor(out=F_(0), in0=A(6), in1=A(1), op=ALU.mult), r, w)
        dve(lambda e: e.tensor_tensor(out=F_(0), in0=F_(0), in1=A(4), op=ALU.add), r, w)
        dve(lambda e: e.tensor_tensor(out=F_(0), in0=F_(0), in1=A(3), op=ALU.mult), r, w)
        dve(lambda e: e.tensor_tensor(out=A(4), in0=A(6), in1=A(0), op=ALU.mult), r, w)
        dve(lambda e: e.tensor_tensor(out=F_(1), in0=A(7), in1=A(1), op=ALU.mult), r, w)
        dve(lambda e: e.tensor_tensor(out=F_(1), in0=A(4), in1=F_(1), op=ALU.subtract), r, w)
        dve(lambda e: e.tensor_tensor(out=F_(1), in0=F_(1), in1=A(3), op=ALU.mult), r, w)
        dve(lambda e: e.tensor_scalar(out=F_(2), in0=F_(1), scalar1=-1.0, scalar2=None, op0=ALU.mult), r, w)
        PW = lambda k, i: s5pow[:, :, :, k, i]
        dve(lambda e: e.tensor_copy(out=PW(0, 0), in_=A(5)), r, w)
        dve(lambda e: e.tensor_copy(out=PW(0, 1), in_=A(6)), r, w)
        for k in range(NLEV):
            dve(lambda e, k=k: e.tensor_scalar(out=PW(k, 2), in0=PW(k, 1), scalar1=-1.0, scalar2=None, op0=ALU.mult), r, w)
            if k + 1 < NLEV:
                dve(lambda e, k=k: e.tensor_tensor(out=A(3), in0=PW(k, 0), in1=PW(k, 0), op=ALU.mult), r, w)
                dve(lambda e, k=k: e.tensor_tensor(out=A(4), in0=PW(k, 1), in1=PW(k, 1), op=ALU.mult), r, w)
                dve(lambda e, k=k: e.tensor_tensor(out=PW(k + 1, 0), in0=A(3), in1=A(4), op=ALU.subtract), r, w)
                dve(lambda e, k=k: e.tensor_tensor(out=A(3), in0=PW(k, 0), in1=PW(k, 1), op=ALU.mult), r, w)
                dve(lambda e, k=k: e.tensor_scalar(out=PW(k + 1, 1), in0=A(3), scalar1=2.0, scalar2=None, op0=ALU.mult), r, w)
        H = lambda i: s5h0[:, :, :, i]
        AH = lambda i: s5ah0[:, :, :, i]
        dve(lambda e: e.tensor_tensor(out=A(3), in0=A(5), in1=H(0), op=ALU.mult), r, w)
        dve(lambda e: e.tensor_tensor(out=A(4), in0=A(6), in1=H(1), op=ALU.mult), r, w)
        dve(lambda e: e.tensor_tensor(out=AH(0), in0=A(3), in1=A(4), op=ALU.subtract), r, w)
        dve(lambda e: e.tensor_tensor(out=A(3), in0=A(5), in1=H(1), op=ALU.mult), r, w)
        dve(lambda e: e.tensor_tensor(out=A(4), in0=A(6), in1=H(0), op=ALU.mult), r, w)
        dve(lambda e: e.tensor_tensor(out=AH(1), in0=A(3), in1=A(4), op=ALU.add), r, w)
        rnd = 0
        for d_ in range(2):
            for kind in range(4):
                st_, tst = S5ST[rnd % 2], t_st[rnd % 2]
                rnd += 1
                P.op("pool", lambda e, st_=st_: e.memset(st_[:], 0.0), [], [tst])
                for m in range(4):
                    for gl in range(2):
                        if kind < 2:
                            src = W["s5_b_re" if kind == 0 else "s5_b_im"][l, d_]
                            sv = src.rearrange("(i r) p c -> r p i c", r=8)[2 * m + gl]
                            dstv = st_[64 * gl:64 * gl + 64, :, 32 * m + 16 * gl:32 * m + 16 * gl + 16].rearrange(
                                "q (i r) c -> q r i c", r=4)[:, m]
                        else:
                            src = W["s5_c_re" if kind == 2 else "s5_c_im"][l, d_]
                            sv = src.rearrange("(i r) c p -> r c i p", r=8)[2 * m + gl]
                            dstv = st_[32 * m + 16 * gl:32 * m + 16 * gl + 16, :, 64 * gl:64 * gl + 64].rearrange(
                                "q (i r) c -> q r i c", r=4)[:, m]
                        P.dma("sp", lambda e, dstv=dstv, sv=sv: e.dma_start(out=dstv, in_=sv), s_s5, writes=[tst])
                dstT = bbT if kind < 2 else ccT
                ri = kind % 2
                for s0 in range(0, 16, 4):
                    b = nbank([0, 1, 2, 3])
                    for j in range(4):
                        pe(lambda e, b=b, j=j, sc=s0 + j, st_=st_: e.transpose(
                            out=psb[b][:, j * 128:(j + 1) * 128], in_=st_[:, sc, :], identity=ident[:]),
                           [tst, t_const], [pst[b]])
                    sgn = -1.0 if kind == 3 else 1.0
                    act(lambda e, b=b, s0=s0, d_=d_, ri=ri, dstT=dstT, sgn=sgn: e.activation(
                        out=dstT[:, d_, s0:s0 + 4, ri, :], in_=psb[b][:].rearrange("p (j c) -> p j c", c=128),
                        func=AF.Copy, scale=sgn), [pst[b]], [t_s5w])

    def s5_scan(l):
        HR, HI = hbuf[0], hbuf[1]
        hbb = [sb("hbb%d" % i, [128, 2 * SBUFW], BF16, at=U0 + i * SBUFW * 4) for i in range(4)]
        HRb, HIb = hbb[2], hbb[3]
        KL = NLEV

        def strided(buf, start, step, count):
            return buf[:, start:start + (count - 1) * step + 1:step]

        def ctxv(buf, start, step, count):
            return buf[:, NLAT:NLAT + NCTX].rearrange("p (s c) -> p s c", c=SEQ)[:, :, start:start + (count - 1) * step + 1:step]

        for cc_ in range(4):
            for d_ in range(2):
                fwd = d_ == 0
                for scl in range(4):
                    sc = 4 * cc_ + scl
                    fr, fi, nfi = (s5f[:, d_, sc, i:i + 1] for i in range(3))
                    last_r = last_i = None
                    for ti in range(NT):
                        tcol = ti * T if ti < NTL else NLAT
                        pe(lambda e: e.matmul(psb[5][:], lhsT=bbT[:, d_, sc, 0, :], rhs=uT[:, cc_, tcol:tcol + T],
                                              start=True, stop=True), [t_s5w, t_u[cc_ * NT + ti]], [pst[5]])
                        pe(lambda e: e.matmul(psb[6][:], lhsT=bbT[:, d_, sc, 1, :], rhs=uT[:, cc_, tcol:tcol + T],
                                              start=True, stop=True), [t_s5w, t_u[cc_ * NT + ti]], [pst[6]])
                        o_r, o_i = HR[:, tcol:tcol + T], HI[:, tcol:tcol + T]
                        e1 = dve(lambda e: e.tensor_scalar(out=o_r, in0=psb[5][:], scalar1=fr, scalar2=None, op0=ALU.mult),
                                 [pst[5], t_s5p], [t_hb[0]])
                        e2 = dve(lambda e: e.tensor_scalar(out=o_i, in0=psb[6][:], scalar1=fr, scalar2=None, op0=ALU.mult),
                                 [pst[6], t_s5p], [t_hb[1]])
                        last_r = dve(lambda e: e.scalar_tensor_tensor(out=o_r, in0=psb[6][:], scalar=nfi, in1=o_r,
                                                                      op0=ALU.mult, op1=ALU.add), [pst[6], t_s5p], [t_hb[0]])
                        last_i = dve(lambda e: e.scalar_tensor_tensor(out=o_i, in0=psb[5][:], scalar=fi, in1=o_i,
                                                                      op0=ALU.mult, op1=ALU.add), [pst[5], t_s5p], [t_hb[1]])
                    col = 0 if fwd else NLAT - 1
                    last_r = dve(lambda e: e.tensor_tensor(out=HR[:, col:col + 1], in0=HR[:, col:col + 1],
                                                           in1=s5ah0[:, d_, sc, 0:1], op=ALU.add), [t_s5p], [t_hb[0]])
                    last_i = dve(lambda e: e.tensor_tensor(out=HI[:, col:col + 1], in0=HI[:, col:col + 1],
                                                           in1=s5ah0[:, d_, sc, 1:2], op=ALU.add), [t_s5p], [t_hb[1]])

                    def update(k, views):
                        nonlocal last_r, last_i
                        pr, pi_, npi = (s5pow[:, d_, sc, k, i:i + 1] for i in range(3))
                        for (dv, sv) in views:
                            rd, rs, id_, is_ = dv(HR), sv(HR), dv(HI), sv(HI)
                            a1 = P.op("dve", lambda e: e.scalar_tensor_tensor(out=rd, in0=rs, scalar=pr, in1=rd, op0=ALU.mult, op1=ALU.add),
                                      [t_hb[0], t_s5p], [t_hb[0]], auto_self=False, extra=[last_r])
                            a3 = P.op("dve", lambda e: e.scalar_tensor_tensor(out=id_, in0=is_, scalar=pr, in1=id_, op0=ALU.mult, op1=ALU.add),
                                      [t_hb[1], t_s5p], [t_hb[1]], auto_self=False, extra=[last_i])
                            a2 = P.op("dve", lambda e: e.scalar_tensor_tensor(out=rd, in0=is_, scalar=npi, in1=rd, op0=ALU.mult, op1=ALU.add),
                                      [t_hb[0], t_hb[1], t_s5p], [t_hb[0]], auto_self=False, extra=[a1, last_i])
                            a4 = P.op("dve", lambda e: e.scalar_tensor_tensor(out=id_, in0=rs, scalar=pi_, in1=id_, op0=ALU.mult, op1=ALU.add),
                                      [t_hb[0], t_hb[1], t_s5p], [t_hb[1]], auto_self=False, extra=[a3, last_r])
                            last_r, last_i = a2, a4

                    for k in range(KL):
                        step, dd = 2 << k, 1 << k
                        ncol = NTOK if step <= SEQ else NLAT
                        cnt_ = ncol // step
                        if fwd:
                            update(k, [(lambda b, step=step, cnt_=cnt_: strided(b, step - 1, step, cnt_),
                                        lambda b, step=step, cnt_=cnt_, dd=dd: strided(b, step - 1 - dd, step, cnt_))])
                        else:
                            update(k, [(lambda b, step=step, cnt_=cnt_: strided(b, 0, step, cnt_),
                                        lambda b, step=step, cnt_=cnt_, dd=dd: strided(b, dd, step, cnt_))])
                    for k in range(KL - 2, -1, -1):
                        step, dd = 2 << k, 1 << k
                        views = []
                        cl = NLAT // step - 1
                        if cl >= 1:
                            if fwd:
                                views.append((lambda b, step=step, dd=dd, cl=cl: strided(b, step + dd - 1, step, cl),
                                              lambda b, step=step, dd=dd, cl=cl: strided(b, step - 1, step, cl)))
                            else:
                                views.append((lambda b, step=step, dd=dd, cl=cl: strided(b, dd, step, cl),
                                              lambda b, step=step, dd=dd, cl=cl: strided(b, 2 * dd, step, cl)))
                        cx = SEQ // step - 1
                        if cx >= 1:
                            if fwd:
                                views.append((lambda b, step=step, dd=dd, cx=cx: ctxv(b, step + dd - 1, step, cx),
                                              lambda b, step=step, dd=dd, cx=cx: ctxv(b, step - 1, step, cx)))
                            else:
                                views.append((lambda b, step=step, dd=dd, cx=cx: ctxv(b, dd, step, cx),
                                              lambda b, step=step, dd=dd, cx=cx: ctxv(b, 2 * dd, step, cx)))
                        update(k, views)
                    col = SEQ - 1 if fwd else 0
                    for i, (hb_, lst) in enumerate(((HR, last_r), (HI, last_i))):
                        P.op("dve", lambda e: e.tensor_copy(out=s5fin[:, :, d_, sc, i:i + 1], in_=ctxv(hb_, col, 1, 1)),
                             [t_hb[i]], [t_s5st], auto_self=False, extra=[lst])
                    act(lambda e: e.activation(out=HRb[:, 0:NTOK], in_=HR[:, 0:NTOK], func=AF.Copy), [t_hb[0]], [t_hb[2]])
                    act(lambda e: e.activation(out=HIb[:, 0:NTOK], in_=HI[:, 0:NTOK], func=AF.Copy), [t_hb[1]], [t_hb[3]])
                    first = (d_ == 0 and scl == 0)
                    last = (d_ == 1 and scl == 3)
                    for ti in range(NT):
                        yb = ti if ti < NTL else 4
                        tcol = ti * T if ti < NTL else NLAT
                        pe(lambda e: e.matmul(psb[yb][:], lhsT=ccT[:, d_, sc, 0, :], rhs=HRb[:, tcol:tcol + T],
                                              start=first, stop=False), [t_s5w, t_hb[2]], [pst[yb]])
                        pe(lambda e: e.matmul(psb[yb][:], lhsT=ccT[:, d_, sc, 1, :], rhs=HIb[:, tcol:tcol + T],
                                              start=False, stop=last), [t_s5w, t_hb[3]], [pst[yb]])
            for ti in range(NT):
                yb = ti if ti < NTL else 4
                tcol = ti * T if ti < NTL else NLAT
                dve(lambda e: e.scalar_tensor_tensor(
                    out=uT[:, cc_, tcol:tcol + T], in0=uT[:, cc_, tcol:tcol + T], scalar=s5d[:, l, cc_:cc_ + 1],
                    in1=psb[yb][:], op0=ALU.mult, op1=ALU.add), [pst[yb], t_par, t_u[cc_ * NT + ti]], [t_u[cc_ * NT + ti]])
        for sq_ in range(2):
            P.dma("sp", lambda e: e.dma_start(out=o_s5[sq_, l].rearrange("d (s q) r -> q d s r", q=128),
                                              in_=s5fin[:, sq_, :, :, :]), s_oo, reads=[t_s5st], writes=[t_oo])

    def s5_glu(l):
        wv, wt = wload(wview(W["s5_w_glu"][l], 0, 4, 0, 512), 4, 512)
        for ti in range(NT):
            tcol = ti * T if ti < NTL else NLAT
            ut = [t_u[c * NT + ti] for c in range(4)]
            for co in range(4):
                for k in range(4):
                    pe(lambda e, co=co, k=k: e.matmul(psb[co][:], lhsT=wv[:, k, co * 128:(co + 1) * 128],
                                                      rhs=uT[:, k, tcol:tcol + T], start=(k == 0), stop=(k == 3)),
                       [wt, ut[k]], [pst[co]])
            for co in range(4):
                y = uT[:, co, tcol:tcol + T]
                t1, tt1 = next_tmp()
                dve(lambda e, y=y, t1=t1: e.tensor_tensor(out=t1[:], in0=y, in1=y, op=ALU.mult), [ut[co]], [tt1])
                dve(lambda e, t1=t1: e.tensor_scalar(out=t1[:], in0=t1[:], scalar1=0.044715, scalar2=1.0,
                                                     op0=ALU.mult, op1=ALU.add), [tt1], [tt1])
                dve(lambda e, y=y, t1=t1: e.tensor_tensor(out=t1[:], in0=t1[:], in1=y, op=ALU.mult), [tt1, ut[co]], [tt1])
                act(lambda e, t1=t1: e.activation(out=t1[:], in_=t1[:], func=AF.Sigmoid, scale=1.5957691216057308),
                    [tt1], [tt1])
                t2, tt2 = next_tmp()
                act(lambda e, t2=t2, co=co: e.activation(out=t2[:], in_=psb[co][:], func=AF.Sigmoid,
                                                         bias=bglu[:, l, co:co + 1]), [pst[co], t_par], [tt2])
                dve(lambda e, t1=t1, t2=t2: e.tensor_tensor(out=t1[:], in0=t1[:], in1=t2[:], op=ALU.mult), [tt1, tt2], [tt1])
                dve(lambda e, y=y, t1=t1: e.tensor_tensor(out=y, in0=y, in1=t1[:], op=ALU.mult), [tt1, ut[co]], [ut[co]])

    def attend(qparts, qtoks, kfn, vfn, kcs, scale, out_c, qc0, n, pair):
        bo, bd = (3, 4) if pair == 0 else (5, 6)
        nk = len(kcs)

        def qk(i):
            bs = i % 3
            kaps = kfn(kcs[i])
            for pi_, (qap, kap) in enumerate(zip(qparts, kaps)):
                pe(lambda e, bs=bs, qap=qap, kap=kap, pi_=pi_: e.matmul(
                    psb[bs][:, 0:n], lhsT=kap, rhs=qap, start=(pi_ == 0), stop=(pi_ == len(qparts) - 1)),
                   [t_kv, t_kn] + qtoks, [pst[bs]])
        qk(0)
        for i in range(nk):
            bs = i % 3
            pT_, tp = next_pT()
            act(lambda e, bs=bs, pT_=pT_: e.activation(out=pT_[:, 0:n], in_=psb[bs][:, 0:n], func=AF.Exp, scale=scale),
                [pst[bs]], [tp])
            if i + 1 < nk:
                qk(i + 1)
            pe(lambda e, pT_=pT_, i=i, vap=vfn(kcs[i]): e.matmul(psb[bo][:, 0:n], lhsT=vap, rhs=pT_[:, 0:n],
                                                                 start=(i == 0), stop=(i == nk - 1)),
               [tp, t_kv, t_vh], [pst[bo]])
            pe(lambda e, pT_=pT_, i=i: e.matmul(psb[bd][:, 0:n], lhsT=ones[:], rhs=pT_[:, 0:n],
                                                start=(i == 0), stop=(i == nk - 1)), [tp, t_const], [pst[bd]])
        tm, tt = next_tmp()
        dve(lambda e, tm=tm: e.reciprocal(out=tm[:, 0:n], in_=psb[bd][:, 0:n]), [pst[bd]], [tt])
        dve(lambda e, tm=tm: e.tensor_tensor(out=catT[:, out_c, qc0:qc0 + n], in0=psb[bo][:, 0:n], in1=tm[:, 0:n],
                                             op=ALU.mult), [pst[bo], tt], [t_cat[out_c]])

    def mix_attn(l, ti):
        lat = ti < NTL
        s = 0 if lat else 1
        tcol = ti * T if lat else NLAT
        load_x_tile(ti, False)
        norm_mod(1, s)
        if lat:
            load_rope(ti)
            kbase, nkeys = 0, NKL
        else:
            kbase, nkeys = NKL, NCTX
        def ev_zq(idx, b, m):
            act(lambda e: e.activation(out=mt[idx][:], in_=psb[b][:], func=AF.Copy), [pst[b]], [t_mt[idx]])
        linear(W["w_in"][l], NCH, lambda k: hT[:, k, :], t_h, [(C_ZQ + 128 * i, 128) for i in range(4)], ev_zq, [0, 1, 2])
        sumsq_rstd([(mt[i][:], t_mt[i], 128) for i in range(4)], 512, 7)
        for c in range(4):
            dve(lambda e, c=c: e.scalar_tensor_tensor(out=zqn[:, c, :], in0=mt[c][:], scalar=qnorm[:, l, c:c + 1],
                                                      in1=rstd[:], op0=ALU.mult, op1=ALU.mult),
                [t_mt[c], t_rstd, t_par], [t_zqn])
        def ev_gq(idx, b, m):
            act(lambda e: e.activation(out=mt[idx][:], in_=psb[b][:], func=AF.Copy), [pst[b]], [t_mt[idx]])
            sumsq_rstd([(mt[idx][:], t_mt[idx], 128)], 128, 7)
            dve(lambda e: e.scalar_tensor_tensor(out=mt[idx][:], in0=mt[idx][:], scalar=gqn[:, l, 0:1], in1=rstd[:],
                                                 op0=ALU.mult, op1=ALU.mult), [t_mt[idx], t_rstd, t_par], [t_mt[idx]])
            if lat:
                rope(mt[idx][:], t_mt[idx], 128, perm128, rope_g, gqT[:, idx, :], [t_gq[idx]])
            else:
                act(lambda e: e.activation(out=gqT[:, idx, :], in_=mt[idx][:], func=AF.Copy), [t_mt[idx]], [t_gq[idx]])
        linear(W["w_in"][l], NCH, lambda k: hT[:, k, :], t_h, [(C_GQ + 128 * i, 128) for i in range(4)], ev_gq, [0, 1, 2])
        for h in range(8):
            wq, wqt = wload(wview(W["mla_w_uq"][l], 0, 4, h * 192, 192), 4, 256, 192)
            b = nbank([0, 1, 2])
            for k in range(4):
                pe(lambda e, b=b, k=k: e.matmul(psb[b][:], lhsT=wq[:, k, 0:128], rhs=zqn[:, k, :], start=(k == 0), stop=(k == 3)),
                   [wqt, t_zqn], [pst[b]])
            act(lambda e, b=b: e.activation(out=qn_h[:], in_=psb[b][:], func=AF.Copy), [pst[b]], [t_qh])
            b = nbank([0, 1, 2])
            for k in range(4):
                pe(lambda e, b=b, k=k: e.matmul(psb[b][:, :], lhsT=wq[:, k, 128:256], rhs=zqn[:, k, :], start=(k == 0), stop=(k == 3)),
                   [wqt, t_zqn], [pst[b]])
            if lat:
                act(lambda e, b=b: e.activation(out=mt[0][0:64, :], in_=psb[b][0:64, :], func=AF.Copy), [pst[b]], [t_mt[0]])
                rope(mt[0][0:64, :], t_mt[0], 64, perm64, rope_m, qr_h[:], [t_qh])
            else:
                act(lambda e, b=b: e.activation(out=qr_h[:], in_=psb[b][0:64, :], func=AF.Copy), [pst[b]], [t_qh])
            wk, wkt = wload(wview(W["mla_w_ukv"][l], 0, 2, h * 256, 256), 2, 256)
            for c0 in range(0, nkeys, T):
                b = nbank([0, 1, 2])
                for k in range(2):
                    pe(lambda e, b=b, k=k, c0=c0: e.matmul(psb[b][:], lhsT=wk[:, k, 0:128], rhs=ckvT[:, k, kbase + c0:kbase + c0 + T],
                                                           start=(k == 0), stop=(k == 1)), [wkt, t_kv], [pst[b]])
                copy_alt(c0 // T, knope[:, c0:c0 + T], psb[b][:], [pst[b]], [t_kn])
            for kc0 in range(0, nkeys // 128, 4):
                b = nbank([0, 1, 2])
                for j in range(4):
                    for k in range(2):
                        kc = kbase // 128 + kc0 + j
                        pe(lambda e, b=b, j=j, k=k, kc=kc: e.matmul(
                            psb[b][:, j * 128:(j + 1) * 128], lhsT=ckvT[:, k, kc * 128:(kc + 1) * 128], rhs=wk[:, k, 128:256],
                            start=(k == 0), stop=(k == 1)), [wkt, t_kv], [pst[b]])
                copy_alt(kc0 // 4 + 1, vh[:, kc0:kc0 + 4, :], psb[b][:].rearrange("p (j c) -> p j c", c=128), [pst[b]], [t_vh])
            if lat:
                attend([qn_h[:], qr_h[:]], [t_qh],
                       lambda kc: [knope[:, kc * 128:(kc + 1) * 128], krT[0:64, kc * 128:(kc + 1) * 128]],
                       lambda kc: vh[:, kc, :], list(range(NKL // 128)), MLA_SCALE, h, 0, T, h % 2)
            else:
                for sq_ in range(2):
                    attend([qn_h[:, sq_ * 256:(sq_ + 1) * 256], qr_h[:, sq_ * 256:(sq_ + 1) * 256]], [t_qh],
                           lambda kc: [knope[:, kc * 128:(kc + 1) * 128], krT[0:64, NKL + kc * 128:NKL + (kc + 1) * 128]],
                           lambda kc: vh[:, kc, :], [2 * sq_, 2 * sq_ + 1], MLA_SCALE, h, sq_ * 256, 256, (2 * h + sq_) % 2)
        for g in range(4):
            kvh = g // 2
            if lat:
                attend([gqT[:, g, :]], [t_gq[g]],
                       lambda kc: [gkT[:, kvh, kc * 128:(kc + 1) * 128]],
                       lambda kc: gvS[:, kc, kvh * 128:(kvh + 1) * 128], list(range(NKL // 128)), GQA_SCALE, 12 + g, 0, T, g % 2)
            else:
                for sq_ in range(2):
                    attend([gqT[:, g, sq_ * 256:(sq_ + 1) * 256]], [t_gq[g]],
                           lambda kc: [gkT[:, kvh, kc * 128:(kc + 1) * 128]],
                           lambda kc: gvS[:, kc, kvh * 128:(kvh + 1) * 128],
                           [NKL // 128 + 2 * sq_, NKL // 128 + 2 * sq_ + 1], GQA_SCALE, 12 + g, sq_ * 256, 256, (2 * g + sq_) % 2)
        def rhs_fn(k):
            if 8 <= k < 12:
                return uT[:, k - 8, tcol:tcol + T]
            return catT[:, k, :]
        rt = [t_u[(k - 8) * NT + ti] if 8 <= k < 12 else t_cat[k] for k in range(NCH)]

        def ev_out(idx, b, m):
            dve(lambda e: e.scalar_tensor_tensor(out=xT[:, idx, :], in0=psb[b][:], scalar=geff[:, 1, idx, s:s + 1],
                                                 in1=xT[:, idx, :], op0=ALU.mult, op1=ALU.add),
                [pst[b], t_mods, t_x[idx]], [t_x[idx]])
        linear(W["w_out"][l], NCH, rhs_fn, rt, [(128 * i, 128) for i in range(NCH)], ev_out, [0, 1, 2, 7])
        store_x_tile(ti)
        P.barrier()

    def mixer(l):
        if STOP == -2:
            return
        P.barrier()
        if STOP == -1:
            return
        load_cache(l)
        if STOP == 1:
            return
        for ti in range(NT):
            mix_proj(l, ti)
        P.barrier()
        if STOP in (2, 21, 22, 23, 24):
            return
        s5_prep(l)
        P.barrier()
        if STOP == 3:
            return
        s5_scan(l)
        if STOP == 4:
            P.barrier()
            return
        s5_glu(l)
        P.barrier()
        if STOP == 5:
            return
        for ti in range(NT):
            mix_attn(l, ti)

    for l in range(L):
        compute_mods(l)
        for ti in range(NT):
            s = 0 if ti < NTL else 1
            load_x_tile(ti, first=(l == 0))
            norm_mod(0, s)
            ffn(l, 1, s)
            store_x_tile(ti)
        mixer(l)
        for ti in range(NT):
            s = 0 if ti < NTL else 1
            load_x_tile(ti, first=False)
            norm_mod(2, s)
            ffn(l, 2, s)
            if l == L - 1:
                final_out(ti)
            else:
                store_x_tile(ti)
    P.barrier()
    P.emit()
    ncd.__exit__(None, None, None)
    return nc


def _rope_tables(nlat, rot_dim):
    n_rows = nlat // 64
    row = np.repeat(np.arange(n_rows, dtype=np.float32), 64)
    col = np.tile(np.arange(64, dtype=np.float32), n_rows)
    n_freq = rot_dim // 4
    inv = (np.float32(10000.0) ** (-np.arange(n_freq, dtype=np.float32) / np.float32(n_freq))).astype(np.float32)
    ang = np.concatenate([row[:, None] * inv, col[:, None] * inv], axis=-1).astype(np.float32)
    c = np.cos(ang).astype(np.float32).T
    s = np.sin(ang).astype(np.float32).T
    return np.ascontiguousarray(np.stack([np.concatenate([c, c], 0), np.concatenate([s, s], 0)], 0))


def _perm(n):
    h = n // 2
    p = np.zeros((n, n), np.float32)
    for d in range(h):
        p[d + h, d] = -1.0
        p[d, d + h] = 1.0
    return p


def make_in_maps(inp, ncores, L, NLAT):
    consts = {
        "ident": np.eye(128, dtype=np.float32),
        "perm128": _perm(128),
        "perm64": _perm(64),
        "ropeg": _rope_tables(NLAT, 128),
        "ropem": _rope_tables(NLAT, 64),
    }
    shared = {}
    for name, shp in WEIGHT_SPECS:
        shared[name] = np.ascontiguousarray(np.asarray(inp[name], dtype=np.float32).reshape((L,) + shp))
    shared["norm_final"] = np.ascontiguousarray(np.asarray(inp["norm_final"], dtype=np.float32))
    shared.update(consts)
    maps = []
    xp = np.asarray(inp["x_prompt"])
    for i in range(ncores):
        m = dict(shared)
        m["x_lat"] = np.ascontiguousarray(np.asarray(inp["x_sample"])[i])
        m["x_ctx"] = np.ascontiguousarray(xp[2 * i:2 * i + 2].reshape(NCTX, D))
        m["cvec"] = np.ascontiguousarray(np.stack([np.asarray(inp["c"])[i], np.asarray(inp["c_ctx"])], 0))
        m["c_ckv"] = np.ascontiguousarray(np.asarray(inp["cache_mla_ckv"])[i])
        m["c_kr"] = np.ascontiguousarray(np.asarray(inp["cache_mla_krope"])[i])
        m["c_k"] = np.ascontiguousarray(np.asarray(inp["cache_gqa_k"])[i].reshape(L, PAST, 256))
        m["c_v"] = np.ascontiguousarray(np.asarray(inp["cache_gqa_v"])[i].reshape(L, PAST, 256))
        m["st_s5"] = np.ascontiguousarray(np.asarray(inp["state_s5"])[i].reshape(L, 2, 2048, 2))
        maps.append(m)
    return maps


def gather_outputs(results, ncores, L, NLAT):
    y_prompt = np.concatenate([r["y_ctx"].reshape(2, SEQ, D) for r in results], 0)
    y_sample = np.stack([r["y_lat"] for r in results], 0)
    ckv = np.concatenate([r["o_ckv"] for r in results], 0)
    kr = np.concatenate([r["o_kr"] for r in results], 0)
    k = np.concatenate([r["o_k"].reshape(2, L, SEQ, 2, 128) for r in results], 0)
    v = np.concatenate([r["o_v"].reshape(2, L, SEQ, 2, 128) for r in results], 0)
    s5 = np.concatenate([r["o_s5"].reshape(2, L, 2, 32, 64, 2) for r in results], 0)
    return tuple(np.ascontiguousarray(a.astype(np.float32)) for a in (y_prompt, y_sample, ckv, kr, k, v, s5))


_NC_CACHE = {}


def kernel(**inputs):
    L, NLAT, ncores = 4, 2048, 8
    key = (L, NLAT)
    if key not in _NC_CACHE:
        _NC_CACHE[key] = build(L, NLAT)
    nc = _NC_CACHE[key]
    maps = make_in_maps(inputs, ncores, L, NLAT)
    res = run_bass_kernel_spmd(nc, maps, core_ids=list(range(ncores)))
    return gather_outputs(res.results, ncores, L, NLAT)
```

```python
import math
import contextlib
import numpy as np
import concourse.bass as bass
import concourse.mybir as mybir
from concourse.bass_utils import run_bass_kernel_spmd

F32 = mybir.dt.float32
BF16 = mybir.dt.bfloat16
ALU = mybir.AluOpType
AF = mybir.ActivationFunctionType

ENGS = ("pe", "act", "dve", "pool", "sp")
SEM_ROT = 16000


class Tok:
    __slots__ = ("w", "r", "rd")

    def __init__(self):
        self.w = None
        self.r = {}
        self.rd = []


def toks(n):
    return [Tok() for _ in range(n)]


class DSem:
    __slots__ = ("idx", "count", "waited_max", "acked")

    def __init__(self, idx):
        self.idx = idx
        self.count = 0
        self.waited_max = 0
        self.acked = {}


class _Rec:
    def __init__(self):
        self.call = None

    def __getattr__(self, name):
        def f(*a, **k):
            self.call = (name, a, k)
            return None
        return f


def _freeze(fn):
    if fn is None:
        return None
    r = _Rec()
    fn(r)
    name, a, k = r.call
    import sys
    try:
        line = sys._getframe(3).f_lineno
    except ValueError:
        line = -1

    def g(e):
        return getattr(e, name)(*a, **k)
    g.line = line
    return g


class Prog:
    def __init__(self, nc, selfsync=True):
        self.nc = nc
        self.ops = {e: [] for e in ENGS}
        self.selfsync = selfsync
        self.dsems = []

    def new_dsem(self):
        d = DSem(len(self.dsems))
        self.dsems.append(d)
        return d

    def _collect(self, eng, reads, writes, auto_self=True, extra=()):
        deps = set(extra)
        for t in reads:
            if t.w is not None:
                deps.add(t.w)
        for t in writes:
            if t.w is not None:
                deps.add(t.w)
            for e, s in t.r.items():
                if e != eng or (self.selfsync and auto_self and eng not in ("pe", "sp")):
                    deps.add(("e", e, s))
            for d in t.rd:
                deps.add(d)
        waits = []
        for d in deps:
            if d[0] == "e":
                if d[1] == eng and d not in extra and (eng in ("pe", "sp") or not self.selfsync or not auto_self):
                    continue
                self.ops[d[1]][d[2]][2] = True
                waits.append(d)
            else:
                ds = self.dsems[d[1]]
                v = max(d[2], ds.count)
                ds.waited_max = max(ds.waited_max, v)
                waits.append(("d", d[1], v))
        return waits

    def op(self, eng, fn, reads=(), writes=(), auto_self=True, extra=()):
        waits = self._collect(eng, reads, writes, auto_self, tuple(e for e in extra if e is not None))
        seq = len(self.ops[eng])
        fn = _freeze(fn)
        self.ops[eng].append([fn, waits, False, None])
        ev = ("e", eng, seq)
        for t in reads:
            if t.r.get(eng, -1) < seq:
                t.r[eng] = seq
        for t in writes:
            t.w = ev
            t.r = {}
            t.rd = []
        return ev

    def dma(self, q, fn, dsem, reads=(), writes=()):
        waits = self._collect(q, reads, writes)
        if dsem.waited_max > dsem.acked.get(q, 0):
            waits.append(("d", dsem.idx, dsem.waited_max))
            dsem.acked[q] = dsem.waited_max
        fn = _freeze(fn)
        self.ops[q].append([fn, waits, False, dsem.idx])
        dsem.count += 16
        ev = ("d", dsem.idx, dsem.count)
        for t in reads:
            t.rd.append(ev)
        for t in writes:
            t.w = ev
            t.r = {}
            t.rd = []
        return ev

    def barrier(self):
        last = {}
        for e in ENGS:
            for i in range(len(self.ops[e]) - 1, -1, -1):
                if self.ops[e][i][0] is not None and self.ops[e][i][3] is None:
                    last[e] = i
                    break
        for e in ENGS:
            waits = []
            for f, s in last.items():
                if f != e:
                    self.ops[f][s][2] = True
                    waits.append(("e", f, s))
            for d in self.dsems:
                if d.count:
                    waits.append(("d", d.idx, d.count))
                    d.waited_max = max(d.waited_max, d.count)
            self.ops[e].append([None, waits, False, None])

    def emit(self):
        nc = self.nc
        with contextlib.ExitStack() as st:
            nsig = {e: sum(1 for o in self.ops[e] if o[2]) for e in ENGS}
            esems = {}
            for e in ENGS:
                n = nsig[e] // SEM_ROT + 1
                esems[e] = [st.enter_context(nc.semaphore("s_%s%d" % (e, i))) for i in range(n)]
            dsem_h = [st.enter_context(nc.semaphore("d%d" % i)) for i in range(len(self.dsems))]
            signum = {}
            for e in ENGS:
                c = 0
                arr = []
                for o in self.ops[e]:
                    if o[2]:
                        c += 1
                    arr.append(c)
                signum[e] = arr
            block = st.enter_context(nc.Block())

            def run(ename, eng):
                waited = {}
                sc = 0
                for (fn, waits, signal, dsi) in self.ops[ename]:
                    for w in waits:
                        if w[0] == "e":
                            n = signum[w[1]][w[2]]
                            k = (n - 1) // SEM_ROT
                            v = n - k * SEM_ROT
                            key = ("e", w[1], k)
                            sem = esems[w[1]][k]
                        else:
                            key = ("d", w[1])
                            v = w[2]
                            sem = dsem_h[w[1]]
                        if waited.get(key, 0) >= v:
                            continue
                        waited[key] = v
                        eng.wait_ge(sem, v)
                    if fn is None:
                        continue
                    try:
                        ins = fn(eng)
                    except BaseException as ex:
                        print("EMIT FAIL on", ename, "line", getattr(fn, "line", None), repr(ex)[:300])
                        raise
                    if dsi is not None:
                        ins.then_inc(dsem_h[dsi], 16)
                    elif signal:
                        sc += 1
                        ins.then_inc(esems[ename][(sc - 1) // SEM_ROT], 1)

            @block.tensor
            def _(eng):
                run("pe", eng)

            @block.scalar
            def _(eng):
                run("act", eng)

            @block.vector
            def _(eng):
                run("dve", eng)

            @block.gpsimd
            def _(eng):
                run("pool", eng)

            @block.sync
            def _(eng):
                run("sp", eng)


D = 2048
DFF = 4096
T = 512
NCH = 16
EPS = 1e-6
PAST = 512
SEQ = 256
NCTX = 512
D_IN = 2368
C_ZQ, C_ZKV, C_ZKR, C_U, C_GQ, C_GK, C_GV = 0, 512, 768, 832, 1344, 1856, 2112
MLA_SCALE = 1.0 / math.sqrt(192.0)
GQA_SCALE = 1.0 / math.sqrt(128.0)
SLOT = 4096
NSLOT = 4

WEIGHT_SPECS = [
    ("w_ada", (D, 9 * D)), ("b_ada", (9 * D,)), ("norm_ffn1", (D,)),
    ("ffn1_w_gate", (D, DFF)), ("ffn1_w_up", (D, DFF)), ("ffn1_w_down", (DFF, D)),
    ("norm_mix", (D,)), ("w_in", (D, D_IN)), ("mla_q_norm", (512,)), ("mla_w_uq", (512, 1536)),
    ("mla_kv_norm", (256,)), ("mla_w_ukv", (256, 2048)),
    ("s5_lambda_re", (2, 32, 64)), ("s5_lambda_im", (2, 32, 64)), ("s5_log_dt", (2, 32)),
    ("s5_b_re", (2, 32, 64, 16)), ("s5_b_im", (2, 32, 64, 16)),
    ("s5_c_re", (2, 32, 16, 64)), ("s5_c_im", (2, 32, 16, 64)),
    ("s5_d", (512,)), ("s5_w_glu", (512, 512)), ("s5_b_glu", (512,)),
    ("gqa_q_norm", (128,)), ("gqa_k_norm", (128,)), ("w_out", (D, D)), ("norm_ffn2", (D,)),
    ("ffn2_w_gate", (D, DFF)), ("ffn2_w_up", (D, DFF)), ("ffn2_w_down", (DFF, D)),
]


def build(L=4, NLAT=2048, debug=None, STOP=0):
    nc = bass.Bass("TRN2", target_bir_lowering=False)
    P = Prog(nc)
    NTL = NLAT // T
    NT = NTL + 1
    NTOK = NLAT + NCTX
    NKL = PAST + NLAT
    KTOT = NKL + NCTX
    SPAD = 1024 if NLAT > 1024 else NLAT // 2
    NLEV = int(math.log2(NLAT))

    def din(name, shape):
        return nc.dram_tensor(name, list(shape), F32, kind="ExternalInput").ap()

    def dout(name, shape):
        return nc.dram_tensor(name, list(shape), F32, kind="ExternalOutput").ap()

    x_lat = din("x_lat", (NLAT, D))
    x_ctx = din("x_ctx", (NCTX, D))
    cvec = din("cvec", (2, D))
    c_ckv = din("c_ckv", (L, PAST, 256))
    c_kr = din("c_kr", (L, PAST, 64))
    c_k = din("c_k", (L, PAST, 256))
    c_v = din("c_v", (L, PAST, 256))
    st_s5 = din("st_s5", (L, 2, 2048, 2))
    W = {}
    for name, shp in WEIGHT_SPECS:
        W[name] = din(name, (L,) + shp)
    norm_final = din("norm_final", (D,))
    ident_d = din("ident", (128, 128))
    perm128_d = din("perm128", (128, 128))
    perm64_d = din("perm64", (64, 64))
    ropeg_d = din("ropeg", (2, 128, NLAT))
    ropem_d = din("ropem", (2, 64, NLAT))

    y_lat = dout("y_lat", (NLAT, D))
    y_ctx = dout("y_ctx", (NCTX, D))
    o_ckv = dout("o_ckv", (2, L, SEQ, 256))
    o_kr = dout("o_kr", (2, L, SEQ, 64))
    o_k = dout("o_k", (2, L, SEQ, 256))
    o_v = dout("o_v", (2, L, SEQ, 256))
    o_s5 = dout("o_s5", (2, L, 2, 2048, 2))
    xs_d = nc.dram_tensor("xs_scratch", [NT, 128, NCH, T], F32).ap()
    dbg = {}
    if debug:
        for nm, shp in debug.items():
            dbg[nm] = dout("dbg_" + nm, shp)

    SB_BASE = 16512
    SB_END = 229376
    cur = [SB_BASE]

    def sb(name, shape, dt, at=None):
        nb = int(np.prod(shape[1:])) * (4 if dt == F32 else 2)
        nb = (nb + 63) // 64 * 64
        if at is None:
            off = cur[0]
            cur[0] += nb
            assert cur[0] <= SB_END, ("SBUF overflow", name, cur[0])
        else:
            off = at
        return nc.alloc_sbuf_tensor_at(name, list(shape), dt, offset=off)

    ident = sb("ident", [128, 128], F32)
    identb = sb("identb", [128, 128], BF16)
    perm128 = sb("perm128", [128, 128], F32)
    perm64 = sb("perm64", [64, 64], F32)
    ones = sb("ones", [128, 128], BF16)
    perm128b = sb("perm128b", [128, 128], BF16)
    perm64b = sb("perm64b", [64, 64], BF16)
    epsc = sb("epsc", [128, 8], F32)
    cT = sb("cT", [128, NCH, 2], F32)
    csT = sb("csT", [128, NCH, 2], BF16)
    badaT = sb("badaT", [128, L, 144], F32)
    gnorm = sb("gnorm", [128, L, 3, NCH], F32)
    gfin = sb("gfin", [128, NCH], F32)
    qnorm = sb("qnorm", [128, L, 4], F32)
    kvnorm = sb("kvnorm", [128, L, 2], F32)
    gqn = sb("gqn", [128, L, 2], F32)
    s5d = sb("s5d", [128, L, 4], F32)
    bglu = sb("bglu", [128, L, 4], F32)
    MB = []
    for i_ in range(2):
        MB.append([sb("mods%d" % i_, [128, 9, NCH, 2], F32),
                   sb("gmod%d" % i_, [128, 3, NCH, 2], F32),
                   sb("geff%d" % i_, [128, 3, NCH, 2], F32),
                   Tok()])
    mods, gmod, geff, t_mods = MB[0]
    ROPE0 = cur[0]
    rope_g = sb("rope_g", [128, 2, T], F32)
    rope_m = sb("rope_m", [64, 2, T], F32)
    ckvT = sb("ckvT", [128, 2, KTOT], BF16)
    krT = sb("krT", [64, KTOT], BF16)
    gkT = sb("gkT", [128, 2, KTOT], BF16)
    gvS = sb("gvS", [128, KTOT // 128, 256], BF16)
    uT = sb("uT", [128, 4, NTOK], BF16)
    ring = [sb("ring%d" % i, [128, SLOT], BF16) for i in range(NSLOT)]
    rstd = sb("rstd", [128, T], F32)
    tmpA = [sb("tmpA%d" % i, [128, T], F32) for i in range(3)]
    sqb = [sb("sqb%d" % i, [128, T], BF16) for i in range(2)]
    pTb = [sb("pTb%d" % i, [128, T], BF16) for i in range(4)]
    U0 = cur[0]
    xT = sb("xT", [128, NCH, T], F32)
    hT = sb("hT", [128, NCH, T], BF16)
    A0 = cur[0]
    aT = sb("aT", [128, 32, T], BF16)
    UEND = cur[0]
    H0_ = U0 + NCH * T * 4
    catT = sb("catT", [128, NCH, T], BF16, at=A0)
    zqn = sb("zqn", [128, 4, T], BF16, at=A0 + 16384)
    gqT = sb("gqT", [128, 4, T], BF16, at=A0 + 20480)
    knope = sb("knope", [128, NKL], BF16, at=H0_)
    vh = sb("vh", [128, NKL // 128, 128], BF16, at=H0_ + NKL * 2)
    qn_h = sb("qn_h", [128, T], BF16, at=H0_ + NKL * 4)
    qr_h = sb("qr_h", [64, T], BF16, at=H0_ + NKL * 4 + 1024)
    assert NKL * 4 + 2048 <= NCH * T * 2
    stage = sb("stage", [128, D], F32, at=A0)
    SBUFW = max(SPAD + NLAT, 1024)
    hbuf = [sb("hbuf%d" % i, [128, SBUFW], F32, at=U0 + i * SBUFW * 4) for i in range(4)]
    o = U0 + 4 * SBUFW * 4
    bbT = sb("bbT", [128, 2, 16, 2, 128], BF16, at=o); o += 2 * 16 * 2 * 128 * 2
    ccT = sb("ccT", [128, 2, 16, 2, 128], BF16, at=o); o += 2 * 16 * 2 * 128 * 2
    assert o <= UEND, ("s5 carve overflow", o - UEND)
    negpi = sb("negpi", [128, 8], F32)
    halfpi = sb("halfpi", [128, 8], F32)
    import os as _os
    S5ENG = _os.environ.get("S5ENG", "dve")
    o = ROPE0
    s5p = sb("s5p", [128, 2, 16, 8], F32, at=o); o += 1024
    s5pow = sb("s5pow", [128, 2, 16, 12, 3], F32, at=o); o += 4608
    s5h0 = sb("s5h0", [128, 2, 16, 2], F32, at=o); o += 256
    s5ah0 = sb("s5ah0", [128, 2, 16, 2], F32, at=o); o += 256
    s5fin = sb("s5fin", [128, 2, 2, 16, 2], F32, at=o); o += 512
    s5f = sb("s5f", [128, 2, 16, 3], F32, at=o); o += 384
    assert o <= ROPE0 + 8192
    s5frow = None
    print("SBUF used", cur[0] - SB_BASE, "of", SB_END - SB_BASE)

    psb = [nc.alloc_psum_tensor("psb%d" % i, [128, T], F32) for i in range(8)]
    pst = toks(8)

    t_const = Tok()
    t_par = Tok()
    t_x = toks(NCH)
    t_h = toks(NCH)
    t_a = toks(32)
    t_ring = toks(NSLOT)
    s_ring = [P.new_dsem() for _ in range(NSLOT)]
    t_xs = [toks(4) for _ in range(NT)]
    s_xld = [P.new_dsem() for _ in range(4)]
    s_xst = [P.new_dsem() for _ in range(4)]
    s_misc = P.new_dsem()
    s_out = P.new_dsem()
    t_out = Tok()
    t_rstd = Tok()
    t_tmpA = toks(3)
    t_sqb = toks(2)
    t_pT = toks(4)
    t_rope = Tok()
    s_rope = P.new_dsem()
    t_kv = Tok()
    t_u = toks(4 * NT)
    t_stage = Tok()
    s_stage = P.new_dsem()
    cnt = {"ring": 0, "tmp": 0, "sq": 0, "pT": 0}

    def act(fn, r, w):
        return P.op("act", fn, r, w)

    def dve(fn, r, w):
        return P.op("dve", fn, r, w)

    def pe(fn, r, w):
        return P.op("pe", fn, r, w)

    def wload(src_ap, nk, ncols, ndma=None):
        s = cnt["ring"] % NSLOT
        cnt["ring"] += 1
        dst = ring[s][:, 0:nk * ncols].rearrange("p (k c) -> p k c", c=ncols)
        dd = dst if ndma is None else dst[:, :, 0:ndma]
        P.dma("pool", lambda e: e.dma_start(out=dd, in_=src_ap), s_ring[s], writes=[t_ring[s]])
        return dst, t_ring[s]

    def wview(wap, k0, nk, c0, ncols):
        return wap.rearrange("(k p) n -> p k n", p=128)[:, k0:k0 + nk, c0:c0 + ncols]

    def next_tmp():
        i = cnt["tmp"] % 3
        cnt["tmp"] += 1
        return tmpA[i], t_tmpA[i]

    def next_sq():
        i = cnt["sq"] % 2
        cnt["sq"] += 1
        return sqb[i], t_sqb[i]

    def next_pT():
        i = cnt["pT"] % 4
        cnt["pT"] += 1
        return pTb[i], t_pT[i]

    ncd = nc.allow_non_contiguous_dma(reason="small strided parameter loads")
    ncd.__enter__()

    def sdma(out, in_, tok):
        P.dma("sp", lambda e: e.dma_start(out=out, in_=in_), s_misc, writes=[tok])

    sdma(ident[:], ident_d, t_const)
    sdma(perm128[:], perm128_d, t_const)
    sdma(perm64[:], perm64_d, t_const)
    for r in range(2):
        sdma(cT[:, :, r], cvec[r].rearrange("(k p) -> p k", p=128), t_par)
    for l in range(L):
        sdma(badaT[:, l, :], W["b_ada"][l].rearrange("(c p) -> p c", p=128), t_par)
        for j, nm in enumerate(("norm_ffn1", "norm_mix", "norm_ffn2")):
            sdma(gnorm[:, l, j, :], W[nm][l].rearrange("(k p) -> p k", p=128), t_par)
        sdma(qnorm[:, l, :], W["mla_q_norm"][l].rearrange("(k p) -> p k", p=128), t_par)
        sdma(kvnorm[:, l, :], W["mla_kv_norm"][l].rearrange("(k p) -> p k", p=128), t_par)
        sdma(gqn[:, l, 0:1], W["gqa_q_norm"][l].rearrange("(p o) -> p o", o=1), t_par)
        sdma(gqn[:, l, 1:2], W["gqa_k_norm"][l].rearrange("(p o) -> p o", o=1), t_par)
        sdma(s5d[:, l, :], W["s5_d"][l].rearrange("(k p) -> p k", p=128), t_par)
        sdma(bglu[:, l, :], W["s5_b_glu"][l].rearrange("(k p) -> p k", p=128), t_par)
    sdma(gfin[:], norm_final.rearrange("(k p) -> p k", p=128), t_par)
    P.op("pool", lambda e: e.memset(ones[:], 1.0), [], [t_const])
    for i_ in range(NSLOT):
        P.op("pool", lambda e, i_=i_: e.memset(ring[i_][:], 0.0), [], [t_ring[i_]])
    P.op("pool", lambda e: e.memset(epsc[:], EPS), [], [t_const])
    P.op("pool", lambda e: e.memset(negpi[:], -math.pi), [], [t_const])
    P.op("pool", lambda e: e.memset(halfpi[:], 0.5 * math.pi), [], [t_const])
    act(lambda e: e.activation(out=identb[:], in_=ident[:], func=AF.Copy), [t_const], [t_const])
    act(lambda e: e.activation(out=perm128b[:], in_=perm128[:], func=AF.Copy), [t_const], [t_const])
    act(lambda e: e.activation(out=perm64b[:], in_=perm64[:], func=AF.Copy), [t_const], [t_const])
    tm, tt = next_tmp()
    act(lambda e: e.activation(out=tm[:, 0:32], in_=cT[:].rearrange("p k r -> p (k r)"), func=AF.Sigmoid), [t_par], [tt])
    dve(lambda e: e.tensor_tensor(out=csT[:].rearrange("p k r -> p (k r)"), in0=tm[:, 0:32],
                                  in1=cT[:].rearrange("p k r -> p (k r)"), op=ALU.mult), [tt, t_par], [t_par])

    def load_x_tile(ti, first):
        if not first:
            for g in range(4):
                P.dma("sp", lambda e: e.dma_start(out=xT[:, 4 * g:4 * g + 4, :], in_=xs_d[ti][:, 4 * g:4 * g + 4, :]),
                      s_xld[g], reads=[t_xs[ti][g]], writes=t_x[4 * g:4 * g + 4])
            return
        src = x_lat[ti * T:(ti + 1) * T, :] if ti < NTL else x_ctx
        for tc4 in range(4):
            P.dma("sp", lambda e, tc4=tc4: e.dma_start(out=stage[:], in_=src[tc4 * 128:(tc4 + 1) * 128, :]),
                  s_stage, writes=[t_stage] + t_a)
            for k0 in range(0, NCH, 4):
                b = (k0 // 4) % 2
                for j in range(4):
                    pe(lambda e, b=b, j=j, k=k0 + j: e.transpose(
                        out=psb[b][:, j * 128:(j + 1) * 128], in_=stage[:, k * 128:(k + 1) * 128], identity=ident[:]),
                       [t_stage, t_const], [pst[b]])
                dve(lambda e, b=b, k0=k0, tc4=tc4: e.tensor_copy(
                    out=xT[:, k0:k0 + 4, tc4 * 128:(tc4 + 1) * 128],
                    in_=psb[b][:].rearrange("p (j t) -> p j t", t=128)), [pst[b]], t_x[k0:k0 + 4])

    def store_x_group(ti, g):
        P.dma("sp", lambda e: e.dma_start(out=xs_d[ti][:, 4 * g:4 * g + 4, :], in_=xT[:, 4 * g:4 * g + 4, :]),
              s_xst[g], reads=t_x[4 * g:4 * g + 4], writes=[t_xs[ti][g]])

    def store_x_tile(ti):
        for g in range(4):
            store_x_group(ti, g)

    def sumsq_rstd(chunks, nfeat, bank, n=T):
        nchk = len(chunks)
        for i, (ap, tk, pn) in enumerate(chunks):
            sq, tq = next_sq()
            act(lambda e, sq=sq, ap=ap, pn=pn: e.activation(out=sq[0:pn, 0:n], in_=ap, func=AF.Square), [tk], [tq])
            pe(lambda e, sq=sq, pn=pn, i=i: e.matmul(psb[bank][:, 0:n], lhsT=ones[0:pn, :], rhs=sq[0:pn, 0:n],
                                                     start=(i == 0), stop=(i == nchk - 1)), [tq, t_const], [pst[bank]])
        act(lambda e: e.activation(out=rstd[:, 0:n], in_=psb[bank][:, 0:n], func=AF.Sqrt, scale=1.0 / nfeat,
                                   bias=epsc[:, 0:1]), [pst[bank], t_const], [t_rstd])
        dve(lambda e: e.reciprocal(out=rstd[:, 0:n], in_=rstd[:, 0:n]), [t_rstd], [t_rstd])

    def norm_mod(j, s):
        sumsq_rstd([(xT[:, k, :], t_x[k], 128) for k in range(NCH)], D, 7)
        for k in range(NCH):
            tm, tt = next_tmp()
            dve(lambda e, tm=tm, k=k: e.tensor_tensor(out=tm[:], in0=xT[:, k, :], in1=rstd[:], op=ALU.mult),
                [t_x[k], t_rstd], [tt])
            act(lambda e, tm=tm, k=k: e.activation(out=hT[:, k, :], in_=tm[:], func=AF.Identity,
                                                   scale=gmod[:, j, k, s:s + 1], bias=mods[:, 3 * j, k, s:s + 1]),
                [tt, t_mods], [t_h[k]])

    def mods_slots(l, a, b, bank):
        for sl in range(a, b):
            wv, wt = wload(wview(W["w_ada"][l], 0, NCH, sl * 256, 256), NCH, 256)
            for c in range(2):
                ch = 2 * sl + c
                for k in range(NCH):
                    pe(lambda e: e.matmul(psb[bank][:, 2 * ch:2 * ch + 2], lhsT=wv[:, k, c * 128:(c + 1) * 128],
                                          rhs=csT[:, k, :], start=(k == 0), stop=(k == NCH - 1)), [wt, t_par], [pst[bank]])

    def mods_finish(l, bank):
        m_, gm_, ge_, tm_ = MB[l % 2]
        dve(lambda e: e.tensor_tensor(out=m_[:].rearrange("p v k s -> p (v k) s"),
                                      in0=psb[bank][:, 0:288].rearrange("p (c s) -> p c s", s=2),
                                      in1=badaT[:, l, :].unsqueeze(2).to_broadcast([128, 144, 2]), op=ALU.add),
            [pst[bank], t_par], [tm_])
        for j in range(3):
            dve(lambda e: e.scalar_tensor_tensor(
                out=gm_[:, j, :, :], in0=m_[:, 3 * j + 1, :, :], scalar=1.0,
                in1=gnorm[:, l, j, :].unsqueeze(2).to_broadcast([128, NCH, 2]), op0=ALU.add, op1=ALU.mult),
                [tm_, t_par], [tm_])
            dve(lambda e: e.tensor_scalar(out=ge_[:, j, :, :], in0=m_[:, 3 * j + 2, :, :],
                                          scalar1=(1.0 if j == 1 else 0.5), scalar2=None, op0=ALU.mult), [tm_], [tm_])

    def compute_mods(l):
        mods_slots(l, 0, 72, 6)
        mods_finish(l, 6)

    def ffn(l, which, s, store_ti=None):
        j = 0 if which == 1 else 2
        wg, wu, wd = (W["ffn%d_w_gate" % which][l], W["ffn%d_w_up" % which][l], W["ffn%d_w_down" % which][l])
        for sl in range(16):
            gv_, gt = wload(wview(wg, 0, NCH, sl * 256, 256), NCH, 256)
            uv_, ut = wload(wview(wu, 0, NCH, sl * 256, 256), NCH, 256)
            for c in range(2):
                f = 2 * sl + c
                bg, bu = f % 2, 2 + f % 2
                for k in range(NCH):
                    pe(lambda e, gv_=gv_, c=c, k=k, bg=bg: e.matmul(
                        psb[bg][:], lhsT=gv_[:, k, c * 128:(c + 1) * 128], rhs=hT[:, k, :],
                        start=(k == 0), stop=(k == NCH - 1)), [gt, t_h[k]], [pst[bg]])
                for k in range(NCH):
                    pe(lambda e, uv_=uv_, c=c, k=k, bu=bu: e.matmul(
                        psb[bu][:], lhsT=uv_[:, k, c * 128:(c + 1) * 128], rhs=hT[:, k, :],
                        start=(k == 0), stop=(k == NCH - 1)), [ut, t_h[k]], [pst[bu]])
                tm, tt = next_tmp()
                act(lambda e, tm=tm, bg=bg: e.activation(out=tm[:], in_=psb[bg][:], func=AF.Silu), [pst[bg]], [tt])
                dve(lambda e, tm=tm, bu=bu, f=f: e.tensor_tensor(out=aT[:, f, :], in0=tm[:], in1=psb[bu][:], op=ALU.mult),
                    [tt, pst[bu]], [t_a[f]])
        for sl in range(8):
            b0 = 4 + 2 * (sl % 2)
            for part in range(2):
                dv_, dt_ = wload(wview(wd, part * 16, 16, sl * 256, 256), 16, 256)
                for c in range(2):
                    for k in range(16):
                        kk = part * 16 + k
                        pe(lambda e, dv_=dv_, c=c, k=k, kk=kk, b=b0 + c, part=part: e.matmul(
                            psb[b][:], lhsT=dv_[:, k, c * 128:(c + 1) * 128], rhs=aT[:, kk, :],
                            start=(kk == 0), stop=(kk == 31)), [dt_, t_a[kk]], [pst[b0 + c]])
            for c in range(2):
                d_ = 2 * sl + c
                dve(lambda e, d_=d_, b=b0 + c: e.scalar_tensor_tensor(
                    out=xT[:, d_, :], in0=psb[b][:], scalar=geff[:, j, d_, s:s + 1], in1=xT[:, d_, :],
                    op0=ALU.mult, op1=ALU.add), [pst[b0 + c], t_mods, t_x[d_]], [t_x[d_]])
            if store_ti is not None and sl % 2 == 1:
                store_x_group(store_ti, sl // 2)

    def final_out(ti):
        sumsq_rstd([(xT[:, k, :], t_x[k], 128) for k in range(NCH)], D, 7)
        for k in range(NCH):
            dve(lambda e, k=k: e.scalar_tensor_tensor(out=xT[:, k, :], in0=xT[:, k, :], scalar=gfin[:, k:k + 1],
                                                      in1=rstd[:], op0=ALU.mult, op1=ALU.mult),
                [t_x[k], t_rstd, t_par], [t_x[k]])
        dst = y_lat[ti * T:(ti + 1) * T, :] if ti < NTL else y_ctx
        for tc4 in range(4):
            for k0 in range(0, NCH, 4):
                b = (k0 // 4) % 2
                for jj in range(4):
                    pe(lambda e, b=b, jj=jj, k=k0 + jj, tc4=tc4: e.transpose(
                        out=psb[b][:, jj * 128:(jj + 1) * 128], in_=xT[:, k, tc4 * 128:(tc4 + 1) * 128], identity=ident[:]),
                       [t_x[k0 + jj], t_const], [pst[b]])
                dve(lambda e, b=b, k0=k0: e.tensor_copy(out=stage[:, k0 * 128:(k0 + 4) * 128], in_=psb[b][:]),
                    [pst[b]], [t_stage])
            P.dma("sp", lambda e, tc4=tc4: e.dma_start(out=dst[tc4 * 128:(tc4 + 1) * 128, :], in_=stage[:]),
                  s_out, reads=[t_stage] + t_a, writes=[t_out])

    H0 = U0 + NCH * T * 4
    mt = [sb("mt%d" % i, [128, T], F32, at=A0 + 24576 + i * 2048) for i in range(4)]
    t_mt = toks(4)
    iost = sb("iost", [128, 4, 832], F32, at=A0)
    t_io = Tok()
    s_io = P.new_dsem()
    s_oo = P.new_dsem()
    t_oo = Tok()
    t_cat = toks(NCH)
    t_kn = Tok()
    t_vh = Tok()
    t_zqn = Tok()
    t_gq = toks(4)
    t_qh = Tok()
    t_hb = toks(4)
    t_s5w = Tok()
    t_s5p = Tok()
    s_s5 = P.new_dsem()
    t_s5st = Tok()
    S5ST = [sb("s5st%d" % i, [128, 16, 128], F32, at=U0 + i * 8192) for i in range(2)]
    t_st = toks(2)
    bankrot = [0]

    def nbank(lst):
        b = lst[bankrot[0] % len(lst)]
        bankrot[0] += 1
        return b

    def copy_alt(i, out, in_, r, w):
        if i % 2 == 0:
            act(lambda e: e.activation(out=out, in_=in_, func=AF.Copy), r, w)
        else:
            dve(lambda e: e.tensor_copy(out=out, in_=in_), r, w)

    def linear(wap, nk, rhs_fn, rhs_toks, chunks, evac, banks, n=T):
        i = 0
        while i < len(chunks):
            c_start = chunks[i][0]
            j = i
            while j < len(chunks) and chunks[j][0] >= c_start and (chunks[j][0] + max(chunks[j][1], 128) - c_start) * nk <= SLOT:
                j += 1
            ncols = max(c[0] + max(c[1], 128) for c in chunks[i:j]) - c_start
            wv, wt = wload(wview(wap, 0, nk, c_start, ncols), nk, ncols)
            for idx in range(i, j):
                c0, m = chunks[idx]
                b = nbank(banks)
                for k in range(nk):
                    pe(lambda e, wv=wv, k=k, b=b, o=c0 - c_start: e.matmul(
                        psb[b][:, 0:n], lhsT=wv[:, k, o:o + 128], rhs=rhs_fn(k), start=(k == 0), stop=(k == nk - 1)),
                       [wt, rhs_toks[k]], [pst[b]])
                evac(idx, b, m)
            i = j

    def rope(xap, xtok, pn, perm, tab, out_ap, out_toks):
        b = nbank([4, 5])
        permb = perm128b if pn == 128 else perm64b
        xb, txb = next_sq()
        act(lambda e: e.activation(out=xb[0:pn, :], in_=xap, func=AF.Copy), [xtok], [txb])
        pe(lambda e: e.matmul(psb[b][0:pn, :], lhsT=permb[0:pn, 0:pn], rhs=xb[0:pn, :], start=True, stop=True),
           [txb, t_const], [pst[b]])
        tm, tt = next_tmp()
        dve(lambda e: e.tensor_tensor(out=tm[0:pn, :], in0=xap, in1=tab[0:pn, 0, :], op=ALU.mult), [xtok, t_rope], [tt])
        dve(lambda e: e.tensor_tensor(out=xap, in0=psb[b][0:pn, :], in1=tab[0:pn, 1, :], op=ALU.mult),
            [pst[b], t_rope], [xtok])
        dve(lambda e: e.tensor_tensor(out=out_ap, in0=tm[0:pn, :], in1=xap, op=ALU.add), [tt, xtok], out_toks)

    def load_rope(ti):
        P.dma("sp", lambda e: e.dma_start(out=rope_g[:], in_=ropeg_d[:, :, ti * T:(ti + 1) * T].rearrange("a p t -> p a t")),
              s_rope, writes=[t_rope])
        P.dma("sp", lambda e: e.dma_start(out=rope_m[:], in_=ropem_d[:, :, ti * T:(ti + 1) * T].rearrange("a p t -> p a t")),
              s_rope, writes=[t_rope])

    def load_cache(l):
        for src, c0, w_ in ((c_ckv, 0, 256), (c_kr, 256, 64), (c_k, 320, 256), (c_v, 576, 256)):
            P.dma("sp", lambda e, src=src, c0=c0, w_=w_: e.dma_start(
                out=iost[:, :, c0:c0 + w_], in_=src[l].rearrange("(tc p) f -> p tc f", p=128)), s_io, writes=[t_io])
        n_ = 0
        for (c0, dst) in ((0, ckvT), (320, gkT)):
            for c in range(2):
                b = nbank([0, 1, 2, 3])
                for tc in range(4):
                    pe(lambda e, b=b, tc=tc, o=c0 + c * 128: e.transpose(
                        out=psb[b][:, tc * 128:(tc + 1) * 128], in_=iost[:, tc, o:o + 128], identity=ident[:]),
                       [t_io, t_const], [pst[b]])
                copy_alt(n_, dst[:, c, 0:PAST], psb[b][:], [pst[b]], [t_kv])
                n_ += 1
        b = nbank([0, 1, 2, 3])
        for tc in range(4):
            pe(lambda e, b=b, tc=tc: e.transpose(out=psb[b][:, tc * 128:(tc + 1) * 128], in_=iost[:, tc, 256:384],
                                                 identity=ident[:]), [t_io, t_const], [pst[b]])
        copy_alt(0, krT[0:64, 0:PAST], psb[b][0:64, :], [pst[b]], [t_kv])
        copy_alt(1, gvS[:, 0:4, :], iost[:, :, 576:832], [t_io], [t_kv])

    def mix_proj(l, ti):
        lat = ti < NTL
        s = 0 if lat else 1
        kcol = PAST + ti * T if lat else NKL
        tcol = ti * T if lat else NLAT
        load_x_tile(ti, False)
        norm_mod(1, s)
        if lat:
            load_rope(ti)
        chunks = [(C_ZKV, 128), (C_ZKV + 128, 128), (C_ZKR, 64), (C_GK, 128), (C_GK + 128, 128)] + \
                 [(C_U + 128 * i, 128) for i in range(4)]
        mtmap = {0: 0, 1: 1, 2: 2, 3: 3, 4: 0}

        def out_T(src_ap_fn, src_tok, pn, col0, width_chunks):
            for tc in range(4):
                b = nbank([0, 1, 2, 3])
                for c in range(width_chunks):
                    pe(lambda e, b=b, c=c, tc=tc: e.transpose(
                        out=psb[b][:, c * 128:(c + 1) * 128], in_=src_ap_fn(c)[:, tc * 128:(tc + 1) * 128],
                        identity=ident[:]), [src_tok(c), t_const], [pst[b]])
                if pn == 128:
                    w_ = width_chunks * 128
                    dve(lambda e, b=b, tc=tc, w_=w_: e.tensor_copy(out=iost[:, tc, col0:col0 + w_], in_=psb[b][:, 0:w_]),
                        [pst[b]], [t_io])
                else:
                    dve(lambda e, b=b, tc=tc: e.tensor_copy(out=iost[:, tc, col0:col0 + pn], in_=psb[b][:, 0:pn]),
                        [pst[b]], [t_io])

        def finish_gk(c, mi):
            sumsq_rstd([(mt[mi][:], t_mt[mi], 128)], 128, 7)
            dve(lambda e: e.scalar_tensor_tensor(out=mt[mi][:], in0=mt[mi][:], scalar=gqn[:, l, 1:2], in1=rstd[:],
                                                 op0=ALU.mult, op1=ALU.mult), [t_mt[mi], t_rstd, t_par], [t_mt[mi]])
            if lat:
                rope(mt[mi][:], t_mt[mi], 128, perm128, rope_g, gkT[:, c, kcol:kcol + T], [t_kv])
            else:
                act(lambda e: e.activation(out=gkT[:, c, kcol:kcol + T], in_=mt[mi][:], func=AF.Copy), [t_mt[mi]], [t_kv])
                out_T(lambda cc_: mt[mi], lambda cc_: t_mt[mi], 128, 320 + c * 128, 1)

        def evac(idx, b, m):
            if idx in mtmap:
                mi = mtmap[idx]
                act(lambda e: e.activation(out=mt[mi][0:m, :], in_=psb[b][0:m, :], func=AF.Copy), [pst[b]], [t_mt[mi]])
                if idx == 1:
                    sumsq_rstd([(mt[0][:], t_mt[0], 128), (mt[1][:], t_mt[1], 128)], 256, 7)
                    for c in range(2):
                        dve(lambda e, c=c: e.scalar_tensor_tensor(
                            out=mt[c][:], in0=mt[c][:], scalar=kvnorm[:, l, c:c + 1], in1=rstd[:],
                            op0=ALU.mult, op1=ALU.mult), [t_mt[c], t_rstd, t_par], [t_mt[c]])
                        act(lambda e, c=c: e.activation(out=ckvT[:, c, kcol:kcol + T], in_=mt[c][:], func=AF.Copy),
                            [t_mt[c]], [t_kv])
                    if not lat:
                        out_T(lambda c: mt[c], lambda c: t_mt[c], 128, 0, 2)
                elif idx == 2:
                    if lat:
                        rope(mt[2][0:64, :], t_mt[2], 64, perm64, rope_m, krT[0:64, kcol:kcol + T], [t_kv])
                    else:
                        act(lambda e: e.activation(out=krT[0:64, kcol:kcol + T], in_=mt[2][0:64, :], func=AF.Copy),
                            [t_mt[2]], [t_kv])
                        out_T(lambda c: mt[2], lambda c: t_mt[2], 64, 256, 1)
                elif idx == 3:
                    finish_gk(0, 3)
                elif idx == 4:
                    finish_gk(1, 0)
            else:
                cc_ = idx - 5
                copy_alt(idx, uT[:, cc_, tcol:tcol + T], psb[b][:], [pst[b]], [t_u[cc_ * NT + ti]])

        if STOP == 21:
            return
        linear(W["w_in"][l], NCH, lambda k: hT[:, k, :], t_h, chunks if STOP != 22 else chunks[5:], evac, [0, 1, 2, 3])
        if STOP in (22, 23):
            return
        wv, wt = wload(wview(W["w_in"][l], 0, NCH, C_GV, 256), NCH, 256)
        for tc in range(4):
            b = nbank([0, 1, 2, 3])
            for k in range(NCH):
                pe(lambda e, b=b, k=k, tc=tc: e.matmul(psb[b][:, 0:256], lhsT=hT[:, k, tc * 128:(tc + 1) * 128],
                                                       rhs=wv[:, k, :], start=(k == 0), stop=(k == NCH - 1)),
                   [wt, t_h[k]], [pst[b]])
            if lat:
                act(lambda e, b=b, tc=tc: e.activation(out=gvS[:, kcol // 128 + tc, :], in_=psb[b][:, 0:256], func=AF.Copy),
                    [pst[b]], [t_kv])
            else:
                dve(lambda e, b=b, tc=tc: e.tensor_copy(out=iost[:, tc, 576:832], in_=psb[b][:, 0:256]), [pst[b]], [t_io])
                act(lambda e, tc=tc: e.activation(out=gvS[:, kcol // 128 + tc, :], in_=iost[:, tc, 576:832], func=AF.Copy),
                    [t_io], [t_kv])
        if not lat and STOP != 24:
            for (dst, c0, w_) in ((o_ckv, 0, 256), (o_kr, 256, 64), (o_k, 320, 256), (o_v, 576, 256)):
                for sq_ in range(2):
                    P.dma("sp", lambda e, dst=dst, c0=c0, w_=w_, sq_=sq_: e.dma_start(
                        out=dst[sq_, l].rearrange("(tc p) f -> p tc f", p=128), in_=iost[:, 2 * sq_:2 * sq_ + 2, c0:c0 + w_]),
                        s_oo, reads=[t_io], writes=[t_oo])

    def s5_prep(l):
        sp_ = s5p
        for d_ in range(2):
            P.dma("sp", lambda e, d_=d_: e.dma_start(out=sp_[:, d_, :, 0], in_=W["s5_lambda_re"][l, d_].rearrange(
                "g p -> (g p)").rearrange("(s q) -> q s", q=128)), s_s5, writes=[t_s5p])
            P.dma("sp", lambda e, d_=d_: e.dma_start(out=sp_[:, d_, :, 1], in_=W["s5_lambda_im"][l, d_].rearrange(
                "g p -> (g p)").rearrange("(s q) -> q s", q=128)), s_s5, writes=[t_s5p])
            for j in range(2):
                P.dma("sp", lambda e, d_=d_, j=j: e.dma_start(
                    out=sp_[64 * j:64 * j + 64, d_, :, 2],
                    in_=W["s5_log_dt"][l, d_].rearrange("(s j) -> j s", j=2)[j:j + 1, :].to_broadcast([64, 16])),
                    s_s5, writes=[t_s5p])
            P.dma("sp", lambda e, d_=d_: e.dma_start(out=s5h0[:, d_, :, :], in_=st_s5[l, d_].rearrange(
                "(s q) r -> q s r", q=128)), s_s5, writes=[t_s5p])
        A = lambda i: sp_[:, :, :, i]
        PI = math.pi
        r, w = [t_s5p, t_const], [t_s5p]
        act(lambda e: e.activation(out=A(3), in_=A(2), func=AF.Exp), r, w)
        dve(lambda e: e.tensor_tensor(out=A(4), in0=A(0), in1=A(3), op=ALU.mult), r, w)
        act(lambda e: e.activation(out=A(4), in_=A(4), func=AF.Exp, scale=1.0 / 32), r, w)
        dve(lambda e: e.tensor_tensor(out=A(7), in0=A(1), in1=A(3), op=ALU.mult), r, w)
        act(lambda e: e.activation(out=A(6), in_=A(7), func=AF.Sin, scale=1.0 / 32), r, w)
        act(lambda e: e.activation(out=A(5), in_=A(7), func=AF.Sin, scale=1.0 / 32, bias=halfpi[:, 0:1]), r, w)
        dve(lambda e: e.tensor_tensor(out=A(5), in0=A(5), in1=A(4), op=ALU.mult), r, w)
        dve(lambda e: e.tensor_tensor(out=A(6), in0=A(6), in1=A(4), op=ALU.mult), r, w)
        for _ in range(5):
            dve(lambda e: e.tensor_tensor(out=A(4), in0=A(5), in1=A(6), op=ALU.mult), r, w)
            dve(lambda e: e.tensor_tensor(out=A(5), in0=A(5), in1=A(5), op=ALU.mult), r, w)
            dve(lambda e: e.tensor_tensor(out=A(6), in0=A(6), in1=A(6), op=ALU.mult), r, w)
            dve(lambda e: e.tensor_tensor(out=A(5), in0=A(5), in1=A(6), op=ALU.subtract), r, w)
            dve(lambda e: e.tensor_tensor(out=A(6), in0=A(4), in1=A(4), op=ALU.add), r, w)
        F_ = lambda i: s5f[:, :, :, i]
        dve(lambda e: e.tensor_tensor(out=A(3), in0=A(0), in1=A(0), op=ALU.mult), r, w)
        dve(lambda e: e.tensor_tensor(out=A(4), in0=A(1), in1=A(1), op=ALU.mult), r, w)
        dve(lambda e: e.tensor_tensor(out=A(3), in0=A(3), in1=A(4), op=ALU.add), r, w)
        dve(lambda e: e.reciprocal(out=A(3), in_=A(3)), r, w)
        dve(lambda e: e.tensor_scalar(out=A(7), in0=A(5), scalar1=-1.0, scalar2=None, op0=ALU.add), r, w)
        dve(lambda e: e.tensor_tensor(out=A(4), in0=A(7), in1=A(0), op=ALU.mult), r, w)
        dve(lambda e: e.tensor_tensor(out=F_(0), in0=A(6), in1=A(1), op=ALU.mult), r, w)
        dve(lambda e: e.tensor_tensor(out=F_(0), in0=F_(0), in1=A(4), op=ALU.add), r, w)
        dve(lambda e: e.tensor_tensor(out=F_(0), in0=F_(0), in1=A(3), op=ALU.mult), r, w)
        dve(lambda e: e.tensor_tensor(out=A(4), in0=A(6), in1=A(0), op=ALU.mult), r, w)
        dve(lambda e: e.tensor_tensor(out=F_(1), in0=A(7), in1=A(1), op=ALU.mult), r, w)
        dve(lambda e: e.tensor_tensor(out=F_(1), in0=A(4), in1=F_(1), op=ALU.subtract), r, w)
        dve(lambda e: e.tensor_tensor(out=F_(1), in0=F_(1), in1=A(3), op=ALU.mult), r, w)
        dve(lambda e: e.tensor_scalar(out=F_(2), in0=F_(1), scalar1=-1.0, scalar2=None, op0=ALU.mult), r, w)
        PW = lambda k, i: s5pow[:, :, :, k, i]
        dve(lambda e: e.tensor_copy(out=PW(0, 0), in_=A(5)), r, w)
        dve(lambda e: e.tensor_copy(out=PW(0, 1), in_=A(6)), r, w)
        for k in range(NLEV):
            dve(lambda e, k=k: e.tensor_scalar(out=PW(k, 2), in0=PW(k, 1), scalar1=-1.0, scalar2=None, op0=ALU.mult), r, w)
            if k + 1 < NLEV:
                dve(lambda e, k=k: e.tensor_tensor(out=A(3), in0=PW(k, 0), in1=PW(k, 0), op=ALU.mult), r, w)
                dve(lambda e, k=k: e.tensor_tensor(out=A(4), in0=PW(k, 1), in1=PW(k, 1), op=ALU.mult), r, w)
                dve(lambda e, k=k: e.tensor_tensor(out=PW(k + 1, 0), in0=A(3), in1=A(4), op=ALU.subtract), r, w)
                dve(lambda e, k=k: e.tensor_tensor(out=A(3), in0=PW(k, 0), in1=PW(k, 1), op=ALU.mult), r, w)
                dve(lambda e, k=k: e.tensor_scalar(out=PW(k + 1, 1), in0=A(3), scalar1=2.0, scalar2=None, op0=ALU.mult), r, w)
        H = lambda i: s5h0[:, :, :, i]
        AH = lambda i: s5ah0[:, :, :, i]
        dve(lambda e: e.tensor_tensor(out=A(3), in0=A(5), in1=H(0), op=ALU.mult), r, w)
        dve(lambda e: e.tensor_tensor(out=A(4), in0=A(6), in1=H(1), op=ALU.mult), r, w)
        dve(lambda e: e.tensor_tensor(out=AH(0), in0=A(3), in1=A(4), op=ALU.subtract), r, w)
        dve(lambda e: e.tensor_tensor(out=A(3), in0=A(5), in1=H(1), op=ALU.mult), r, w)
        dve(lambda e: e.tensor_tensor(out=A(4), in0=A(6), in1=H(0), op=ALU.mult), r, w)
        dve(lambda e: e.tensor_tensor(out=AH(1), in0=A(3), in1=A(4), op=ALU.add), r, w)
        rnd = 0
        for d_ in range(2):
            for kind in range(4):
                st_, tst = S5ST[rnd % 2], t_st[rnd % 2]
                rnd += 1
                P.op("pool", lambda e, st_=st_: e.memset(st_[:], 0.0), [], [tst])
                for m in range(4):
                    for gl in range(2):
                        if kind < 2:
                            src = W["s5_b_re" if kind == 0 else "s5_b_im"][l, d_]
                            sv = src.rearrange("(i r) p c -> r p i c", r=8)[2 * m + gl]
                            dstv = st_[64 * gl:64 * gl + 64, :, 32 * m + 16 * gl:32 * m + 16 * gl + 16].rearrange(
                                "q (i r) c -> q r i c", r=4)[:, m]
                        else:
                            src = W["s5_c_re" if kind == 2 else "s5_c_im"][l, d_]
                            sv = src.rearrange("(i r) c p -> r c i p", r=8)[2 * m + gl]
                            dstv = st_[32 * m + 16 * gl:32 * m + 16 * gl + 16, :, 64 * gl:64 * gl + 64].rearrange(
                                "q (i r) c -> q r i c", r=4)[:, m]
                        P.dma("sp", lambda e, dstv=dstv, sv=sv: e.dma_start(out=dstv, in_=sv), s_s5, writes=[tst])
                dstT = bbT if kind < 2 else ccT
                ri = kind % 2
                for s0 in range(0, 16, 4):
                    b = nbank([0, 1, 2, 3])
                    for j in range(4):
                        pe(lambda e, b=b, j=j, sc=s0 + j, st_=st_: e.transpose(
                            out=psb[b][:, j * 128:(j + 1) * 128], in_=st_[:, sc, :], identity=ident[:]),
                           [tst, t_const], [pst[b]])
                    sgn = -1.0 if kind == 3 else 1.0
                    act(lambda e, b=b, s0=s0, d_=d_, ri=ri, dstT=dstT, sgn=sgn: e.activation(
                        out=dstT[:, d_, s0:s0 + 4, ri, :], in_=psb[b][:].rearrange("p (j c) -> p j c", c=128),
                        func=AF.Copy, scale=sgn), [pst[b]], [t_s5w])

    def s5_scan(l):
        HR, HI = hbuf[0], hbuf[1]
        hbb = [sb("hbb%d" % i, [128, 2 * SBUFW], BF16, at=U0 + i * SBUFW * 4) for i in range(4)]
        HRb, HIb = hbb[2], hbb[3]
        KL = NLEV

        def strided(buf, start, step, count):
            return buf[:, start:start + (count - 1) * step + 1:step]

        def ctxv(buf, start, step, count):
            return buf[:, NLAT:NLAT + NCTX].rearrange("p (s c) -> p s c", c=SEQ)[:, :, start:start + (count - 1) * step + 1:step]

        for cc_ in range(4):
            for d_ in range(2):
                fwd = d_ == 0
                for scl in range(4):
                    sc = 4 * cc_ + scl
                    fr, fi, nfi = (s5f[:, d_, sc, i:i + 1] for i in range(3))
                    last_r = last_i = None
                    for ti in range(NT):
                        tcol = ti * T if ti < NTL else NLAT
                        pe(lambda e: e.matmul(psb[5][:], lhsT=bbT[:, d_, sc, 0, :], rhs=uT[:, cc_, tcol:tcol + T],
                                              start=True, stop=True), [t_s5w, t_u[cc_ * NT + ti]], [pst[5]])
                        pe(lambda e: e.matmul(psb[6][:], lhsT=bbT[:, d_, sc, 1, :], rhs=uT[:, cc_, tcol:tcol + T],
                                              start=True, stop=True), [t_s5w, t_u[cc_ * NT + ti]], [pst[6]])
                        o_r, o_i = HR[:, tcol:tcol + T], HI[:, tcol:tcol + T]
                        e1 = dve(lambda e: e.tensor_scalar(out=o_r, in0=psb[5][:], scalar1=fr, scalar2=None, op0=ALU.mult),
                                 [pst[5], t_s5p], [t_hb[0]])
                        e2 = dve(lambda e: e.tensor_scalar(out=o_i, in0=psb[6][:], scalar1=fr, scalar2=None, op0=ALU.mult),
                                 [pst[6], t_s5p], [t_hb[1]])
                        last_r = dve(lambda e: e.scalar_tensor_tensor(out=o_r, in0=psb[6][:], scalar=nfi, in1=o_r,
                                                                      op0=ALU.mult, op1=ALU.add), [pst[6], t_s5p], [t_hb[0]])
                        last_i = dve(lambda e: e.scalar_tensor_tensor(out=o_i, in0=psb[5][:], scalar=fi, in1=o_i,
                                                                      op0=ALU.mult, op1=ALU.add), [pst[5], t_s5p], [t_hb[1]])
                    if l + 1 < L:
                        n_ = (cc_ * 2 + d_) * 4 + scl
                        mods_slots(l + 1, n_ * 72 // 32, (n_ + 1) * 72 // 32, 7)
                    col = 0 if fwd else NLAT - 1
                    last_r = dve(lambda e: e.tensor_tensor(out=HR[:, col:col + 1], in0=HR[:, col:col + 1],
                                                           in1=s5ah0[:, d_, sc, 0:1], op=ALU.add), [t_s5p], [t_hb[0]])
                    last_i = dve(lambda e: e.tensor_tensor(out=HI[:, col:col + 1], in0=HI[:, col:col + 1],
                                                           in1=s5ah0[:, d_, sc, 1:2], op=ALU.add), [t_s5p], [t_hb[1]])

                    def update(k, views):
                        nonlocal last_r, last_i
                        pr, pi_, npi = (s5pow[:, d_, sc, k, i:i + 1] for i in range(3))
                        for (dv, sv) in views:
                            rd, rs, id_, is_ = dv(HR), sv(HR), dv(HI), sv(HI)
                            a1 = P.op("dve", lambda e: e.scalar_tensor_tensor(out=rd, in0=rs, scalar=pr, in1=rd, op0=ALU.mult, op1=ALU.add),
                                      [t_hb[0], t_s5p], [t_hb[0]], auto_self=False, extra=[last_r])
                            a3 = P.op("dve", lambda e: e.scalar_tensor_tensor(out=id_, in0=is_, scalar=pr, in1=id_, op0=ALU.mult, op1=ALU.add),
                                      [t_hb[1], t_s5p], [t_hb[1]], auto_self=False, extra=[last_i])
                            a2 = P.op("dve", lambda e: e.scalar_tensor_tensor(out=rd, in0=is_, scalar=npi, in1=rd, op0=ALU.mult, op1=ALU.add),
                                      [t_hb[0], t_hb[1], t_s5p], [t_hb[0]], auto_self=False, extra=[a1, last_i])
                            a4 = P.op("dve", lambda e: e.scalar_tensor_tensor(out=id_, in0=rs, scalar=pi_, in1=id_, op0=ALU.mult, op1=ALU.add),
                                      [t_hb[0], t_hb[1], t_s5p], [t_hb[1]], auto_self=False, extra=[a3, last_r])
                            last_r, last_i = a2, a4

                    for k in range(KL):
                        step, dd = 2 << k, 1 << k
                        ncol = NTOK if step <= SEQ else NLAT
                        cnt_ = ncol // step
                        if fwd:
                            update(k, [(lambda b, step=step, cnt_=cnt_: strided(b, step - 1, step, cnt_),
                                        lambda b, step=step, cnt_=cnt_, dd=dd: strided(b, step - 1 - dd, step, cnt_))])
                        else:
                            update(k, [(lambda b, step=step, cnt_=cnt_: strided(b, 0, step, cnt_),
                                        lambda b, step=step, cnt_=cnt_, dd=dd: strided(b, dd, step, cnt_))])
                    for k in range(KL - 2, -1, -1):
                        step, dd = 2 << k, 1 << k
                        views = []
                        cl = NLAT // step - 1
                        if cl >= 1:
                            if fwd:
                                views.append((lambda b, step=step, dd=dd, cl=cl: strided(b, step + dd - 1, step, cl),
                                              lambda b, step=step, dd=dd, cl=cl: strided(b, step - 1, step, cl)))
                            else:
                                views.append((lambda b, step=step, dd=dd, cl=cl: strided(b, dd, step, cl),
                                              lambda b, step=step, dd=dd, cl=cl: strided(b, 2 * dd, step, cl)))
                        cx = SEQ // step - 1
                        if cx >= 1:
                            if fwd:
                                views.append((lambda b, step=step, dd=dd, cx=cx: ctxv(b, step + dd - 1, step, cx),
                                              lambda b, step=step, dd=dd, cx=cx: ctxv(b, step - 1, step, cx)))
                            else:
                                views.append((lambda b, step=step, dd=dd, cx=cx: ctxv(b, dd, step, cx),
                                              lambda b, step=step, dd=dd, cx=cx: ctxv(b, 2 * dd, step, cx)))
                        update(k, views)
                    col = SEQ - 1 if fwd else 0
                    for i, (hb_, lst) in enumerate(((HR, last_r), (HI, last_i))):
                        P.op("dve", lambda e: e.tensor_copy(out=s5fin[:, :, d_, sc, i:i + 1], in_=ctxv(hb_, col, 1, 1)),
                             [t_hb[i]], [t_s5st], auto_self=False, extra=[lst])
                    act(lambda e: e.activation(out=HRb[:, 0:NTOK], in_=HR[:, 0:NTOK], func=AF.Copy), [t_hb[0]], [t_hb[2]])
                    act(lambda e: e.activation(out=HIb[:, 0:NTOK], in_=HI[:, 0:NTOK], func=AF.Copy), [t_hb[1]], [t_hb[3]])
                    first = (d_ == 0 and scl == 0)
                    last = (d_ == 1 and scl == 3)
                    for ti in range(NT):
                        yb = ti if ti < NTL else 4
                        tcol = ti * T if ti < NTL else NLAT
                        pe(lambda e: e.matmul(psb[yb][:], lhsT=ccT[:, d_, sc, 0, :], rhs=HRb[:, tcol:tcol + T],
                                              start=first, stop=False), [t_s5w, t_hb[2]], [pst[yb]])
                        pe(lambda e: e.matmul(psb[yb][:], lhsT=ccT[:, d_, sc, 1, :], rhs=HIb[:, tcol:tcol + T],
                                              start=False, stop=last), [t_s5w, t_hb[3]], [pst[yb]])
            for ti in range(NT):
                yb = ti if ti < NTL else 4
                tcol = ti * T if ti < NTL else NLAT
                dve(lambda e: e.scalar_tensor_tensor(
                    out=uT[:, cc_, tcol:tcol + T], in0=uT[:, cc_, tcol:tcol + T], scalar=s5d[:, l, cc_:cc_ + 1],
                    in1=psb[yb][:], op0=ALU.mult, op1=ALU.add), [pst[yb], t_par, t_u[cc_ * NT + ti]], [t_u[cc_ * NT + ti]])
        for sq_ in range(2):
            P.dma("sp", lambda e: e.dma_start(out=o_s5[sq_, l].rearrange("d (s q) r -> q d s r", q=128),
                                              in_=s5fin[:, sq_, :, :, :]), s_oo, reads=[t_s5st], writes=[t_oo])
        if l + 1 < L:
            mods_finish(l + 1, 7)

    def s5_glu(l):
        wv, wt = wload(wview(W["s5_w_glu"][l], 0, 4, 0, 512), 4, 512)
        for ti in range(NT):
            tcol = ti * T if ti < NTL else NLAT
            ut = [t_u[c * NT + ti] for c in range(4)]
            for co in range(4):
                for k in range(4):
                    pe(lambda e, co=co, k=k: e.matmul(psb[co][:], lhsT=wv[:, k, co * 128:(co + 1) * 128],
                                                      rhs=uT[:, k, tcol:tcol + T], start=(k == 0), stop=(k == 3)),
                       [wt, ut[k]], [pst[co]])
            for co in range(4):
                y = uT[:, co, tcol:tcol + T]
                t1, tt1 = next_tmp()
                dve(lambda e, y=y, t1=t1: e.tensor_tensor(out=t1[:], in0=y, in1=y, op=ALU.mult), [ut[co]], [tt1])
                dve(lambda e, t1=t1: e.tensor_scalar(out=t1[:], in0=t1[:], scalar1=0.044715, scalar2=1.0,
                                                     op0=ALU.mult, op1=ALU.add), [tt1], [tt1])
                dve(lambda e, y=y, t1=t1: e.tensor_tensor(out=t1[:], in0=t1[:], in1=y, op=ALU.mult), [tt1, ut[co]], [tt1])
                act(lambda e, t1=t1: e.activation(out=t1[:], in_=t1[:], func=AF.Sigmoid, scale=1.5957691216057308),
                    [tt1], [tt1])
                t2, tt2 = next_tmp()
                act(lambda e, t2=t2, co=co: e.activation(out=t2[:], in_=psb[co][:], func=AF.Sigmoid,
                                                         bias=bglu[:, l, co:co + 1]), [pst[co], t_par], [tt2])
                dve(lambda e, t1=t1, t2=t2: e.tensor_tensor(out=t1[:], in0=t1[:], in1=t2[:], op=ALU.mult), [tt1, tt2], [tt1])
                dve(lambda e, y=y, t1=t1: e.tensor_tensor(out=y, in0=y, in1=t1[:], op=ALU.mult), [tt1, ut[co]], [ut[co]])

    def attend(qparts, qtoks, kfn, vfn, kcs, scale, out_c, qc0, n, pair):
        bo, bd = (3, 4) if pair == 0 else (5, 6)
        nk = len(kcs)

        def qk(i):
            bs = i % 3
            kaps = kfn(kcs[i])
            for pi_, (qap, kap) in enumerate(zip(qparts, kaps)):
                pe(lambda e, bs=bs, qap=qap, kap=kap, pi_=pi_: e.matmul(
                    psb[bs][:, 0:n], lhsT=kap, rhs=qap, start=(pi_ == 0), stop=(pi_ == len(qparts) - 1)),
                   [t_kv, t_kn] + qtoks, [pst[bs]])
        qk(0)
        for i in range(nk):
            bs = i % 3
            pT_, tp = next_pT()
            act(lambda e, bs=bs, pT_=pT_: e.activation(out=pT_[:, 0:n], in_=psb[bs][:, 0:n], func=AF.Exp, scale=scale),
                [pst[bs]], [tp])
            if i + 1 < nk:
                qk(i + 1)
            pe(lambda e, pT_=pT_, i=i, vap=vfn(kcs[i]): e.matmul(psb[bo][:, 0:n], lhsT=vap, rhs=pT_[:, 0:n],
                                                                 start=(i == 0), stop=(i == nk - 1)),
               [tp, t_kv, t_vh], [pst[bo]])
            pe(lambda e, pT_=pT_, i=i: e.matmul(psb[bd][:, 0:n], lhsT=ones[:], rhs=pT_[:, 0:n],
                                                start=(i == 0), stop=(i == nk - 1)), [tp, t_const], [pst[bd]])
        tm, tt = next_tmp()
        dve(lambda e, tm=tm: e.reciprocal(out=tm[:, 0:n], in_=psb[bd][:, 0:n]), [pst[bd]], [tt])
        dve(lambda e, tm=tm: e.tensor_tensor(out=catT[:, out_c, qc0:qc0 + n], in0=psb[bo][:, 0:n], in1=tm[:, 0:n],
                                             op=ALU.mult), [pst[bo], tt], [t_cat[out_c]])

    def mix_attn(l, ti):
        lat = ti < NTL
        s = 0 if lat else 1
        tcol = ti * T if lat else NLAT
        load_x_tile(ti, False)
        norm_mod(1, s)
        if lat:
            load_rope(ti)
            kbase, nkeys = 0, NKL
        else:
            kbase, nkeys = NKL, NCTX
        def ev_zq(idx, b, m):
            act(lambda e: e.activation(out=mt[idx][:], in_=psb[b][:], func=AF.Copy), [pst[b]], [t_mt[idx]])
        linear(W["w_in"][l], NCH, lambda k: hT[:, k, :], t_h, [(C_ZQ + 128 * i, 128) for i in range(4)], ev_zq, [0, 1, 2])
        sumsq_rstd([(mt[i][:], t_mt[i], 128) for i in range(4)], 512, 7)
        for c in range(4):
            dve(lambda e, c=c: e.scalar_tensor_tensor(out=zqn[:, c, :], in0=mt[c][:], scalar=qnorm[:, l, c:c + 1],
                                                      in1=rstd[:], op0=ALU.mult, op1=ALU.mult),
                [t_mt[c], t_rstd, t_par], [t_zqn])
        def ev_gq(idx, b, m):
            act(lambda e: e.activation(out=mt[idx][:], in_=psb[b][:], func=AF.Copy), [pst[b]], [t_mt[idx]])
            sumsq_rstd([(mt[idx][:], t_mt[idx], 128)], 128, 7)
            dve(lambda e: e.scalar_tensor_tensor(out=mt[idx][:], in0=mt[idx][:], scalar=gqn[:, l, 0:1], in1=rstd[:],
                                                 op0=ALU.mult, op1=ALU.mult), [t_mt[idx], t_rstd, t_par], [t_mt[idx]])
            if lat:
                rope(mt[idx][:], t_mt[idx], 128, perm128, rope_g, gqT[:, idx, :], [t_gq[idx]])
            else:
                act(lambda e: e.activation(out=gqT[:, idx, :], in_=mt[idx][:], func=AF.Copy), [t_mt[idx]], [t_gq[idx]])
        linear(W["w_in"][l], NCH, lambda k: hT[:, k, :], t_h, [(C_GQ + 128 * i, 128) for i in range(4)], ev_gq, [0, 1, 2])
        for h in range(8):
            wq, wqt = wload(wview(W["mla_w_uq"][l], 0, 4, h * 192, 192), 4, 256, 192)
            b = nbank([0, 1, 2])
            for k in range(4):
                pe(lambda e, b=b, k=k: e.matmul(psb[b][:], lhsT=wq[:, k, 0:128], rhs=zqn[:, k, :], start=(k == 0), stop=(k == 3)),
                   [wqt, t_zqn], [pst[b]])
            act(lambda e, b=b: e.activation(out=qn_h[:], in_=psb[b][:], func=AF.Copy), [pst[b]], [t_qh])
            b = nbank([0, 1, 2])
            for k in range(4):
                pe(lambda e, b=b, k=k: e.matmul(psb[b][:, :], lhsT=wq[:, k, 128:256], rhs=zqn[:, k, :], start=(k == 0), stop=(k == 3)),
                   [wqt, t_zqn], [pst[b]])
            if lat:
                act(lambda e, b=b: e.activation(out=mt[0][0:64, :], in_=psb[b][0:64, :], func=AF.Copy), [pst[b]], [t_mt[0]])
                rope(mt[0][0:64, :], t_mt[0], 64, perm64, rope_m, qr_h[:], [t_qh])
            else:
                act(lambda e, b=b: e.activation(out=qr_h[:], in_=psb[b][0:64, :], func=AF.Copy), [pst[b]], [t_qh])
            wk, wkt = wload(wview(W["mla_w_ukv"][l], 0, 2, h * 256, 256), 2, 256)
            for c0 in range(0, nkeys, T):
                b = nbank([0, 1, 2])
                for k in range(2):
                    pe(lambda e, b=b, k=k, c0=c0: e.matmul(psb[b][:], lhsT=wk[:, k, 0:128], rhs=ckvT[:, k, kbase + c0:kbase + c0 + T],
                                                           start=(k == 0), stop=(k == 1)), [wkt, t_kv], [pst[b]])
                copy_alt(c0 // T, knope[:, c0:c0 + T], psb[b][:], [pst[b]], [t_kn])
            for kc0 in range(0, nkeys // 128, 4):
                b = nbank([0, 1, 2])
                for j in range(4):
                    for k in range(2):
                        kc = kbase // 128 + kc0 + j
                        pe(lambda e, b=b, j=j, k=k, kc=kc: e.matmul(
                            psb[b][:, j * 128:(j + 1) * 128], lhsT=ckvT[:, k, kc * 128:(kc + 1) * 128], rhs=wk[:, k, 128:256],
                            start=(k == 0), stop=(k == 1)), [wkt, t_kv], [pst[b]])
                copy_alt(kc0 // 4 + 1, vh[:, kc0:kc0 + 4, :], psb[b][:].rearrange("p (j c) -> p j c", c=128), [pst[b]], [t_vh])
            if lat:
                attend([qn_h[:], qr_h[:]], [t_qh],
                       lambda kc: [knope[:, kc * 128:(kc + 1) * 128], krT[0:64, kc * 128:(kc + 1) * 128]],
                       lambda kc: vh[:, kc, :], list(range(NKL // 128)), MLA_SCALE, h, 0, T, h % 2)
            else:
                for sq_ in range(2):
                    attend([qn_h[:, sq_ * 256:(sq_ + 1) * 256], qr_h[:, sq_ * 256:(sq_ + 1) * 256]], [t_qh],
                           lambda kc: [knope[:, kc * 128:(kc + 1) * 128], krT[0:64, NKL + kc * 128:NKL + (kc + 1) * 128]],
                           lambda kc: vh[:, kc, :], [2 * sq_, 2 * sq_ + 1], MLA_SCALE, h, sq_ * 256, 256, (2 * h + sq_) % 2)
        for g in range(4):
            kvh = g // 2
            if lat:
                attend([gqT[:, g, :]], [t_gq[g]],
                       lambda kc: [gkT[:, kvh, kc * 128:(kc + 1) * 128]],
                       lambda kc: gvS[:, kc, kvh * 128:(kvh + 1) * 128], list(range(NKL // 128)), GQA_SCALE, 12 + g, 0, T, g % 2)
            else:
                for sq_ in range(2):
                    attend([gqT[:, g, sq_ * 256:(sq_ + 1) * 256]], [t_gq[g]],
                           lambda kc: [gkT[:, kvh, kc * 128:(kc + 1) * 128]],
                           lambda kc: gvS[:, kc, kvh * 128:(kvh + 1) * 128],
                           [NKL // 128 + 2 * sq_, NKL // 128 + 2 * sq_ + 1], GQA_SCALE, 12 + g, sq_ * 256, 256, (2 * g + sq_) % 2)
        def rhs_fn(k):
            if 8 <= k < 12:
                return uT[:, k - 8, tcol:tcol + T]
            return catT[:, k, :]
        rt = [t_u[(k - 8) * NT + ti] if 8 <= k < 12 else t_cat[k] for k in range(NCH)]

        def ev_out(idx, b, m):
            dve(lambda e: e.scalar_tensor_tensor(out=xT[:, idx, :], in0=psb[b][:], scalar=geff[:, 1, idx, s:s + 1],
                                                 in1=xT[:, idx, :], op0=ALU.mult, op1=ALU.add),
                [pst[b], t_mods, t_x[idx]], [t_x[idx]])
            if idx % 4 == 3:
                store_x_group(ti, idx // 4)
        linear(W["w_out"][l], NCH, rhs_fn, rt, [(128 * i, 128) for i in range(NCH)], ev_out, [0, 1, 2, 7])
        P.barrier()

    def mixer(l):
        if STOP == -2:
            return
        P.barrier()
        if STOP == -1:
            return
        load_cache(l)
        if STOP == 1:
            return
        for ti in range(NT):
            mix_proj(l, ti)
        P.barrier()
        if STOP in (2, 21, 22, 23, 24):
            return
        s5_prep(l)
        P.barrier()
        if STOP == 3:
            return
        s5_scan(l)
        if STOP == 4:
            P.barrier()
            return
        s5_glu(l)
        P.barrier()
        if STOP == 5:
            return
        for ti in range(NT):
            mix_attn(l, ti)

    compute_mods(0)
    for l in range(L):
        mods, gmod, geff, t_mods = MB[l % 2]
        for ti in range(NT):
            s = 0 if ti < NTL else 1
            load_x_tile(ti, first=(l == 0))
            norm_mod(0, s)
            ffn(l, 1, s, store_ti=ti)
        mixer(l)
        for ti in range(NT):
            s = 0 if ti < NTL else 1
            load_x_tile(ti, first=False)
            norm_mod(2, s)
            ffn(l, 2, s, store_ti=(None if l == L - 1 else ti))
            if l == L - 1:
                final_out(ti)
    P.barrier()
    P.emit()
    ncd.__exit__(None, None, None)
    return nc


def _rope_tables(nlat, rot_dim):
    n_rows = nlat // 64
    row = np.repeat(np.arange(n_rows, dtype=np.float32), 64)
    col = np.tile(np.arange(64, dtype=np.float32), n_rows)
    n_freq = rot_dim // 4
    inv = (np.float32(10000.0) ** (-np.arange(n_freq, dtype=np.float32) / np.float32(n_freq))).astype(np.float32)
    ang = np.concatenate([row[:, None] * inv, col[:, None] * inv], axis=-1).astype(np.float32)
    c = np.cos(ang).astype(np.float32).T
    s = np.sin(ang).astype(np.float32).T
    return np.ascontiguousarray(np.stack([np.concatenate([c, c], 0), np.concatenate([s, s], 0)], 0))


def _perm(n):
    h = n // 2
    p = np.zeros((n, n), np.float32)
    for d in range(h):
        p[d + h, d] = -1.0
        p[d, d + h] = 1.0
    return p


def make_in_maps(inp, ncores, L, NLAT):
    consts = {
        "ident": np.eye(128, dtype=np.float32),
        "perm128": _perm(128),
        "perm64": _perm(64),
        "ropeg": _rope_tables(NLAT, 128),
        "ropem": _rope_tables(NLAT, 64),
    }
    shared = {}
    for name, shp in WEIGHT_SPECS:
        shared[name] = np.ascontiguousarray(np.asarray(inp[name], dtype=np.float32).reshape((L,) + shp))
    shared["norm_final"] = np.ascontiguousarray(np.asarray(inp["norm_final"], dtype=np.float32))
    shared.update(consts)
    maps = []
    xp = np.asarray(inp["x_prompt"])
    for i in range(ncores):
        m = dict(shared)
        m["x_lat"] = np.ascontiguousarray(np.asarray(inp["x_sample"])[i])
        m["x_ctx"] = np.ascontiguousarray(xp[2 * i:2 * i + 2].reshape(NCTX, D))
        m["cvec"] = np.ascontiguousarray(np.stack([np.asarray(inp["c"])[i], np.asarray(inp["c_ctx"])], 0))
        m["c_ckv"] = np.ascontiguousarray(np.asarray(inp["cache_mla_ckv"])[i])
        m["c_kr"] = np.ascontiguousarray(np.asarray(inp["cache_mla_krope"])[i])
        m["c_k"] = np.ascontiguousarray(np.asarray(inp["cache_gqa_k"])[i].reshape(L, PAST, 256))
        m["c_v"] = np.ascontiguousarray(np.asarray(inp["cache_gqa_v"])[i].reshape(L, PAST, 256))
        m["st_s5"] = np.ascontiguousarray(np.asarray(inp["state_s5"])[i].reshape(L, 2, 2048, 2))
        maps.append(m)
    return maps


def gather_outputs(results, ncores, L, NLAT):
    y_prompt = np.concatenate([r["y_ctx"].reshape(2, SEQ, D) for r in results], 0)
    y_sample = np.stack([r["y_lat"] for r in results], 0)
    ckv = np.concatenate([r["o_ckv"] for r in results], 0)
    kr = np.concatenate([r["o_kr"] for r in results], 0)
    k = np.concatenate([r["o_k"].reshape(2, L, SEQ, 2, 128) for r in results], 0)
    v = np.concatenate([r["o_v"].reshape(2, L, SEQ, 2, 128) for r in results], 0)
    s5 = np.concatenate([r["o_s5"].reshape(2, L, 2, 32, 64, 2) for r in results], 0)
    return tuple(np.ascontiguousarray(a.astype(np.float32)) for a in (y_prompt, y_sample, ckv, kr, k, v, s5))


_NC_CACHE = {}


def kernel(**inputs):
    L, NLAT, ncores = 4, 2048, 8
    key = (L, NLAT)
    if key not in _NC_CACHE:
        _NC_CACHE[key] = build(L, NLAT)
    nc = _NC_CACHE[key]
    maps = make_in_maps(inputs, ncores, L, NLAT)
    res = run_bass_kernel_spmd(nc, maps, core_ids=list(range(ncores)))
    return gather_outputs(res.results, ncores, L, NLAT)
```

```python
import math
import contextlib
import numpy as np
import concourse.bass as bass
import concourse.mybir as mybir
from concourse.bass_utils import run_bass_kernel_spmd

F32 = mybir.dt.float32
BF16 = mybir.dt.bfloat16
ALU = mybir.AluOpType
AF = mybir.ActivationFunctionType

ENGS = ("pe", "act", "dve", "pool", "sp")
SEM_ROT = 16000
EMBED_WAIT = 1


class Tok:
    __slots__ = ("w", "r", "rd")

    def __init__(self):
        self.w = None
        self.r = {}
        self.rd = []


def toks(n):
    return [Tok() for _ in range(n)]


class DSem:
    __slots__ = ("idx", "count", "waited_max", "acked")

    def __init__(self, idx):
        self.idx = idx
        self.count = 0
        self.waited_max = 0
        self.acked = {}


class _Rec:
    def __init__(self):
        self.call = None

    def __getattr__(self, name):
        def f(*a, **k):
            self.call = (name, a, k)
            return None
        return f


def _freeze(fn):
    if fn is None:
        return None
    r = _Rec()
    fn(r)
    name, a, k = r.call
    import sys
    try:
        line = sys._getframe(3).f_lineno
    except ValueError:
        line = -1

    def g(e):
        return getattr(e, name)(*a, **k)
    g.line = line
    return g


class Prog:
    def __init__(self, nc, selfsync=True):
        self.nc = nc
        self.ops = {e: [] for e in ENGS}
        self.selfsync = selfsync
        self.dsems = []

    def new_dsem(self):
        d = DSem(len(self.dsems))
        self.dsems.append(d)
        return d

    def _collect(self, eng, reads, writes, auto_self=True, extra=()):
        deps = set(extra)
        for t in reads:
            if t.w is not None:
                deps.add(t.w)
        for t in writes:
            if t.w is not None:
                deps.add(t.w)
            for e, s in t.r.items():
                if e != eng or (self.selfsync and auto_self and eng not in ("pe", "sp")):
                    deps.add(("e", e, s))
            for d in t.rd:
                deps.add(d)
        waits = []
        for d in deps:
            if d[0] == "e":
                if d[1] == eng and d not in extra and (eng in ("pe", "sp") or not self.selfsync or not auto_self):
                    continue
                self.ops[d[1]][d[2]][2] = True
                waits.append(d)
            else:
                ds = self.dsems[d[1]]
                v = max(d[2], ds.count)
                ds.waited_max = max(ds.waited_max, v)
                waits.append(("d", d[1], v))
        return waits

    def op(self, eng, fn, reads=(), writes=(), auto_self=True, extra=()):
        waits = self._collect(eng, reads, writes, auto_self, tuple(e for e in extra if e is not None))
        seq = len(self.ops[eng])
        fn = _freeze(fn)
        self.ops[eng].append([fn, waits, False, None])
        ev = ("e", eng, seq)
        for t in reads:
            if t.r.get(eng, -1) < seq:
                t.r[eng] = seq
        for t in writes:
            t.w = ev
            t.r = {}
            t.rd = []
        return ev

    def dma(self, q, fn, dsem, reads=(), writes=()):
        waits = self._collect(q, reads, writes)
        if dsem.waited_max > dsem.acked.get(q, 0):
            waits.append(("d", dsem.idx, dsem.waited_max))
            dsem.acked[q] = dsem.waited_max
        fn = _freeze(fn)
        self.ops[q].append([fn, waits, False, dsem.idx])
        dsem.count += 16
        ev = ("d", dsem.idx, dsem.count)
        for t in reads:
            t.rd.append(ev)
        for t in writes:
            t.w = ev
            t.r = {}
            t.rd = []
        return ev

    def barrier(self):
        last = {}
        for e in ENGS:
            for i in range(len(self.ops[e]) - 1, -1, -1):
                if self.ops[e][i][0] is not None and self.ops[e][i][3] is None:
                    last[e] = i
                    break
        for e in ENGS:
            waits = []
            for f, s in last.items():
                if f != e:
                    self.ops[f][s][2] = True
                    waits.append(("e", f, s))
            for d in self.dsems:
                if d.count:
                    waits.append(("d", d.idx, d.count))
                    d.waited_max = max(d.waited_max, d.count)
            self.ops[e].append([None, waits, False, None])

    def emit(self):
        nc = self.nc
        with contextlib.ExitStack() as st:
            nsig = {e: sum(1 for o in self.ops[e] if o[2]) for e in ENGS}
            esems = {}
            for e in ENGS:
                n = nsig[e] // SEM_ROT + 1
                esems[e] = [st.enter_context(nc.semaphore("s_%s%d" % (e, i))) for i in range(n)]
            dsem_h = [st.enter_context(nc.semaphore("d%d" % i)) for i in range(len(self.dsems))]
            signum = {}
            for e in ENGS:
                c = 0
                arr = []
                for o in self.ops[e]:
                    if o[2]:
                        c += 1
                    arr.append(c)
                signum[e] = arr
            block = st.enter_context(nc.Block())

            def run(ename, eng):
                waited = {}
                sc = 0
                for (fn, waits, signal, dsi) in self.ops[ename]:
                    pend = []
                    for w in waits:
                        if w[0] == "e":
                            n = signum[w[1]][w[2]]
                            k = (n - 1) // SEM_ROT
                            v = n - k * SEM_ROT
                            key = ("e", w[1], k)
                            sem = esems[w[1]][k]
                        else:
                            key = ("d", w[1])
                            v = w[2]
                            sem = dsem_h[w[1]]
                        if waited.get(key, 0) >= v:
                            continue
                        waited[key] = v
                        pend.append((sem, v))
                    emb = None
                    if EMBED_WAIT and fn is not None and dsi is None and pend:
                        emb = pend.pop()
                    for (sem, v) in pend:
                        eng.wait_ge(sem, v)
                    if fn is None:
                        continue
                    try:
                        ins = fn(eng)
                    except BaseException as ex:
                        print("EMIT FAIL on", ename, "line", getattr(fn, "line", None), repr(ex)[:300])
                        raise
                    if emb is not None:
                        ins._wait_ge(emb[0], emb[1])
                    if dsi is not None:
                        ins.then_inc(dsem_h[dsi], 16)
                    elif signal:
                        sc += 1
                        ins.then_inc(esems[ename][(sc - 1) // SEM_ROT], 1)

            @block.tensor
            def _(eng):
                run("pe", eng)

            @block.scalar
            def _(eng):
                run("act", eng)

            @block.vector
            def _(eng):
                run("dve", eng)

            @block.gpsimd
            def _(eng):
                run("pool", eng)

            @block.sync
            def _(eng):
                run("sp", eng)


D = 2048
DFF = 4096
T = 512
NCH = 16
EPS = 1e-6
PAST = 512
SEQ = 256
NCTX = 512
D_IN = 2368
C_ZQ, C_ZKV, C_ZKR, C_U, C_GQ, C_GK, C_GV = 0, 512, 768, 832, 1344, 1856, 2112
MLA_SCALE = 1.0 / math.sqrt(192.0)
GQA_SCALE = 1.0 / math.sqrt(128.0)
SLOT = 4096
NSLOT = 4

WEIGHT_SPECS = [
    ("w_ada", (D, 9 * D)), ("b_ada", (9 * D,)), ("norm_ffn1", (D,)),
    ("ffn1_w_gate", (D, DFF)), ("ffn1_w_up", (D, DFF)), ("ffn1_w_down", (DFF, D)),
    ("norm_mix", (D,)), ("w_in", (D, D_IN)), ("mla_q_norm", (512,)), ("mla_w_uq", (512, 1536)),
    ("mla_kv_norm", (256,)), ("mla_w_ukv", (256, 2048)),
    ("s5_lambda_re", (2, 32, 64)), ("s5_lambda_im", (2, 32, 64)), ("s5_log_dt", (2, 32)),
    ("s5_b_re", (2, 32, 64, 16)), ("s5_b_im", (2, 32, 64, 16)),
    ("s5_c_re", (2, 32, 16, 64)), ("s5_c_im", (2, 32, 16, 64)),
    ("s5_d", (512,)), ("s5_w_glu", (512, 512)), ("s5_b_glu", (512,)),
    ("gqa_q_norm", (128,)), ("gqa_k_norm", (128,)), ("w_out", (D, D)), ("norm_ffn2", (D,)),
    ("ffn2_w_gate", (D, DFF)), ("ffn2_w_up", (D, DFF)), ("ffn2_w_down", (DFF, D)),
]


def build(L=4, NLAT=2048, debug=None, STOP=0):
    nc = bass.Bass("TRN2", target_bir_lowering=False)
    P = Prog(nc)
    NTL = NLAT // T
    NT = NTL + 1
    NTOK = NLAT + NCTX
    NKL = PAST + NLAT
    KTOT = NKL + NCTX
    SPAD = 1024 if NLAT > 1024 else NLAT // 2
    NLEV = int(math.log2(NLAT))

    def din(name, shape):
        return nc.dram_tensor(name, list(shape), F32, kind="ExternalInput").ap()

    def dout(name, shape):
        return nc.dram_tensor(name, list(shape), F32, kind="ExternalOutput").ap()

    x_lat = din("x_lat", (NLAT, D))
    x_ctx = din("x_ctx", (NCTX, D))
    cvec = din("cvec", (2, D))
    c_ckv = din("c_ckv", (L, PAST, 256))
    c_kr = din("c_kr", (L, PAST, 64))
    c_k = din("c_k", (L, PAST, 256))
    c_v = din("c_v", (L, PAST, 256))
    st_s5 = din("st_s5", (L, 2, 2048, 2))
    W = {}
    for name, shp in WEIGHT_SPECS:
        W[name] = din(name, (L,) + shp)
    norm_final = din("norm_final", (D,))
    ident_d = din("ident", (128, 128))
    perm128_d = din("perm128", (128, 128))
    perm64_d = din("perm64", (64, 64))
    ropeg_d = din("ropeg", (2, 128, NLAT))
    ropem_d = din("ropem", (2, 64, NLAT))

    y_lat = dout("y_lat", (NLAT, D))
    y_ctx = dout("y_ctx", (NCTX, D))
    o_ckv = dout("o_ckv", (2, L, SEQ, 256))
    o_kr = dout("o_kr", (2, L, SEQ, 64))
    o_k = dout("o_k", (2, L, SEQ, 256))
    o_v = dout("o_v", (2, L, SEQ, 256))
    o_s5 = dout("o_s5", (2, L, 2, 2048, 2))
    xs_d = nc.dram_tensor("xs_scratch", [NT, 128, NCH, T], F32).ap()
    dbg = {}
    if debug:
        for nm, shp in debug.items():
            dbg[nm] = dout("dbg_" + nm, shp)

    SB_BASE = 16512
    SB_END = 229376
    cur = [SB_BASE]

    def sb(name, shape, dt, at=None):
        nb = int(np.prod(shape[1:])) * (4 if dt == F32 else 2)
        nb = (nb + 63) // 64 * 64
        if at is None:
            off = cur[0]
            cur[0] += nb
            assert cur[0] <= SB_END, ("SBUF overflow", name, cur[0])
        else:
            off = at
        return nc.alloc_sbuf_tensor_at(name, list(shape), dt, offset=off)

    ident = sb("ident", [128, 128], F32)
    identb = sb("identb", [128, 128], BF16)
    perm128 = sb("perm128", [128, 128], F32)
    perm64 = sb("perm64", [64, 64], F32)
    ones = sb("ones", [128, 128], BF16)
    perm128b = sb("perm128b", [128, 128], BF16)
    perm64b = sb("perm64b", [64, 64], BF16)
    epsc = sb("epsc", [128, 8], F32)
    cT = sb("cT", [128, NCH, 2], F32)
    csT = sb("csT", [128, NCH, 2], BF16)
    badaT = sb("badaT", [128, L, 144], F32)
    gnorm = sb("gnorm", [128, L, 3, NCH], F32)
    gfin = sb("gfin", [128, NCH], F32)
    qnorm = sb("qnorm", [128, L, 4], F32)
    kvnorm = sb("kvnorm", [128, L, 2], F32)
    gqn = sb("gqn", [128, L, 2], F32)
    s5d = sb("s5d", [128, L, 4], F32)
    bglu = sb("bglu", [128, L, 4], F32)
    MB = []
    for i_ in range(2):
        MB.append([sb("mods%d" % i_, [128, 9, NCH, 2], F32),
                   sb("gmod%d" % i_, [128, 3, NCH, 2], F32),
                   sb("geff%d" % i_, [128, 3, NCH, 2], F32),
                   Tok()])
    mods, gmod, geff, t_mods = MB[0]
    ROPE0 = cur[0]
    rope_g = sb("rope_g", [128, 2, T], F32)
    rope_m = sb("rope_m", [64, 2, T], F32)
    ckvT = sb("ckvT", [128, 2, KTOT], BF16)
    krT = sb("krT", [64, KTOT], BF16)
    gkT = sb("gkT", [128, 2, KTOT], BF16)
    gvS = sb("gvS", [128, KTOT // 128, 256], BF16)
    uT = sb("uT", [128, 4, NTOK], BF16)
    ring = [sb("ring%d" % i, [128, SLOT], BF16) for i in range(NSLOT)]
    rstd = sb("rstd", [128, T], F32)
    TMP0 = cur[0]
    tmpA = [sb("tmpA%d" % i, [128, T], F32) for i in range(3)]
    sqb = [sb("sqb%d" % i, [128, T], BF16) for i in range(2)]
    pTb = [sb("pTb%d" % i, [128, T], BF16) for i in range(4)]
    U0 = cur[0]
    xT = sb("xT", [128, NCH, T], F32)
    hT = sb("hT", [128, NCH, T], BF16)
    A0 = cur[0]
    aT = sb("aT", [128, 32, T], BF16)
    UEND = cur[0]
    H0_ = U0 + NCH * T * 4
    catT = sb("catT", [128, NCH, T], BF16, at=A0)
    zqn = sb("zqn", [128, 4, T], BF16, at=A0 + 16384)
    gqT = sb("gqT", [128, 4, T], BF16, at=A0 + 20480)
    knope = sb("knope", [128, NKL], BF16, at=H0_)
    vh = sb("vh", [128, NKL // 128, 128], BF16, at=H0_ + NKL * 2)
    qn_h = sb("qn_h", [128, T], BF16, at=H0_ + NKL * 4)
    qr_h = sb("qr_h", [64, T], BF16, at=H0_ + NKL * 4 + 1024)
    assert NKL * 4 + 2048 <= NCH * T * 2
    stage = sb("stage", [128, D], F32, at=A0)
    SBUFW = max(SPAD + NLAT, 1024)
    hbuf = [sb("hbuf%d" % i, [128, SBUFW], F32, at=U0 + i * SBUFW * 4) for i in range(4)]
    o = U0 + 4 * SBUFW * 4
    bbT = sb("bbT", [128, 2, 16, 2, 128], BF16, at=o); o += 2 * 16 * 2 * 128 * 2
    ccT = sb("ccT", [128, 2, 16, 2, 128], BF16, at=o); o += 2 * 16 * 2 * 128 * 2
    assert o <= UEND, ("s5 carve overflow", o - UEND)
    negpi = sb("negpi", [128, 8], F32)
    halfpi = sb("halfpi", [128, 8], F32)
    S5ENG = "dve"
    o = ROPE0
    s5p = sb("s5p", [128, 2, 16, 8], F32, at=o); o += 1024
    s5pow = sb("s5pow", [128, 2, 16, 12, 3], F32, at=o); o += 4608
    s5h0 = sb("s5h0", [128, 2, 16, 2], F32, at=o); o += 256
    s5ah0 = sb("s5ah0", [128, 2, 16, 2], F32, at=o); o += 256
    s5fin = sb("s5fin", [128, 2, 2, 16, 2], F32, at=o); o += 512
    s5f = sb("s5f", [128, 2, 16, 3], F32, at=o); o += 384
    assert o <= ROPE0 + 8192
    s5frow = None
    print("SBUF used", cur[0] - SB_BASE, "of", SB_END - SB_BASE)

    psb = [nc.alloc_psum_tensor("psb%d" % i, [128, T], F32) for i in range(8)]
    pst = toks(8)

    t_const = Tok()
    t_par = Tok()
    t_x = toks(NCH)
    t_h = toks(NCH)
    t_a = toks(32)
    t_ring = toks(NSLOT)
    s_ring = [P.new_dsem() for _ in range(NSLOT)]
    t_xs = [toks(4) for _ in range(NT)]
    s_xld = [P.new_dsem() for _ in range(4)]
    s_xst = [P.new_dsem() for _ in range(4)]
    s_misc = P.new_dsem()
    s_out = P.new_dsem()
    t_out = Tok()
    t_rstd = Tok()
    t_tmpA = toks(3)
    t_sqb = toks(2)
    t_pT = toks(4)
    t_rope = Tok()
    s_rope = P.new_dsem()
    t_kv = Tok()
    t_u = toks(4 * NT)
    t_stage = Tok()
    s_stage = P.new_dsem()
    cnt = {"ring": 0, "tmp": 0, "sq": 0, "pT": 0}

    def act(fn, r, w):
        return P.op("act", fn, r, w)

    def dve(fn, r, w):
        return P.op("dve", fn, r, w)

    def pe(fn, r, w):
        return P.op("pe", fn, r, w)

    def wload(src_ap, nk, ncols, ndma=None):
        s = cnt["ring"] % NSLOT
        cnt["ring"] += 1
        dst = ring[s][:, 0:nk * ncols].rearrange("p (k c) -> p k c", c=ncols)
        dd = dst if ndma is None else dst[:, :, 0:ndma]
        P.dma("pool", lambda e: e.dma_start(out=dd, in_=src_ap), s_ring[s], writes=[t_ring[s]])
        return dst, t_ring[s]

    def wview(wap, k0, nk, c0, ncols):
        return wap.rearrange("(k p) n -> p k n", p=128)[:, k0:k0 + nk, c0:c0 + ncols]

    def next_tmp():
        i = cnt["tmp"] % 3
        cnt["tmp"] += 1
        return tmpA[i], t_tmpA[i]

    def next_sq():
        i = cnt["sq"] % 2
        cnt["sq"] += 1
        return sqb[i], t_sqb[i]

    def next_pT():
        i = cnt["pT"] % 4
        cnt["pT"] += 1
        return pTb[i], t_pT[i]

    ncd = nc.allow_non_contiguous_dma(reason="small strided parameter loads")
    ncd.__enter__()

    def sdma(out, in_, tok):
        P.dma("sp", lambda e: e.dma_start(out=out, in_=in_), s_misc, writes=[tok])

    sdma(ident[:], ident_d, t_const)
    sdma(perm128[:], perm128_d, t_const)
    sdma(perm64[:], perm64_d, t_const)
    for r in range(2):
        sdma(cT[:, :, r], cvec[r].rearrange("(k p) -> p k", p=128), t_par)
    for l in range(L):
        sdma(badaT[:, l, :], W["b_ada"][l].rearrange("(c p) -> p c", p=128), t_par)
        for j, nm in enumerate(("norm_ffn1", "norm_mix", "norm_ffn2")):
            sdma(gnorm[:, l, j, :], W[nm][l].rearrange("(k p) -> p k", p=128), t_par)
        sdma(qnorm[:, l, :], W["mla_q_norm"][l].rearrange("(k p) -> p k", p=128), t_par)
        sdma(kvnorm[:, l, :], W["mla_kv_norm"][l].rearrange("(k p) -> p k", p=128), t_par)
        sdma(gqn[:, l, 0:1], W["gqa_q_norm"][l].rearrange("(p o) -> p o", o=1), t_par)
        sdma(gqn[:, l, 1:2], W["gqa_k_norm"][l].rearrange("(p o) -> p o", o=1), t_par)
        sdma(s5d[:, l, :], W["s5_d"][l].rearrange("(k p) -> p k", p=128), t_par)
        sdma(bglu[:, l, :], W["s5_b_glu"][l].rearrange("(k p) -> p k", p=128), t_par)
    sdma(gfin[:], norm_final.rearrange("(k p) -> p k", p=128), t_par)
    P.op("pool", lambda e: e.memset(ones[:], 1.0), [], [t_const])
    for i_ in range(NSLOT):
        P.op("pool", lambda e, i_=i_: e.memset(ring[i_][:], 0.0), [], [t_ring[i_]])
    P.op("pool", lambda e: e.memset(epsc[:], EPS), [], [t_const])
    P.op("pool", lambda e: e.memset(negpi[:], -math.pi), [], [t_const])
    P.op("pool", lambda e: e.memset(halfpi[:], 0.5 * math.pi), [], [t_const])
    act(lambda e: e.activation(out=identb[:], in_=ident[:], func=AF.Copy), [t_const], [t_const])
    act(lambda e: e.activation(out=perm128b[:], in_=perm128[:], func=AF.Copy), [t_const], [t_const])
    act(lambda e: e.activation(out=perm64b[:], in_=perm64[:], func=AF.Copy), [t_const], [t_const])
    tm, tt = next_tmp()
    act(lambda e: e.activation(out=tm[:, 0:32], in_=cT[:].rearrange("p k r -> p (k r)"), func=AF.Sigmoid), [t_par], [tt])
    dve(lambda e: e.tensor_tensor(out=csT[:].rearrange("p k r -> p (k r)"), in0=tm[:, 0:32],
                                  in1=cT[:].rearrange("p k r -> p (k r)"), op=ALU.mult), [tt, t_par], [t_par])

    def load_x_tile(ti, first):
        if not first:
            for g in range(4):
                P.dma("sp", lambda e: e.dma_start(out=xT[:, 4 * g:4 * g + 4, :], in_=xs_d[ti][:, 4 * g:4 * g + 4, :]),
                      s_xld[g], reads=[t_xs[ti][g]], writes=t_x[4 * g:4 * g + 4])
            return
        src = x_lat[ti * T:(ti + 1) * T, :] if ti < NTL else x_ctx
        for tc4 in range(4):
            P.dma("sp", lambda e, tc4=tc4: e.dma_start(out=stage[:], in_=src[tc4 * 128:(tc4 + 1) * 128, :]),
                  s_stage, writes=[t_stage] + t_a)
            for k0 in range(0, NCH, 4):
                b = (k0 // 4) % 2
                for j in range(4):
                    pe(lambda e, b=b, j=j, k=k0 + j: e.transpose(
                        out=psb[b][:, j * 128:(j + 1) * 128], in_=stage[:, k * 128:(k + 1) * 128], identity=ident[:]),
                       [t_stage, t_const], [pst[b]])
                dve(lambda e, b=b, k0=k0, tc4=tc4: e.tensor_copy(
                    out=xT[:, k0:k0 + 4, tc4 * 128:(tc4 + 1) * 128],
                    in_=psb[b][:].rearrange("p (j t) -> p j t", t=128)), [pst[b]], t_x[k0:k0 + 4])

    def store_x_group(ti, g):
        P.dma("sp", lambda e: e.dma_start(out=xs_d[ti][:, 4 * g:4 * g + 4, :], in_=xT[:, 4 * g:4 * g + 4, :]),
              s_xst[g], reads=t_x[4 * g:4 * g + 4], writes=[t_xs[ti][g]])

    def store_x_tile(ti):
        for g in range(4):
            store_x_group(ti, g)

    def sumsq_rstd(chunks, nfeat, bank, n=T):
        nchk = len(chunks)
        for i, (ap, tk, pn) in enumerate(chunks):
            sq, tq = next_sq()
            act(lambda e, sq=sq, ap=ap, pn=pn: e.activation(out=sq[0:pn, 0:n], in_=ap, func=AF.Square), [tk], [tq])
            pe(lambda e, sq=sq, pn=pn, i=i: e.matmul(psb[bank][:, 0:n], lhsT=ones[0:pn, :], rhs=sq[0:pn, 0:n],
                                                     start=(i == 0), stop=(i == nchk - 1)), [tq, t_const], [pst[bank]])
        act(lambda e: e.activation(out=rstd[:, 0:n], in_=psb[bank][:, 0:n], func=AF.Sqrt, scale=1.0 / nfeat,
                                   bias=epsc[:, 0:1]), [pst[bank], t_const], [t_rstd])
        dve(lambda e: e.reciprocal(out=rstd[:, 0:n], in_=rstd[:, 0:n]), [t_rstd], [t_rstd])

    def norm_mod(j, s):
        sumsq_rstd([(xT[:, k, :], t_x[k], 128) for k in range(NCH)], D, 7)
        for k in range(NCH):
            tm, tt = next_tmp()
            dve(lambda e, tm=tm, k=k: e.tensor_tensor(out=tm[:], in0=xT[:, k, :], in1=rstd[:], op=ALU.mult),
                [t_x[k], t_rstd], [tt])
            act(lambda e, tm=tm, k=k: e.activation(out=hT[:, k, :], in_=tm[:], func=AF.Identity,
                                                   scale=gmod[:, j, k, s:s + 1], bias=mods[:, 3 * j, k, s:s + 1]),
                [tt, t_mods], [t_h[k]])

    def mods_slots(l, a, b, bank):
        for sl in range(a, b):
            wv, wt = wload(wview(W["w_ada"][l], 0, NCH, sl * 256, 256), NCH, 256)
            for c in range(2):
                ch = 2 * sl + c
                for k in range(NCH):
                    pe(lambda e: e.matmul(psb[bank][:, 2 * ch:2 * ch + 2], lhsT=wv[:, k, c * 128:(c + 1) * 128],
                                          rhs=csT[:, k, :], start=(k == 0), stop=(k == NCH - 1)), [wt, t_par], [pst[bank]])

    def mods_finish(l, bank):
        m_, gm_, ge_, tm_ = MB[l % 2]
        dve(lambda e: e.tensor_tensor(out=m_[:].rearrange("p v k s -> p (v k) s"),
                                      in0=psb[bank][:, 0:288].rearrange("p (c s) -> p c s", s=2),
                                      in1=badaT[:, l, :].unsqueeze(2).to_broadcast([128, 144, 2]), op=ALU.add),
            [pst[bank], t_par], [tm_])
        for j in range(3):
            dve(lambda e: e.scalar_tensor_tensor(
                out=gm_[:, j, :, :], in0=m_[:, 3 * j + 1, :, :], scalar=1.0,
                in1=gnorm[:, l, j, :].unsqueeze(2).to_broadcast([128, NCH, 2]), op0=ALU.add, op1=ALU.mult),
                [tm_, t_par], [tm_])
            dve(lambda e: e.tensor_scalar(out=ge_[:, j, :, :], in0=m_[:, 3 * j + 2, :, :],
                                          scalar1=(1.0 if j == 1 else 0.5), scalar2=None, op0=ALU.mult), [tm_], [tm_])

    def compute_mods(l):
        mods_slots(l, 0, 72, 6)
        mods_finish(l, 6)

    def ffn(l, which, s, store_ti=None):
        j = 0 if which == 1 else 2
        wg, wu, wd = (W["ffn%d_w_gate" % which][l], W["ffn%d_w_up" % which][l], W["ffn%d_w_down" % which][l])
        for sl in range(16):
            gv_, gt = wload(wview(wg, 0, NCH, sl * 256, 256), NCH, 256)
            uv_, ut = wload(wview(wu, 0, NCH, sl * 256, 256), NCH, 256)
            for c in range(2):
                f = 2 * sl + c
                bg, bu = f % 2, 2 + f % 2
                for k in range(NCH):
                    pe(lambda e, gv_=gv_, c=c, k=k, bg=bg: e.matmul(
                        psb[bg][:], lhsT=gv_[:, k, c * 128:(c + 1) * 128], rhs=hT[:, k, :],
                        start=(k == 0), stop=(k == NCH - 1)), [gt, t_h[k]], [pst[bg]])
                for k in range(NCH):
                    pe(lambda e, uv_=uv_, c=c, k=k, bu=bu: e.matmul(
                        psb[bu][:], lhsT=uv_[:, k, c * 128:(c + 1) * 128], rhs=hT[:, k, :],
                        start=(k == 0), stop=(k == NCH - 1)), [ut, t_h[k]], [pst[bu]])
                tm, tt = next_tmp()
                act(lambda e, tm=tm, bg=bg: e.activation(out=tm[:], in_=psb[bg][:], func=AF.Silu), [pst[bg]], [tt])
                dve(lambda e, tm=tm, bu=bu, f=f: e.tensor_tensor(out=aT[:, f, :], in0=tm[:], in1=psb[bu][:], op=ALU.mult),
                    [tt, pst[bu]], [t_a[f]])
        for sl in range(8):
            b0 = 4 + 2 * (sl % 2)
            for part in range(2):
                dv_, dt_ = wload(wview(wd, part * 16, 16, sl * 256, 256), 16, 256)
                for c in range(2):
                    for k in range(16):
                        kk = part * 16 + k
                        pe(lambda e, dv_=dv_, c=c, k=k, kk=kk, b=b0 + c, part=part: e.matmul(
                            psb[b][:], lhsT=dv_[:, k, c * 128:(c + 1) * 128], rhs=aT[:, kk, :],
                            start=(kk == 0), stop=(kk == 31)), [dt_, t_a[kk]], [pst[b0 + c]])
            for c in range(2):
                d_ = 2 * sl + c
                dve(lambda e, d_=d_, b=b0 + c: e.scalar_tensor_tensor(
                    out=xT[:, d_, :], in0=psb[b][:], scalar=geff[:, j, d_, s:s + 1], in1=xT[:, d_, :],
                    op0=ALU.mult, op1=ALU.add), [pst[b0 + c], t_mods, t_x[d_]], [t_x[d_]])
            if store_ti is not None and sl % 2 == 1:
                store_x_group(store_ti, sl // 2)

    def final_out(ti):
        sumsq_rstd([(xT[:, k, :], t_x[k], 128) for k in range(NCH)], D, 7)
        for k in range(NCH):
            dve(lambda e, k=k: e.scalar_tensor_tensor(out=xT[:, k, :], in0=xT[:, k, :], scalar=gfin[:, k:k + 1],
                                                      in1=rstd[:], op0=ALU.mult, op1=ALU.mult),
                [t_x[k], t_rstd, t_par], [t_x[k]])
        dst = y_lat[ti * T:(ti + 1) * T, :] if ti < NTL else y_ctx
        for tc4 in range(4):
            for k0 in range(0, NCH, 4):
                b = (k0 // 4) % 2
                for jj in range(4):
                    pe(lambda e, b=b, jj=jj, k=k0 + jj, tc4=tc4: e.transpose(
                        out=psb[b][:, jj * 128:(jj + 1) * 128], in_=xT[:, k, tc4 * 128:(tc4 + 1) * 128], identity=ident[:]),
                       [t_x[k0 + jj], t_const], [pst[b]])
                dve(lambda e, b=b, k0=k0: e.tensor_copy(out=stage[:, k0 * 128:(k0 + 4) * 128], in_=psb[b][:]),
                    [pst[b]], [t_stage])
            P.dma("sp", lambda e, tc4=tc4: e.dma_start(out=dst[tc4 * 128:(tc4 + 1) * 128, :], in_=stage[:]),
                  s_out, reads=[t_stage] + t_a, writes=[t_out])

    H0 = U0 + NCH * T * 4
    mt = [sb("mt%d" % i, [128, T], F32, at=A0 + 24576 + i * 2048) for i in range(4)]
    t_mt = toks(4)
    iost = sb("iost", [128, 4, 832], F32, at=A0)
    t_io = Tok()
    s_io = P.new_dsem()
    s_oo = P.new_dsem()
    t_oo = Tok()
    t_cat = toks(NCH)
    t_kn = Tok()
    t_vh = Tok()
    t_zqn = Tok()
    t_gq = toks(4)
    t_qh = Tok()
    t_hb = toks(4)
    t_s5w = Tok()
    t_s5p = Tok()
    s_s5 = P.new_dsem()
    t_s5st = Tok()
    S5ST = [sb("s5st%d" % i, [128, 16, 128], F32, at=U0 + i * 8192) for i in range(2)]
    t_st = toks(2)
    bankrot = [0]

    def nbank(lst):
        b = lst[bankrot[0] % len(lst)]
        bankrot[0] += 1
        return b

    def copy_alt(i, out, in_, r, w):
        if i % 2 == 0:
            act(lambda e: e.activation(out=out, in_=in_, func=AF.Copy), r, w)
        else:
            dve(lambda e: e.tensor_copy(out=out, in_=in_), r, w)

    def linear(wap, nk, rhs_fn, rhs_toks, chunks, evac, banks, n=T):
        i = 0
        while i < len(chunks):
            c_start = chunks[i][0]
            j = i
            while j < len(chunks) and chunks[j][0] >= c_start and (chunks[j][0] + max(chunks[j][1], 128) - c_start) * nk <= SLOT:
                j += 1
            ncols = max(c[0] + max(c[1], 128) for c in chunks[i:j]) - c_start
            wv, wt = wload(wview(wap, 0, nk, c_start, ncols), nk, ncols)
            for idx in range(i, j):
                c0, m = chunks[idx]
                b = nbank(banks)
                for k in range(nk):
                    pe(lambda e, wv=wv, k=k, b=b, o=c0 - c_start: e.matmul(
                        psb[b][:, 0:n], lhsT=wv[:, k, o:o + 128], rhs=rhs_fn(k), start=(k == 0), stop=(k == nk - 1)),
                       [wt, rhs_toks[k]], [pst[b]])
                evac(idx, b, m)
            i = j

    def rope(xap, xtok, pn, perm, tab, out_ap, out_toks):
        b = nbank([4, 5])
        permb = perm128b if pn == 128 else perm64b
        xb, txb = next_sq()
        act(lambda e: e.activation(out=xb[0:pn, :], in_=xap, func=AF.Copy), [xtok], [txb])
        pe(lambda e: e.matmul(psb[b][0:pn, :], lhsT=permb[0:pn, 0:pn], rhs=xb[0:pn, :], start=True, stop=True),
           [txb, t_const], [pst[b]])
        tm, tt = next_tmp()
        dve(lambda e: e.tensor_tensor(out=tm[0:pn, :], in0=xap, in1=tab[0:pn, 0, :], op=ALU.mult), [xtok, t_rope], [tt])
        dve(lambda e: e.tensor_tensor(out=xap, in0=psb[b][0:pn, :], in1=tab[0:pn, 1, :], op=ALU.mult),
            [pst[b], t_rope], [xtok])
        dve(lambda e: e.tensor_tensor(out=out_ap, in0=tm[0:pn, :], in1=xap, op=ALU.add), [tt, xtok], out_toks)

    def load_rope(ti):
        P.dma("sp", lambda e: e.dma_start(out=rope_g[:], in_=ropeg_d[:, :, ti * T:(ti + 1) * T].rearrange("a p t -> p a t")),
              s_rope, writes=[t_rope])
        P.dma("sp", lambda e: e.dma_start(out=rope_m[:], in_=ropem_d[:, :, ti * T:(ti + 1) * T].rearrange("a p t -> p a t")),
              s_rope, writes=[t_rope])

    def load_cache(l):
        for src, c0, w_ in ((c_ckv, 0, 256), (c_kr, 256, 64), (c_k, 320, 256), (c_v, 576, 256)):
            P.dma("sp", lambda e, src=src, c0=c0, w_=w_: e.dma_start(
                out=iost[:, :, c0:c0 + w_], in_=src[l].rearrange("(tc p) f -> p tc f", p=128)), s_io, writes=[t_io])
        n_ = 0
        for (c0, dst) in ((0, ckvT), (320, gkT)):
            for c in range(2):
                b = nbank([0, 1, 2, 3])
                for tc in range(4):
                    pe(lambda e, b=b, tc=tc, o=c0 + c * 128: e.transpose(
                        out=psb[b][:, tc * 128:(tc + 1) * 128], in_=iost[:, tc, o:o + 128], identity=ident[:]),
                       [t_io, t_const], [pst[b]])
                copy_alt(n_, dst[:, c, 0:PAST], psb[b][:], [pst[b]], [t_kv])
                n_ += 1
        b = nbank([0, 1, 2, 3])
        for tc in range(4):
            pe(lambda e, b=b, tc=tc: e.transpose(out=psb[b][:, tc * 128:(tc + 1) * 128], in_=iost[:, tc, 256:384],
                                                 identity=ident[:]), [t_io, t_const], [pst[b]])
        copy_alt(0, krT[0:64, 0:PAST], psb[b][0:64, :], [pst[b]], [t_kv])
        copy_alt(1, gvS[:, 0:4, :], iost[:, :, 576:832], [t_io], [t_kv])

    def mix_proj(l, ti):
        lat = ti < NTL
        s = 0 if lat else 1
        kcol = PAST + ti * T if lat else NKL
        tcol = ti * T if lat else NLAT
        load_x_tile(ti, False)
        norm_mod(1, s)
        if lat:
            load_rope(ti)
        chunks = [(C_ZKV, 128), (C_ZKV + 128, 128), (C_ZKR, 64), (C_GK, 128), (C_GK + 128, 128)] + \
                 [(C_U + 128 * i, 128) for i in range(4)]
        mtmap = {0: 0, 1: 1, 2: 2, 3: 3, 4: 0}

        def out_T(src_ap_fn, src_tok, pn, col0, width_chunks):
            for tc in range(4):
                b = nbank([0, 1, 2, 3])
                for c in range(width_chunks):
                    pe(lambda e, b=b, c=c, tc=tc: e.transpose(
                        out=psb[b][:, c * 128:(c + 1) * 128], in_=src_ap_fn(c)[:, tc * 128:(tc + 1) * 128],
                        identity=ident[:]), [src_tok(c), t_const], [pst[b]])
                if pn == 128:
                    w_ = width_chunks * 128
                    dve(lambda e, b=b, tc=tc, w_=w_: e.tensor_copy(out=iost[:, tc, col0:col0 + w_], in_=psb[b][:, 0:w_]),
                        [pst[b]], [t_io])
                else:
                    dve(lambda e, b=b, tc=tc: e.tensor_copy(out=iost[:, tc, col0:col0 + pn], in_=psb[b][:, 0:pn]),
                        [pst[b]], [t_io])

        def finish_gk(c, mi):
            sumsq_rstd([(mt[mi][:], t_mt[mi], 128)], 128, 7)
            dve(lambda e: e.scalar_tensor_tensor(out=mt[mi][:], in0=mt[mi][:], scalar=gqn[:, l, 1:2], in1=rstd[:],
                                                 op0=ALU.mult, op1=ALU.mult), [t_mt[mi], t_rstd, t_par], [t_mt[mi]])
            if lat:
                rope(mt[mi][:], t_mt[mi], 128, perm128, rope_g, gkT[:, c, kcol:kcol + T], [t_kv])
            else:
                act(lambda e: e.activation(out=gkT[:, c, kcol:kcol + T], in_=mt[mi][:], func=AF.Copy), [t_mt[mi]], [t_kv])
                out_T(lambda cc_: mt[mi], lambda cc_: t_mt[mi], 128, 320 + c * 128, 1)

        def evac(idx, b, m):
            if idx in mtmap:
                mi = mtmap[idx]
                act(lambda e: e.activation(out=mt[mi][0:m, :], in_=psb[b][0:m, :], func=AF.Copy), [pst[b]], [t_mt[mi]])
                if idx == 1:
                    sumsq_rstd([(mt[0][:], t_mt[0], 128), (mt[1][:], t_mt[1], 128)], 256, 7)
                    for c in range(2):
                        dve(lambda e, c=c: e.scalar_tensor_tensor(
                            out=mt[c][:], in0=mt[c][:], scalar=kvnorm[:, l, c:c + 1], in1=rstd[:],
                            op0=ALU.mult, op1=ALU.mult), [t_mt[c], t_rstd, t_par], [t_mt[c]])
                        act(lambda e, c=c: e.activation(out=ckvT[:, c, kcol:kcol + T], in_=mt[c][:], func=AF.Copy),
                            [t_mt[c]], [t_kv])
                    if not lat:
                        out_T(lambda c: mt[c], lambda c: t_mt[c], 128, 0, 2)
                elif idx == 2:
                    if lat:
                        rope(mt[2][0:64, :], t_mt[2], 64, perm64, rope_m, krT[0:64, kcol:kcol + T], [t_kv])
                    else:
                        act(lambda e: e.activation(out=krT[0:64, kcol:kcol + T], in_=mt[2][0:64, :], func=AF.Copy),
                            [t_mt[2]], [t_kv])
                        out_T(lambda c: mt[2], lambda c: t_mt[2], 64, 256, 1)
                elif idx == 3:
                    finish_gk(0, 3)
                elif idx == 4:
                    finish_gk(1, 0)
            else:
                cc_ = idx - 5
                copy_alt(idx, uT[:, cc_, tcol:tcol + T], psb[b][:], [pst[b]], [t_u[cc_ * NT + ti]])

        if STOP == 21:
            return
        linear(W["w_in"][l], NCH, lambda k: hT[:, k, :], t_h, chunks if STOP != 22 else chunks[5:], evac, [0, 1, 2, 3])
        if STOP in (22, 23):
            return
        wv, wt = wload(wview(W["w_in"][l], 0, NCH, C_GV, 256), NCH, 256)
        for tc in range(4):
            b = nbank([0, 1, 2, 3])
            for k in range(NCH):
                pe(lambda e, b=b, k=k, tc=tc: e.matmul(psb[b][:, 0:256], lhsT=hT[:, k, tc * 128:(tc + 1) * 128],
                                                       rhs=wv[:, k, :], start=(k == 0), stop=(k == NCH - 1)),
                   [wt, t_h[k]], [pst[b]])
            if lat:
                act(lambda e, b=b, tc=tc: e.activation(out=gvS[:, kcol // 128 + tc, :], in_=psb[b][:, 0:256], func=AF.Copy),
                    [pst[b]], [t_kv])
            else:
                dve(lambda e, b=b, tc=tc: e.tensor_copy(out=iost[:, tc, 576:832], in_=psb[b][:, 0:256]), [pst[b]], [t_io])
                act(lambda e, tc=tc: e.activation(out=gvS[:, kcol // 128 + tc, :], in_=iost[:, tc, 576:832], func=AF.Copy),
                    [t_io], [t_kv])
        if not lat and STOP != 24:
            for (dst, c0, w_) in ((o_ckv, 0, 256), (o_kr, 256, 64), (o_k, 320, 256), (o_v, 576, 256)):
                for sq_ in range(2):
                    P.dma("sp", lambda e, dst=dst, c0=c0, w_=w_, sq_=sq_: e.dma_start(
                        out=dst[sq_, l].rearrange("(tc p) f -> p tc f", p=128), in_=iost[:, 2 * sq_:2 * sq_ + 2, c0:c0 + w_]),
                        s_oo, reads=[t_io], writes=[t_oo])

    def s5_prep(l):
        sp_ = s5p
        for d_ in range(2):
            P.dma("sp", lambda e, d_=d_: e.dma_start(out=sp_[:, d_, :, 0], in_=W["s5_lambda_re"][l, d_].rearrange(
                "g p -> (g p)").rearrange("(s q) -> q s", q=128)), s_s5, writes=[t_s5p])
            P.dma("sp", lambda e, d_=d_: e.dma_start(out=sp_[:, d_, :, 1], in_=W["s5_lambda_im"][l, d_].rearrange(
                "g p -> (g p)").rearrange("(s q) -> q s", q=128)), s_s5, writes=[t_s5p])
            for j in range(2):
                P.dma("sp", lambda e, d_=d_, j=j: e.dma_start(
                    out=sp_[64 * j:64 * j + 64, d_, :, 2],
                    in_=W["s5_log_dt"][l, d_].rearrange("(s j) -> j s", j=2)[j:j + 1, :].to_broadcast([64, 16])),
                    s_s5, writes=[t_s5p])
            P.dma("sp", lambda e, d_=d_: e.dma_start(out=s5h0[:, d_, :, :], in_=st_s5[l, d_].rearrange(
                "(s q) r -> q s r", q=128)), s_s5, writes=[t_s5p])
        A = lambda i: sp_[:, :, :, i]
        PI = math.pi
        r, w = [t_s5p, t_const], [t_s5p]
        act(lambda e: e.activation(out=A(3), in_=A(2), func=AF.Exp), r, w)
        dve(lambda e: e.tensor_tensor(out=A(4), in0=A(0), in1=A(3), op=ALU.mult), r, w)
        act(lambda e: e.activation(out=A(4), in_=A(4), func=AF.Exp, scale=1.0 / 32), r, w)
        dve(lambda e: e.tensor_tensor(out=A(7), in0=A(1), in1=A(3), op=ALU.mult), r, w)
        act(lambda e: e.activation(out=A(6), in_=A(7), func=AF.Sin, scale=1.0 / 32), r, w)
        act(lambda e: e.activation(out=A(5), in_=A(7), func=AF.Sin, scale=1.0 / 32, bias=halfpi[:, 0:1]), r, w)
        dve(lambda e: e.tensor_tensor(out=A(5), in0=A(5), in1=A(4), op=ALU.mult), r, w)
        dve(lambda e: e.tensor_tensor(out=A(6), in0=A(6), in1=A(4), op=ALU.mult), r, w)
        for _ in range(5):
            dve(lambda e: e.tensor_tensor(out=A(4), in0=A(5), in1=A(6), op=ALU.mult), r, w)
            dve(lambda e: e.tensor_tensor(out=A(5), in0=A(5), in1=A(5), op=ALU.mult), r, w)
            dve(lambda e: e.tensor_tensor(out=A(6), in0=A(6), in1=A(6), op=ALU.mult), r, w)
            dve(lambda e: e.tensor_tensor(out=A(5), in0=A(5), in1=A(6), op=ALU.subtract), r, w)
            dve(lambda e: e.tensor_tensor(out=A(6), in0=A(4), in1=A(4), op=ALU.add), r, w)
        F_ = lambda i: s5f[:, :, :, i]
        dve(lambda e: e.tensor_tensor(out=A(3), in0=A(0), in1=A(0), op=ALU.mult), r, w)
        dve(lambda e: e.tensor_tensor(out=A(4), in0=A(1), in1=A(1), op=ALU.mult), r, w)
        dve(lambda e: e.tensor_tensor(out=A(3), in0=A(3), in1=A(4), op=ALU.add), r, w)
        dve(lambda e: e.reciprocal(out=A(3), in_=A(3)), r, w)
        dve(lambda e: e.tensor_scalar(out=A(7), in0=A(5), scalar1=-1.0, scalar2=None, op0=ALU.add), r, w)
        dve(lambda e: e.tensor_tensor(out=A(4), in0=A(7), in1=A(0), op=ALU.mult), r, w)
        dve(lambda e: e.tensor_tensor(out=F_(0), in0=A(6), in1=A(1), op=ALU.mult), r, w)
        dve(lambda e: e.tensor_tensor(out=F_(0), in0=F_(0), in1=A(4), op=ALU.add), r, w)
        dve(lambda e: e.tensor_tensor(out=F_(0), in0=F_(0), in1=A(3), op=ALU.mult), r, w)
        dve(lambda e: e.tensor_tensor(out=A(4), in0=A(6), in1=A(0), op=ALU.mult), r, w)
        dve(lambda e: e.tensor_tensor(out=F_(1), in0=A(7), in1=A(1), op=ALU.mult), r, w)
        dve(lambda e: e.tensor_tensor(out=F_(1), in0=A(4), in1=F_(1), op=ALU.subtract), r, w)
        dve(lambda e: e.tensor_tensor(out=F_(1), in0=F_(1), in1=A(3), op=ALU.mult), r, w)
        dve(lambda e: e.tensor_scalar(out=F_(2), in0=F_(1), scalar1=-1.0, scalar2=None, op0=ALU.mult), r, w)
        PW = lambda k, i: s5pow[:, :, :, k, i]
        dve(lambda e: e.tensor_copy(out=PW(0, 0), in_=A(5)), r, w)
        dve(lambda e: e.tensor_copy(out=PW(0, 1), in_=A(6)), r, w)
        for k in range(NLEV):
            dve(lambda e, k=k: e.tensor_scalar(out=PW(k, 2), in0=PW(k, 1), scalar1=-1.0, scalar2=None, op0=ALU.mult), r, w)
            if k + 1 < NLEV:
                dve(lambda e, k=k: e.tensor_tensor(out=A(3), in0=PW(k, 0), in1=PW(k, 0), op=ALU.mult), r, w)
                dve(lambda e, k=k: e.tensor_tensor(out=A(4), in0=PW(k, 1), in1=PW(k, 1), op=ALU.mult), r, w)
                dve(lambda e, k=k: e.tensor_tensor(out=PW(k + 1, 0), in0=A(3), in1=A(4), op=ALU.subtract), r, w)
                dve(lambda e, k=k: e.tensor_tensor(out=A(3), in0=PW(k, 0), in1=PW(k, 1), op=ALU.mult), r, w)
                dve(lambda e, k=k: e.tensor_scalar(out=PW(k + 1, 1), in0=A(3), scalar1=2.0, scalar2=None, op0=ALU.mult), r, w)
        H = lambda i: s5h0[:, :, :, i]
        AH = lambda i: s5ah0[:, :, :, i]
        dve(lambda e: e.tensor_tensor(out=A(3), in0=A(5), in1=H(0), op=ALU.mult), r, w)
        dve(lambda e: e.tensor_tensor(out=A(4), in0=A(6), in1=H(1), op=ALU.mult), r, w)
        dve(lambda e: e.tensor_tensor(out=AH(0), in0=A(3), in1=A(4), op=ALU.subtract), r, w)
        dve(lambda e: e.tensor_tensor(out=A(3), in0=A(5), in1=H(1), op=ALU.mult), r, w)
        dve(lambda e: e.tensor_tensor(out=A(4), in0=A(6), in1=H(0), op=ALU.mult), r, w)
        dve(lambda e: e.tensor_tensor(out=AH(1), in0=A(3), in1=A(4), op=ALU.add), r, w)
        rnd = 0
        for d_ in range(2):
            for kind in range(4):
                st_, tst = S5ST[rnd % 2], t_st[rnd % 2]
                rnd += 1
                P.op("pool", lambda e, st_=st_: e.memset(st_[:], 0.0), [], [tst])
                for m in range(4):
                    for gl in range(2):
                        if kind < 2:
                            src = W["s5_b_re" if kind == 0 else "s5_b_im"][l, d_]
                            sv = src.rearrange("(i r) p c -> r p i c", r=8)[2 * m + gl]
                            dstv = st_[64 * gl:64 * gl + 64, :, 32 * m + 16 * gl:32 * m + 16 * gl + 16].rearrange(
                                "q (i r) c -> q r i c", r=4)[:, m]
                        else:
                            src = W["s5_c_re" if kind == 2 else "s5_c_im"][l, d_]
                            sv = src.rearrange("(i r) c p -> r c i p", r=8)[2 * m + gl]
                            dstv = st_[32 * m + 16 * gl:32 * m + 16 * gl + 16, :, 64 * gl:64 * gl + 64].rearrange(
                                "q (i r) c -> q r i c", r=4)[:, m]
                        P.dma("sp", lambda e, dstv=dstv, sv=sv: e.dma_start(out=dstv, in_=sv), s_s5, writes=[tst])
                dstT = bbT if kind < 2 else ccT
                ri = kind % 2
                for s0 in range(0, 16, 4):
                    b = nbank([0, 1, 2, 3])
                    for j in range(4):
                        pe(lambda e, b=b, j=j, sc=s0 + j, st_=st_: e.transpose(
                            out=psb[b][:, j * 128:(j + 1) * 128], in_=st_[:, sc, :], identity=ident[:]),
                           [tst, t_const], [pst[b]])
                    sgn = -1.0 if kind == 3 else 1.0
                    act(lambda e, b=b, s0=s0, d_=d_, ri=ri, dstT=dstT, sgn=sgn: e.activation(
                        out=dstT[:, d_, s0:s0 + 4, ri, :], in_=psb[b][:].rearrange("p (j c) -> p j c", c=128),
                        func=AF.Copy, scale=sgn), [pst[b]], [t_s5w])

    def s5_scan(l):
        HRIs = [sb("hriA", [128, 2, SBUFW], F32, at=U0), sb("hriB", [128, 2, SBUFW], F32, at=U0 + 2 * SBUFW * 4)]
        assert 2 * NTOK * 2 <= 12288
        HRb = sb("s5hrb", [128, NTOK], BF16, at=TMP0)
        HIb = sb("s5hib", [128, NTOK], BF16, at=TMP0 + NTOK * 2)
        t_h2 = [toks(2), toks(2)]
        t_bf = toks(2)
        KL = NLEV

        def strided(buf, start, step, count):
            return buf[:, start:start + (count - 1) * step + 1:step]

        def ctxv(buf, start, step, count):
            return buf[:, NLAT:NLAT + NCTX].rearrange("p (s c) -> p s c", c=SEQ)[:, :, start:start + (count - 1) * step + 1:step]

        def bu_mm(cc_, sc, d_, ti):
            tcol = ti * T if ti < NTL else NLAT
            pe(lambda e: e.matmul(psb[5][:], lhsT=bbT[:, d_, sc, 0, :], rhs=uT[:, cc_, tcol:tcol + T],
                                  start=True, stop=True), [t_s5w, t_u[cc_ * NT + ti]], [pst[5]])
            pe(lambda e: e.matmul(psb[6][:], lhsT=bbT[:, d_, sc, 1, :], rhs=uT[:, cc_, tcol:tcol + T],
                                  start=True, stop=True), [t_s5w, t_u[cc_ * NT + ti]], [pst[6]])

        def bu_rest(cc_, scl, hoisted):
            sc = 4 * cc_ + scl
            st = {}
            for d_ in range(2):
                HR, HI = HRIs[d_][:, 0, :], HRIs[d_][:, 1, :]
                tr, ti_ = t_h2[d_]
                fr, fi, nfi = (s5f[:, d_, sc, i:i + 1] for i in range(3))
                lr = li = None
                for ti in range(NT):
                    tcol = ti * T if ti < NTL else NLAT
                    if not (hoisted and d_ == 0 and ti == 0):
                        bu_mm(cc_, sc, d_, ti)
                    o_r, o_i = HR[:, tcol:tcol + T], HI[:, tcol:tcol + T]
                    dve(lambda e: e.tensor_scalar(out=o_r, in0=psb[5][:], scalar1=fr, scalar2=None, op0=ALU.mult),
                        [pst[5], t_s5p], [tr])
                    dve(lambda e: e.tensor_scalar(out=o_i, in0=psb[6][:], scalar1=fr, scalar2=None, op0=ALU.mult),
                        [pst[6], t_s5p], [ti_])
                    lr = dve(lambda e: e.scalar_tensor_tensor(out=o_r, in0=psb[6][:], scalar=nfi, in1=o_r,
                                                              op0=ALU.mult, op1=ALU.add), [pst[6], t_s5p], [tr])
                    li = dve(lambda e: e.scalar_tensor_tensor(out=o_i, in0=psb[5][:], scalar=fi, in1=o_i,
                                                              op0=ALU.mult, op1=ALU.add), [pst[5], t_s5p], [ti_])
                col = 0 if d_ == 0 else NLAT - 1
                lr = dve(lambda e: e.tensor_tensor(out=HR[:, col:col + 1], in0=HR[:, col:col + 1],
                                                   in1=s5ah0[:, d_, sc, 0:1], op=ALU.add), [t_s5p], [tr])
                li = dve(lambda e: e.tensor_tensor(out=HI[:, col:col + 1], in0=HI[:, col:col + 1],
                                                   in1=s5ah0[:, d_, sc, 1:2], op=ALU.add), [t_s5p], [ti_])
                st[d_] = [lr, li]
            if l + 1 < L:
                for d_ in range(2):
                    n_ = (cc_ * 2 + d_) * 4 + scl
                    mods_slots(l + 1, n_ * 72 // 32, (n_ + 1) * 72 // 32, 7)
            return st

        def scan_pair(cc_, scl, st):
            sc = 4 * cc_ + scl

            def update(d_, k, views):
                HR, HI = HRIs[d_][:, 0, :], HRIs[d_][:, 1, :]
                tr, ti_ = t_h2[d_]
                pr, pi_, npi = (s5pow[:, d_, sc, k, i:i + 1] for i in range(3))
                last_r, last_i = st[d_]
                for (dv, sv) in views:
                    rd, rs, id_, is_ = dv(HR), sv(HR), dv(HI), sv(HI)
                    a1 = P.op("dve", lambda e: e.scalar_tensor_tensor(out=rd, in0=rs, scalar=pr, in1=rd, op0=ALU.mult, op1=ALU.add),
                              [tr, t_s5p], [tr], auto_self=False, extra=[last_r])
                    a3 = P.op("dve", lambda e: e.scalar_tensor_tensor(out=id_, in0=is_, scalar=pr, in1=id_, op0=ALU.mult, op1=ALU.add),
                              [ti_, t_s5p], [ti_], auto_self=False, extra=[last_i])
                    a2 = P.op("dve", lambda e: e.scalar_tensor_tensor(out=rd, in0=is_, scalar=npi, in1=rd, op0=ALU.mult, op1=ALU.add),
                              [tr, ti_, t_s5p], [tr], auto_self=False, extra=[a1, last_i])
                    a4 = P.op("dve", lambda e: e.scalar_tensor_tensor(out=id_, in0=rs, scalar=pi_, in1=id_, op0=ALU.mult, op1=ALU.add),
                              [tr, ti_, t_s5p], [ti_], auto_self=False, extra=[a3, last_r])
                    last_r, last_i = a2, a4
                st[d_] = [last_r, last_i]

            def views_up(d_, k):
                step, dd = 2 << k, 1 << k
                ncol = NTOK if step <= SEQ else NLAT
                c_ = ncol // step
                if d_ == 0:
                    return [(lambda b: strided(b, step - 1, step, c_), lambda b: strided(b, step - 1 - dd, step, c_))]
                return [(lambda b: strided(b, 0, step, c_), lambda b: strided(b, dd, step, c_))]

            def views_down(d_, k):
                step, dd = 2 << k, 1 << k
                v = []
                cl = NLAT // step - 1
                cx = SEQ // step - 1
                if d_ == 0:
                    if cl >= 1:
                        v.append((lambda b: strided(b, step + dd - 1, step, cl), lambda b: strided(b, step - 1, step, cl)))
                    if cx >= 1:
                        v.append((lambda b: ctxv(b, step + dd - 1, step, cx), lambda b: ctxv(b, step - 1, step, cx)))
                else:
                    if cl >= 1:
                        v.append((lambda b: strided(b, dd, step, cl), lambda b: strided(b, 2 * dd, step, cl)))
                    if cx >= 1:
                        v.append((lambda b: ctxv(b, dd, step, cx), lambda b: ctxv(b, 2 * dd, step, cx)))
                return v

            for k in range(KL):
                for d_ in range(2):
                    update(d_, k, views_up(d_, k))
            for k in range(KL - 2, -1, -1):
                for d_ in range(2):
                    update(d_, k, views_down(d_, k))
            for d_ in range(2):
                col = SEQ - 1 if d_ == 0 else 0
                for i in range(2):
                    P.op("dve", lambda e: e.tensor_copy(out=s5fin[:, :, d_, sc, i:i + 1], in_=ctxv(HRIs[d_][:, i, :], col, 1, 1)),
                         [t_h2[d_][i]], [t_s5st], auto_self=False, extra=[st[d_][i]])

        def out_pair(cc_, scl):
            sc = 4 * cc_ + scl
            for d_ in range(2):
                act(lambda e: e.activation(out=HRb[:, :], in_=HRIs[d_][:, 0, 0:NTOK], func=AF.Copy), [t_h2[d_][0]], [t_bf[0]])
                act(lambda e: e.activation(out=HIb[:, :], in_=HRIs[d_][:, 1, 0:NTOK], func=AF.Copy), [t_h2[d_][1]], [t_bf[1]])
                first = (d_ == 0 and scl == 0)
                last = (d_ == 1 and scl == 3)
                for ti in range(NT):
                    yb = ti if ti < NTL else 4
                    tcol = ti * T if ti < NTL else NLAT
                    pe(lambda e: e.matmul(psb[yb][:], lhsT=ccT[:, d_, sc, 0, :], rhs=HRb[:, tcol:tcol + T],
                                          start=first, stop=False), [t_s5w, t_bf[0]], [pst[yb]])
                    pe(lambda e: e.matmul(psb[yb][:], lhsT=ccT[:, d_, sc, 1, :], rhs=HIb[:, tcol:tcol + T],
                                          start=False, stop=last), [t_s5w, t_bf[1]], [pst[yb]])
            if scl == 3:
                for ti in range(NT):
                    yb = ti if ti < NTL else 4
                    tcol = ti * T if ti < NTL else NLAT
                    dve(lambda e: e.scalar_tensor_tensor(
                        out=uT[:, cc_, tcol:tcol + T], in0=uT[:, cc_, tcol:tcol + T], scalar=s5d[:, l, cc_:cc_ + 1],
                        in1=psb[yb][:], op0=ALU.mult, op1=ALU.add), [pst[yb], t_par, t_u[cc_ * NT + ti]], [t_u[cc_ * NT + ti]])

        pairs = [(c_, s_) for c_ in range(4) for s_ in range(4)]
        st = bu_rest(pairs[0][0], pairs[0][1], hoisted=False)
        for i, (cc_, scl) in enumerate(pairs):
            scan_pair(cc_, scl, st)
            nxt = pairs[i + 1] if i + 1 < len(pairs) else None
            if nxt is not None:
                bu_mm(nxt[0], 4 * nxt[0] + nxt[1], 0, 0)
            out_pair(cc_, scl)
            if nxt is not None:
                st = bu_rest(nxt[0], nxt[1], hoisted=True)
        for sq_ in range(2):
            P.dma("sp", lambda e: e.dma_start(out=o_s5[sq_, l].rearrange("d (s q) r -> q d s r", q=128),
                                              in_=s5fin[:, sq_, :, :, :]), s_oo, reads=[t_s5st], writes=[t_oo])
        if l + 1 < L:
            mods_finish(l + 1, 7)
        P.barrier()

    def s5_glu(l):
        wv, wt = wload(wview(W["s5_w_glu"][l], 0, 4, 0, 512), 4, 512)
        for ti in range(NT):
            tcol = ti * T if ti < NTL else NLAT
            ut = [t_u[c * NT + ti] for c in range(4)]
            for co in range(4):
                for k in range(4):
                    pe(lambda e, co=co, k=k: e.matmul(psb[co][:], lhsT=wv[:, k, co * 128:(co + 1) * 128],
                                                      rhs=uT[:, k, tcol:tcol + T], start=(k == 0), stop=(k == 3)),
                       [wt, ut[k]], [pst[co]])
            for co in range(4):
                y = uT[:, co, tcol:tcol + T]
                t1, tt1 = next_tmp()
                dve(lambda e, y=y, t1=t1: e.tensor_tensor(out=t1[:], in0=y, in1=y, op=ALU.mult), [ut[co]], [tt1])
                dve(lambda e, t1=t1: e.tensor_scalar(out=t1[:], in0=t1[:], scalar1=0.044715, scalar2=1.0,
                                                     op0=ALU.mult, op1=ALU.add), [tt1], [tt1])
                dve(lambda e, y=y, t1=t1: e.tensor_tensor(out=t1[:], in0=t1[:], in1=y, op=ALU.mult), [tt1, ut[co]], [tt1])
                act(lambda e, t1=t1: e.activation(out=t1[:], in_=t1[:], func=AF.Sigmoid, scale=1.5957691216057308),
                    [tt1], [tt1])
                t2, tt2 = next_tmp()
                act(lambda e, t2=t2, co=co: e.activation(out=t2[:], in_=psb[co][:], func=AF.Sigmoid,
                                                         bias=bglu[:, l, co:co + 1]), [pst[co], t_par], [tt2])
                dve(lambda e, t1=t1, t2=t2: e.tensor_tensor(out=t1[:], in0=t1[:], in1=t2[:], op=ALU.mult), [tt1, tt2], [tt1])
                dve(lambda e, y=y, t1=t1: e.tensor_tensor(out=y, in0=y, in1=t1[:], op=ALU.mult), [tt1, ut[co]], [ut[co]])

    def attend(qparts, qtoks, kfn, vfn, kcs, scale, out_c, qc0, n, pair):
        bo, bd = (3, 4) if pair == 0 else (5, 6)
        nk = len(kcs)

        def qk(i):
            bs = i % 3
            kaps = kfn(kcs[i])
            for pi_, (qap, kap) in enumerate(zip(qparts, kaps)):
                pe(lambda e, bs=bs, qap=qap, kap=kap, pi_=pi_: e.matmul(
                    psb[bs][:, 0:n], lhsT=kap, rhs=qap, start=(pi_ == 0), stop=(pi_ == len(qparts) - 1)),
                   [t_kv, t_kn] + qtoks, [pst[bs]])
        qk(0)
        for i in range(nk):
            bs = i % 3
            pT_, tp = next_pT()
            act(lambda e, bs=bs, pT_=pT_: e.activation(out=pT_[:, 0:n], in_=psb[bs][:, 0:n], func=AF.Exp, scale=scale),
                [pst[bs]], [tp])
            if i + 1 < nk:
                qk(i + 1)
            pe(lambda e, pT_=pT_, i=i, vap=vfn(kcs[i]): e.matmul(psb[bo][:, 0:n], lhsT=vap, rhs=pT_[:, 0:n],
                                                                 start=(i == 0), stop=(i == nk - 1)),
               [tp, t_kv, t_vh], [pst[bo]])
            pe(lambda e, pT_=pT_, i=i: e.matmul(psb[bd][:, 0:n], lhsT=ones[:], rhs=pT_[:, 0:n],
                                                start=(i == 0), stop=(i == nk - 1)), [tp, t_const], [pst[bd]])
        tm, tt = next_tmp()
        dve(lambda e, tm=tm: e.reciprocal(out=tm[:, 0:n], in_=psb[bd][:, 0:n]), [pst[bd]], [tt])
        dve(lambda e, tm=tm: e.tensor_tensor(out=catT[:, out_c, qc0:qc0 + n], in0=psb[bo][:, 0:n], in1=tm[:, 0:n],
                                             op=ALU.mult), [pst[bo], tt], [t_cat[out_c]])

    def mix_attn(l, ti):
        lat = ti < NTL
        s = 0 if lat else 1
        tcol = ti * T if lat else NLAT
        load_x_tile(ti, False)
        norm_mod(1, s)
        if lat:
            load_rope(ti)
            kbase, nkeys = 0, NKL
        else:
            kbase, nkeys = NKL, NCTX
        def ev_zq(idx, b, m):
            act(lambda e: e.activation(out=mt[idx][:], in_=psb[b][:], func=AF.Copy), [pst[b]], [t_mt[idx]])
        linear(W["w_in"][l], NCH, lambda k: hT[:, k, :], t_h, [(C_ZQ + 128 * i, 128) for i in range(4)], ev_zq, [0, 1, 2])
        sumsq_rstd([(mt[i][:], t_mt[i], 128) for i in range(4)], 512, 7)
        for c in range(4):
            dve(lambda e, c=c: e.scalar_tensor_tensor(out=zqn[:, c, :], in0=mt[c][:], scalar=qnorm[:, l, c:c + 1],
                                                      in1=rstd[:], op0=ALU.mult, op1=ALU.mult),
                [t_mt[c], t_rstd, t_par], [t_zqn])
        def ev_gq(idx, b, m):
            act(lambda e: e.activation(out=mt[idx][:], in_=psb[b][:], func=AF.Copy), [pst[b]], [t_mt[idx]])
            sumsq_rstd([(mt[idx][:], t_mt[idx], 128)], 128, 7)
            dve(lambda e: e.scalar_tensor_tensor(out=mt[idx][:], in0=mt[idx][:], scalar=gqn[:, l, 0:1], in1=rstd[:],
                                                 op0=ALU.mult, op1=ALU.mult), [t_mt[idx], t_rstd, t_par], [t_mt[idx]])
            if lat:
                rope(mt[idx][:], t_mt[idx], 128, perm128, rope_g, gqT[:, idx, :], [t_gq[idx]])
            else:
                act(lambda e: e.activation(out=gqT[:, idx, :], in_=mt[idx][:], func=AF.Copy), [t_mt[idx]], [t_gq[idx]])
        linear(W["w_in"][l], NCH, lambda k: hT[:, k, :], t_h, [(C_GQ + 128 * i, 128) for i in range(4)], ev_gq, [0, 1, 2])
        for h in range(8):
            wq, wqt = wload(wview(W["mla_w_uq"][l], 0, 4, h * 192, 192), 4, 256, 192)
            b = nbank([0, 1, 2])
            for k in range(4):
                pe(lambda e, b=b, k=k: e.matmul(psb[b][:], lhsT=wq[:, k, 0:128], rhs=zqn[:, k, :], start=(k == 0), stop=(k == 3)),
                   [wqt, t_zqn], [pst[b]])
            act(lambda e, b=b: e.activation(out=qn_h[:], in_=psb[b][:], func=AF.Copy), [pst[b]], [t_qh])
            b = nbank([0, 1, 2])
            for k in range(4):
                pe(lambda e, b=b, k=k: e.matmul(psb[b][:, :], lhsT=wq[:, k, 128:256], rhs=zqn[:, k, :], start=(k == 0), stop=(k == 3)),
                   [wqt, t_zqn], [pst[b]])
            if lat:
                act(lambda e, b=b: e.activation(out=mt[0][0:64, :], in_=psb[b][0:64, :], func=AF.Copy), [pst[b]], [t_mt[0]])
                rope(mt[0][0:64, :], t_mt[0], 64, perm64, rope_m, qr_h[:], [t_qh])
            else:
                act(lambda e, b=b: e.activation(out=qr_h[:], in_=psb[b][0:64, :], func=AF.Copy), [pst[b]], [t_qh])
            wk, wkt = wload(wview(W["mla_w_ukv"][l], 0, 2, h * 256, 256), 2, 256)
            for c0 in range(0, nkeys, T):
                b = nbank([0, 1, 2])
                for k in range(2):
                    pe(lambda e, b=b, k=k, c0=c0: e.matmul(psb[b][:], lhsT=wk[:, k, 0:128], rhs=ckvT[:, k, kbase + c0:kbase + c0 + T],
                                                           start=(k == 0), stop=(k == 1)), [wkt, t_kv], [pst[b]])
                copy_alt(c0 // T, knope[:, c0:c0 + T], psb[b][:], [pst[b]], [t_kn])
            for kc0 in range(0, nkeys // 128, 4):
                b = nbank([0, 1, 2])
                for j in range(4):
                    for k in range(2):
                        kc = kbase // 128 + kc0 + j
                        pe(lambda e, b=b, j=j, k=k, kc=kc: e.matmul(
                            psb[b][:, j * 128:(j + 1) * 128], lhsT=ckvT[:, k, kc * 128:(kc + 1) * 128], rhs=wk[:, k, 128:256],
                            start=(k == 0), stop=(k == 1)), [wkt, t_kv], [pst[b]])
                copy_alt(kc0 // 4 + 1, vh[:, kc0:kc0 + 4, :], psb[b][:].rearrange("p (j c) -> p j c", c=128), [pst[b]], [t_vh])
            if lat:
                attend([qn_h[:], qr_h[:]], [t_qh],
                       lambda kc: [knope[:, kc * 128:(kc + 1) * 128], krT[0:64, kc * 128:(kc + 1) * 128]],
                       lambda kc: vh[:, kc, :], list(range(NKL // 128)), MLA_SCALE, h, 0, T, h % 2)
            else:
                for sq_ in range(2):
                    attend([qn_h[:, sq_ * 256:(sq_ + 1) * 256], qr_h[:, sq_ * 256:(sq_ + 1) * 256]], [t_qh],
                           lambda kc: [knope[:, kc * 128:(kc + 1) * 128], krT[0:64, NKL + kc * 128:NKL + (kc + 1) * 128]],
                           lambda kc: vh[:, kc, :], [2 * sq_, 2 * sq_ + 1], MLA_SCALE, h, sq_ * 256, 256, (2 * h + sq_) % 2)
        for g in range(4):
            kvh = g // 2
            if lat:
                attend([gqT[:, g, :]], [t_gq[g]],
                       lambda kc: [gkT[:, kvh, kc * 128:(kc + 1) * 128]],
                       lambda kc: gvS[:, kc, kvh * 128:(kvh + 1) * 128], list(range(NKL // 128)), GQA_SCALE, 12 + g, 0, T, g % 2)
            else:
                for sq_ in range(2):
                    attend([gqT[:, g, sq_ * 256:(sq_ + 1) * 256]], [t_gq[g]],
                           lambda kc: [gkT[:, kvh, kc * 128:(kc + 1) * 128]],
                           lambda kc: gvS[:, kc, kvh * 128:(kvh + 1) * 128],
                           [NKL // 128 + 2 * sq_, NKL // 128 + 2 * sq_ + 1], GQA_SCALE, 12 + g, sq_ * 256, 256, (2 * g + sq_) % 2)
        def rhs_fn(k):
            if 8 <= k < 12:
                return uT[:, k - 8, tcol:tcol + T]
            return catT[:, k, :]
        rt = [t_u[(k - 8) * NT + ti] if 8 <= k < 12 else t_cat[k] for k in range(NCH)]

        def ev_out(idx, b, m):
            dve(lambda e: e.scalar_tensor_tensor(out=xT[:, idx, :], in0=psb[b][:], scalar=geff[:, 1, idx, s:s + 1],
                                                 in1=xT[:, idx, :], op0=ALU.mult, op1=ALU.add),
                [pst[b], t_mods, t_x[idx]], [t_x[idx]])
            if idx % 4 == 3:
                store_x_group(ti, idx // 4)
        linear(W["w_out"][l], NCH, rhs_fn, rt, [(128 * i, 128) for i in range(NCH)], ev_out, [0, 1, 2, 7])
        P.barrier()

    def mixer(l):
        if STOP == -2:
            return
        P.barrier()
        if STOP == -1:
            return
        load_cache(l)
        if STOP == 1:
            return
        for ti in range(NT):
            mix_proj(l, ti)
        P.barrier()
        if STOP in (2, 21, 22, 23, 24):
            return
        s5_prep(l)
        P.barrier()
        if STOP == 3:
            return
        s5_scan(l)
        if STOP == 4:
            P.barrier()
            return
        s5_glu(l)
        P.barrier()
        if STOP == 5:
            return
        for ti in range(NT):
            mix_attn(l, ti)

    compute_mods(0)
    for l in range(L):
        mods, gmod, geff, t_mods = MB[l % 2]
        for ti in range(NT):
            s = 0 if ti < NTL else 1
            load_x_tile(ti, first=(l == 0))
            norm_mod(0, s)
            ffn(l, 1, s, store_ti=ti)
        mixer(l)
        for ti in range(NT):
            s = 0 if ti < NTL else 1
            load_x_tile(ti, first=False)
            norm_mod(2, s)
            ffn(l, 2, s, store_ti=(None if l == L - 1 else ti))
            if l == L - 1:
                final_out(ti)
    P.barrier()
    P.emit()
    ncd.__exit__(None, None, None)
    return nc


def _rope_tables(nlat, rot_dim):
    n_rows = nlat // 64
    row = np.repeat(np.arange(n_rows, dtype=np.float32), 64)
    col = np.tile(np.arange(64, dtype=np.float32), n_rows)
    n_freq = rot_dim // 4
    inv = (np.float32(10000.0) ** (-np.arange(n_freq, dtype=np.float32) / np.float32(n_freq))).astype(np.float32)
    ang = np.concatenate([row[:, None] * inv, col[:, None] * inv], axis=-1).astype(np.float32)
    c = np.cos(ang).astype(np.float32).T
    s = np.sin(ang).astype(np.float32).T
    return np.ascontiguousarray(np.stack([np.concatenate([c, c], 0), np.concatenate([s, s], 0)], 0))


def _perm(n):
    h = n // 2
    p = np.zeros((n, n), np.float32)
    for d in range(h):
        p[d + h, d] = -1.0
        p[d, d + h] = 1.0
    return p


def make_in_maps(inp, ncores, L, NLAT):
    consts = {
        "ident": np.eye(128, dtype=np.float32),
        "perm128": _perm(128),
        "perm64": _perm(64),
        "ropeg": _rope_tables(NLAT, 128),
        "ropem": _rope_tables(NLAT, 64),
    }
    shared = {}
    for name, shp in WEIGHT_SPECS:
        shared[name] = np.ascontiguousarray(np.asarray(inp[name], dtype=np.float32).reshape((L,) + shp))
    shared["norm_final"] = np.ascontiguousarray(np.asarray(inp["norm_final"], dtype=np.float32))
    shared.update(consts)
    maps = []
    xp = np.asarray(inp["x_prompt"])
    for i in range(ncores):
        m = dict(shared)
        m["x_lat"] = np.ascontiguousarray(np.asarray(inp["x_sample"])[i])
        m["x_ctx"] = np.ascontiguousarray(xp[2 * i:2 * i + 2].reshape(NCTX, D))
        m["cvec"] = np.ascontiguousarray(np.stack([np.asarray(inp["c"])[i], np.asarray(inp["c_ctx"])], 0))
        m["c_ckv"] = np.ascontiguousarray(np.asarray(inp["cache_mla_ckv"])[i])
        m["c_kr"] = np.ascontiguousarray(np.asarray(inp["cache_mla_krope"])[i])
        m["c_k"] = np.ascontiguousarray(np.asarray(inp["cache_gqa_k"])[i].reshape(L, PAST, 256))
        m["c_v"] = np.ascontiguousarray(np.asarray(inp["cache_gqa_v"])[i].reshape(L, PAST, 256))
        m["st_s5"] = np.ascontiguousarray(np.asarray(inp["state_s5"])[i].reshape(L, 2, 2048, 2))
        maps.append(m)
    return maps


def gather_outputs(results, ncores, L, NLAT):
    y_prompt = np.concatenate([r["y_ctx"].reshape(2, SEQ, D) for r in results], 0)
    y_sample = np.stack([r["y_lat"] for r in results], 0)
    ckv = np.concatenate([r["o_ckv"] for r in results], 0)
    kr = np.concatenate([r["o_kr"] for r in results], 0)
    k = np.concatenate([r["o_k"].reshape(2, L, SEQ, 2, 128) for r in results], 0)
    v = np.concatenate([r["o_v"].reshape(2, L, SEQ, 2, 128) for r in results], 0)
    s5 = np.concatenate([r["o_s5"].reshape(2, L, 2, 32, 64, 2) for r in results], 0)
    return tuple(np.ascontiguousarray(a.astype(np.float32)) for a in (y_prompt, y_sample, ckv, kr, k, v, s5))


_NC_CACHE = {}


def kernel(**inputs):
    L, NLAT, ncores = 4, 2048, 8
    key = (L, NLAT)
    if key not in _NC_CACHE:
        _NC_CACHE[key] = build(L, NLAT)
    nc = _NC_CACHE[key]
    maps = make_in_maps(inputs, ncores, L, NLAT)
    res = run_bass_kernel_spmd(nc, maps, core_ids=list(range(ncores)))
    return gather_outputs(res.results, ncores, L, NLAT)
```
